# Optimizing a Trainium2 kernel written in Bass

```python
import math
import jax
import jax.numpy as jnp
from jax import lax
import numpy as np

D_MODEL = 1024
BATCH = 8
SEQ = 4096
DEPTH = 2

N_META = 16
CONV_K = 4
RMS_EPS = 1e-6
L2_EPS = 1e-6
D_FF = 4 * D_MODEL
N_BRANCH = 3

GDN_HEADS = 8
GDN_DK = 128
GDN_DV = 128
GDN_CHUNK = 64
GDN_QK_W = GDN_HEADS * GDN_DK
GDN_V_W = GDN_HEADS * GDN_DV

SSD_HEADS = 16
SSD_HEAD_DIM = 64
SSD_INNER = SSD_HEADS * SSD_HEAD_DIM
SSD_GROUPS = 4
SSD_HPG = SSD_HEADS // SSD_GROUPS
SSD_STATE = 128
SSD_CHUNK = 128
SSD_CONV_DIM = SSD_INNER + 2 * SSD_GROUPS * SSD_STATE

SWA_Q_HEADS = 16
SWA_KV_HEADS = 4
SWA_REP = SWA_Q_HEADS // SWA_KV_HEADS
SWA_HEAD_DIM = 64
SWA_WINDOW = 128
SWA_Q_W = SWA_Q_HEADS * SWA_HEAD_DIM
SWA_KV_W = SWA_KV_HEADS * SWA_HEAD_DIM

IN_SIZES = (GDN_QK_W, GDN_QK_W, GDN_V_W, GDN_V_W, GDN_HEADS, GDN_HEADS,
            SSD_INNER, SSD_CONV_DIM, SSD_HEADS,
            SWA_Q_W, SWA_KV_W, SWA_KV_W,
            N_BRANCH * D_MODEL)
IN_W = sum(IN_SIZES)

kernel_name = "hybrid_gdn_ssd_swa_sink_block"


def rmsnorm(x, w):
    xf = x.astype(jnp.float32)
    y = xf * lax.rsqrt(jnp.mean(xf * xf, axis=-1, keepdims=True) + RMS_EPS)
    return (y * w.astype(jnp.float32)).astype(x.dtype)


def l2norm(x):
    return x * lax.rsqrt(jnp.sum(x * x, axis=-1, keepdims=True) + L2_EPS)


def split_in(u):
    offs = np.cumsum(np.array(IN_SIZES))[:-1].tolist()
    return jnp.split(u, offs, axis=-1)


def causal_dwconv(x, w, b=None):
    y = lax.conv_general_dilated(
        x, w[:, None, :].astype(x.dtype), window_strides=(1,), padding=[(CONV_K - 1, 0)],
        dimension_numbers=("NWC", "WIO", "NWC"), feature_group_count=x.shape[-1])
    if b is not None:
        y = y + b.astype(x.dtype)
    return y


def pad_front(t, pad):
    return jnp.pad(t, [(0, 0), (pad, 0)] + [(0, 0)] * (t.ndim - 2))


def softmax_with_sink(s, sink):
    m = jnp.maximum(jnp.max(s, axis=-1, keepdims=True), sink)
    e = jnp.exp(s - m)
    return e / (jnp.sum(e, axis=-1, keepdims=True) + jnp.exp(sink - m))


def gated_delta_chunked(q, k, v, g, beta):
    bsz, t_len, nh, dk = q.shape
    dv = v.shape[-1]
    c = GDN_CHUNK
    nc = t_len // c
    q = q.reshape(bsz, nc, c, nh, dk).transpose(0, 3, 1, 2, 4) * (dk ** -0.5)
    k = k.reshape(bsz, nc, c, nh, dk).transpose(0, 3, 1, 2, 4)
    v = v.reshape(bsz, nc, c, nh, dv).transpose(0, 3, 1, 2, 4)
    g = g.reshape(bsz, nc, c, nh).transpose(0, 3, 1, 2)
    beta = beta.reshape(bsz, nc, c, nh).transpose(0, 3, 1, 2)
    gam = jnp.cumsum(g, axis=-1)
    tri_incl = jnp.tril(jnp.ones((c, c), dtype=bool))
    tri_strict = jnp.tril(jnp.ones((c, c), dtype=bool), -1)
    decay = jnp.exp(jnp.where(tri_incl, gam[..., :, None] - gam[..., None, :], -jnp.inf))
    kb = k * beta[..., None]
    a_low = jnp.where(tri_strict, jnp.einsum("bhncd,bhnsd->bhncs", kb, k) * decay, 0.0)
    eye = jnp.eye(c, dtype=jnp.float32)
    rhs = jnp.concatenate([v * beta[..., None], kb * jnp.exp(gam)[..., None]], axis=-1)
    sol = lax.linalg.triangular_solve(a_low + eye, rhs, left_side=True, lower=True, unit_diagonal=True)
    u = sol[..., :dv]
    w = sol[..., dv:]
    attn_qk = jnp.einsum("bhncd,bhnsd->bhncs", q, k) * decay
    q_dec = q * jnp.exp(gam)[..., None]
    g_last = gam[..., -1]
    k_tail = k * jnp.exp(g_last[..., None] - gam)[..., None]

    def step(s_state, inp):
        u_c, w_c, qd_c, a_c, kt_c, gl_c = inp
        v_new = u_c - jnp.einsum("bhcd,bhde->bhce", w_c, s_state)
        o_c = jnp.einsum("bhcd,bhde->bhce", qd_c, s_state) + jnp.einsum("bhcs,bhse->bhce", a_c, v_new)
        s_state = s_state * jnp.exp(gl_c)[..., None, None] + jnp.einsum("bhcd,bhce->bhde", kt_c, v_new)
        return s_state, o_c

    xs = (jnp.moveaxis(u, 2, 0), jnp.moveaxis(w, 2, 0), jnp.moveaxis(q_dec, 2, 0),
          jnp.moveaxis(attn_qk, 2, 0), jnp.moveaxis(k_tail, 2, 0), jnp.moveaxis(g_last, -1, 0))
    s0 = jnp.zeros((bsz, nh, dk, dv), jnp.float32)
    _, o = lax.scan(step, s0, xs)
    return o.transpose(1, 0, 3, 2, 4).reshape(bsz, t_len, nh, dv)


def gdn_branch(q, k, v, gate, b, a, conv_w, a_log, dt_bias, norm_w):
    dtype = q.dtype
    bsz, seq_len = q.shape[:2]
    qkv = jax.nn.silu(causal_dwconv(jnp.concatenate([q, k, v], axis=-1), conv_w))
    q, k, v = jnp.split(qkv, [GDN_QK_W, 2 * GDN_QK_W], axis=-1)
    q = l2norm(q.astype(jnp.float32).reshape(bsz, seq_len, GDN_HEADS, GDN_DK))
    k = l2norm(k.astype(jnp.float32).reshape(bsz, seq_len, GDN_HEADS, GDN_DK))
    v = v.astype(jnp.float32).reshape(bsz, seq_len, GDN_HEADS, GDN_DV)
    beta = jax.nn.sigmoid(b.astype(jnp.float32))
    g = -jnp.exp(a_log.astype(jnp.float32)) * jax.nn.softplus(a.astype(jnp.float32) + dt_bias.astype(jnp.float32))
    pad = GDN_CHUNK - N_META
    o = gated_delta_chunked(pad_front(q, pad), pad_front(k, pad), pad_front(v, pad),
                            pad_front(g, pad), pad_front(beta, pad))[:, pad:]
    gate = gate.astype(jnp.float32).reshape(bsz, seq_len, GDN_HEADS, GDN_DV)
    o = rmsnorm(o, norm_w) * jax.nn.silu(gate)
    return o.reshape(bsz, seq_len, GDN_V_W).astype(dtype)


def ssd_chunked(xdt, adt, bm, cm):
    bsz, t_len, ng, nr, hp = xdt.shape
    c = SSD_CHUNK
    nc = t_len // c
    xdt = xdt.reshape(bsz, nc, c, ng, nr, hp)
    bm = bm.reshape(bsz, nc, c, ng, -1)
    cm = cm.reshape(bsz, nc, c, ng, -1)
    acum = jnp.cumsum(adt.reshape(bsz, nc, c, ng, nr).transpose(0, 3, 4, 1, 2), axis=-1)
    tri = jnp.tril(jnp.ones((c, c), dtype=bool))
    lmat = jnp.exp(jnp.where(tri, acum[..., :, None] - acum[..., None, :], -jnp.inf))
    cb = jnp.einsum("bclgn,bcsgn->bgcls", cm, bm)
    y_diag = jnp.einsum("bgcls,bgrcls,bcsgrp->bclgrp", cb, lmat, xdt)
    decay_states = jnp.exp(acum[..., -1:] - acum)
    states = jnp.einsum("bclgn,bgrcl,bclgrp->bcgrpn", bm, decay_states, xdt)
    chunk_decay = jnp.exp(acum[..., -1])

    def step(h, inp):
        st, dec = inp
        return h * dec[..., None, None] + st, h

    h0 = jnp.zeros((bsz, ng, nr, hp, states.shape[-1]), jnp.float32)
    _, h_in = lax.scan(step, h0, (jnp.moveaxis(states, 1, 0), jnp.moveaxis(chunk_decay, -1, 0)))
    h_in = jnp.moveaxis(h_in, 0, 1)
    y_off = jnp.einsum("bclgn,bcgrpn,bgrcl->bclgrp", cm, h_in, jnp.exp(acum))
    return (y_diag + y_off).reshape(bsz, t_len, ng, nr, hp)


def ssd_branch(z, xbc, dt, conv_w, conv_b, dt_bias, a_log, d_skip, norm_w):
    dtype = z.dtype
    bsz, seq_len = z.shape[:2]
    xbc = jax.nn.silu(causal_dwconv(xbc, conv_w, conv_b)).astype(jnp.float32)
    xs, bm, cm = jnp.split(xbc, [SSD_INNER, SSD_INNER + SSD_GROUPS * SSD_STATE], axis=-1)
    xs = xs.reshape(bsz, seq_len, SSD_GROUPS, SSD_HPG, SSD_HEAD_DIM)
    bm = bm.reshape(bsz, seq_len, SSD_GROUPS, SSD_STATE)
    cm = cm.reshape(bsz, seq_len, SSD_GROUPS, SSD_STATE)
    dtp = jax.nn.softplus(dt.astype(jnp.float32) + dt_bias.astype(jnp.float32))
    dtp = dtp.reshape(bsz, seq_len, SSD_GROUPS, SSD_HPG)
    a = -jnp.exp(a_log.astype(jnp.float32)).reshape(SSD_GROUPS, SSD_HPG)
    pad = SSD_CHUNK - N_META
    y = ssd_chunked(pad_front(xs * dtp[..., None], pad), pad_front(dtp * a, pad),
                    pad_front(bm, pad), pad_front(cm, pad))[:, pad:]
    y = y + d_skip.astype(jnp.float32).reshape(SSD_GROUPS, SSD_HPG)[:, :, None] * xs
    y = y.reshape(bsz, seq_len, SSD_INNER) * jax.nn.silu(z.astype(jnp.float32))
    y = y.reshape(bsz, seq_len, SSD_GROUPS, SSD_INNER // SSD_GROUPS)
    y = y * lax.rsqrt(jnp.mean(y * y, axis=-1, keepdims=True) + RMS_EPS)
    y = y * norm_w.astype(jnp.float32).reshape(SSD_GROUPS, SSD_INNER // SSD_GROUPS)
    return y.reshape(bsz, seq_len, SSD_INNER).astype(dtype)


def swa_branch(q, k, v, sinks):
    dtype = q.dtype
    bsz, seq_len = q.shape[:2]
    s_len = seq_len - N_META
    wdw = SWA_WINDOW
    nb = s_len // wdw
    q = q.astype(jnp.float32).reshape(bsz, seq_len, SWA_KV_HEADS, SWA_REP, SWA_HEAD_DIM) * (SWA_HEAD_DIM ** -0.5)
    k = k.astype(jnp.float32).reshape(bsz, seq_len, SWA_KV_HEADS, SWA_HEAD_DIM)
    v = v.astype(jnp.float32).reshape(bsz, seq_len, SWA_KV_HEADS, SWA_HEAD_DIM)
    sink = sinks.astype(jnp.float32).reshape(SWA_KV_HEADS, SWA_REP)
    qm, km, vm = q[:, :N_META], k[:, :N_META], v[:, :N_META]
    s_m = jnp.einsum("bqhrd,bkhd->bhrqk", qm, km)
    s_m = jnp.where(jnp.tril(jnp.ones((N_META, N_META), dtype=bool)), s_m, -jnp.inf)
    p_m = softmax_with_sink(s_m, sink[None, :, :, None, None])
    o_m = jnp.einsum("bhrqk,bkhd->bqhrd", p_m, vm).reshape(bsz, N_META, SWA_Q_W)
    qr = q[:, N_META:].reshape(bsz, nb, wdw, SWA_KV_HEADS, SWA_REP, SWA_HEAD_DIM)
    kr = k[:, N_META:].reshape(bsz, nb, wdw, SWA_KV_HEADS, SWA_HEAD_DIM)
    vr = v[:, N_META:].reshape(bsz, nb, wdw, SWA_KV_HEADS, SWA_HEAD_DIM)
    zpad = [(0, 0), (1, 0), (0, 0), (0, 0), (0, 0)]
    kband = jnp.concatenate([jnp.pad(kr, zpad)[:, :-1], kr], axis=2)
    vband = jnp.concatenate([jnp.pad(vr, zpad)[:, :-1], vr], axis=2)
    qpos = jnp.arange(wdw)[:, None] + wdw
    kpos = jnp.arange(2 * wdw)[None, :]
    band = (kpos <= qpos) & (kpos > qpos - wdw)
    prev_ok = (jnp.arange(nb)[:, None] > 0) | (jnp.arange(2 * wdw)[None, :] >= wdw)
    mask = band[None, :, :] & prev_ok[:, None, :]
    s_loc = jnp.einsum("bnqhrd,bnkhd->bnhrqk", qr, kband)
    s_loc = jnp.where(mask[None, :, None, None, :, :], s_loc, -jnp.inf)
    s_meta = jnp.einsum("bnqhrd,bkhd->bnhrqk", qr, km)
    p = softmax_with_sink(jnp.concatenate([s_loc, s_meta], axis=-1), sink[None, None, :, :, None, None])
    o_r = (jnp.einsum("bnhrqk,bnkhd->bnqhrd", p[..., :2 * wdw], vband)
           + jnp.einsum("bnhrqk,bkhd->bnqhrd", p[..., 2 * wdw:], vm))
    o_r = o_r.reshape(bsz, s_len, SWA_Q_W)
    return jnp.concatenate([o_m, o_r], axis=1).astype(dtype)


def setup_inputs(seed: int = 0) -> dict:
    key = jax.random.key(seed)
    ks = jax.random.split(key, 23)
    f32 = jnp.float32

    def nrm(k, shape, scale):
        return jax.random.normal(k, shape, f32) * scale

    def gain(k, shape):
        return 1.0 + 0.02 * jax.random.normal(k, shape, f32)

    def dt_bias(k, shape):
        dt = jnp.exp(jax.random.uniform(k, shape, f32, math.log(1e-3), math.log(1e-1)))
        return dt + jnp.log(-jnp.expm1(-dt))

    def a_log(k, shape):
        return jnp.log(jax.random.uniform(k, shape, f32, 1.0, 16.0))

    return {
        "x": nrm(ks[0], (BATCH, SEQ, D_MODEL), 1.0),
        "meta_tokens": nrm(ks[1], (N_META, D_MODEL), 1.0),
        "norm1_w": gain(ks[2], (DEPTH, D_MODEL)),
        "w_in": nrm(ks[3], (DEPTH, D_MODEL, IN_W), D_MODEL ** -0.5),
        "gdn_conv_w": nrm(ks[4], (DEPTH, CONV_K, 2 * GDN_QK_W + GDN_V_W), CONV_K ** -0.5),
        "gdn_a_log": a_log(ks[5], (DEPTH, GDN_HEADS)),
        "gdn_dt_bias": dt_bias(ks[6], (DEPTH, GDN_HEADS)),
        "gdn_norm_w": gain(ks[7], (DEPTH, GDN_DV)),
        "ssd_conv_w": nrm(ks[8], (DEPTH, CONV_K, SSD_CONV_DIM), CONV_K ** -0.5),
        "ssd_conv_b": nrm(ks[9], (DEPTH, SSD_CONV_DIM), 0.02),
        "ssd_dt_bias": dt_bias(ks[10], (DEPTH, SSD_HEADS)),
        "ssd_a_log": a_log(ks[11], (DEPTH, SSD_HEADS)),
        "ssd_d": 1.0 + 0.1 * jax.random.normal(ks[12], (DEPTH, SSD_HEADS), f32),
        "ssd_norm_w": gain(ks[13], (DEPTH, SSD_INNER)),
        "swa_sinks": nrm(ks[14], (DEPTH, SWA_Q_HEADS), 0.5),
        "w_proj_gdn": nrm(ks[15], (DEPTH, GDN_V_W, D_MODEL), GDN_V_W ** -0.5),
        "w_proj_ssd": nrm(ks[16], (DEPTH, SSD_INNER, D_MODEL), SSD_INNER ** -0.5),
        "w_proj_swa": nrm(ks[17], (DEPTH, SWA_Q_W, D_MODEL), SWA_Q_W ** -0.5),
        "w_out": nrm(ks[18], (DEPTH, D_MODEL, D_MODEL), D_MODEL ** -0.5),
        "norm2_w": gain(ks[19], (DEPTH, D_MODEL)),
        "w_up": nrm(ks[20], (DEPTH, D_MODEL, D_FF), D_MODEL ** -0.5),
        "w_down": nrm(ks[21], (DEPTH, D_FF, D_MODEL), D_FF ** -0.5),
        "final_norm_w": gain(ks[22], (D_MODEL,)),
    }


def reference(x, meta_tokens, norm1_w, w_in, gdn_conv_w, gdn_a_log, gdn_dt_bias, gdn_norm_w,
              ssd_conv_w, ssd_conv_b, ssd_dt_bias, ssd_a_log, ssd_d, ssd_norm_w, swa_sinks,
              w_proj_gdn, w_proj_ssd, w_proj_swa, w_out, norm2_w, w_up, w_down, final_norm_w):
    bsz = x.shape[0]
    meta = jnp.broadcast_to(meta_tokens.astype(x.dtype)[None], (bsz, N_META, D_MODEL))
    h = jnp.concatenate([meta, x], axis=1)
    for l in range(DEPTH):
        u = rmsnorm(h, norm1_w[l]) @ w_in[l]
        (a_q, a_k, a_v, a_gate, a_b, a_a, b_z, b_xbc, b_dt, c_q, c_k, c_v, gate_logits) = split_in(u)
        y_gdn = gdn_branch(a_q, a_k, a_v, a_gate, a_b, a_a, gdn_conv_w[l], gdn_a_log[l], gdn_dt_bias[l], gdn_norm_w[l])
        y_ssd = ssd_branch(b_z, b_xbc, b_dt, ssd_conv_w[l], ssd_conv_b[l], ssd_dt_bias[l], ssd_a_log[l], ssd_d[l], ssd_norm_w[l])
        y_swa = swa_branch(c_q, c_k, c_v, swa_sinks[l])
        gates = jax.nn.sigmoid(gate_logits.astype(jnp.float32)).astype(h.dtype)
        g_a, g_b, g_c = jnp.split(gates, N_BRANCH, axis=-1)
        merged = (g_a * (y_gdn @ w_proj_gdn[l]) + g_b * (y_ssd @ w_proj_ssd[l])
                  + g_c * (y_swa @ w_proj_swa[l]))
        h = h + merged @ w_out[l]
        hn = rmsnorm(h, norm2_w[l])
        h = h + jnp.square(jax.nn.relu(hn @ w_up[l])) @ w_down[l]
    return rmsnorm(h, final_norm_w)[:, N_META:]
```

```python
import numpy as np
from contextlib import ExitStack
import concourse.bass as bass
import concourse.mybir as mybir
from concourse.bass_utils import run_bass_kernel_spmd

F32 = mybir.dt.float32
BF16 = mybir.dt.bfloat16
AF = mybir.ActivationFunctionType
ALU = mybir.AluOpType
AX = mybir.AxisListType

D = 1024
SEQ = 4096
NMETA = 16
DFF = 4096
IN_W = 11808
O_GQ, O_GK, O_GV, O_GG, O_GB, O_GA = 0, 1024, 2048, 3072, 4096, 4104
O_SZ, O_SX, O_SB, O_SC, O_SDT = 4112, 5136, 6160, 6672, 7184
O_CQ, O_CK, O_CV, O_GATE = 7200, 8224, 8480, 8736
RMS_EPS = 1e-6
L2_EPS = 1e-6
R_N1, R_N2, R_GCW, R_SCW, R_SCB, R_GNW, R_SNW = 0, 8, 16, 112, 176, 192, 193
C_GAL, C_GDB, C_SDB, C_SAL, C_SD, C_SNK, PTM_W = 0, 8, 16, 32, 48, 64, 80


class Buf:
    __slots__ = ("name", "lw", "rd")

    def __init__(self, name=""):
        self.name = name
        self.lw = None
        self.rd = {}


class Sched:
    ENG = ("pe", "act", "dve", "pool", "sp")
    EPOCH = 30000

    def __init__(self, nc, stack, n_dma_sems=24):
        self.nc = nc
        self.stack = stack
        self.eng = {"pe": nc.tensor, "act": nc.scalar, "dve": nc.vector,
                    "pool": nc.gpsimd, "sp": nc.sync}
        self.semh = {}
        self.cnt = {}
        self.epoch = {}
        for e in self.ENG:
            self.epoch[e] = 0
            self.cnt[e] = 0
            self.semh[(e, 0)] = stack.enter_context(nc.semaphore(f"s_{e}_0"))
        self.waited = {e: {} for e in self.ENG}
        self.ndma = n_dma_sems
        self.dma_tot = [0] * n_dma_sems
        for j in range(n_dma_sems):
            self.semh[("d", j)] = stack.enter_context(nc.semaphore(f"s_dma_{j}"))
        self.dma_next = 0
        self.n_ins = 0
        self.n_wait = 0

    def _wait(self, e, tok):
        key, val = tok
        if self.waited[e].get(key, 0) >= val:
            return
        if key[0] == e and e == "pe":
            return
        self.eng[e].wait_ge(self.semh[key], val)
        self.waited[e][key] = val
        self.n_wait += 1

    def _deps(self, reads, writes):
        deps = {}

        def add(k, v):
            if deps.get(k, 0) < v:
                deps[k] = v
        for b in reads:
            if b.lw is not None:
                add(*b.lw)
        for b in writes:
            if b.lw is not None:
                add(*b.lw)
            for k, v in b.rd.items():
                add(k, v)
        return deps

    def _mark(self, tok, reads, writes):
        k, v = tok
        for b in reads:
            if b.rd.get(k, 0) < v:
                b.rd[k] = v
        for b in writes:
            b.lw = tok
            b.rd = {}

    def op(self, e, fn, reads=(), writes=()):
        deps = self._deps(reads, writes)
        for k, v in deps.items():
            self._wait(e, (k, v))
        if self.cnt[e] >= self.EPOCH:
            self.epoch[e] += 1
            self.cnt[e] = 0
            self.semh[(e, self.epoch[e])] = self.stack.enter_context(
                self.nc.semaphore(f"s_{e}_{self.epoch[e]}"))
        ins = fn(self.eng[e])
        self.cnt[e] += 1
        key = (e, self.epoch[e])
        ins.then_inc(self.semh[key], 1)
        tok = (key, self.cnt[e])
        self._mark(tok, reads, writes)
        self.n_ins += 1
        return tok

    def dma(self, pairs, reads=(), writes=(), q="sp"):
        j = self.dma_next
        self.dma_next = (self.dma_next + 1) % self.ndma
        key = ("d", j)
        if self.dma_tot[j] > 0:
            self._wait(q, (key, self.dma_tot[j]))
        deps = self._deps(reads, writes)
        for k, v in deps.items():
            self._wait(q, (k, v))
        for (o, i) in pairs:
            self.eng[q].dma_start(out=o, in_=i).then_inc(self.semh[key], 16)
            self.dma_tot[j] += 16
            self.n_ins += 1
        tok = (key, self.dma_tot[j])
        self._mark(tok, reads, writes)
        return tok

    def all_tokens(self):
        toks = []
        for e in self.ENG:
            if self.cnt[e] > 0:
                toks.append(((e, self.epoch[e]), self.cnt[e]))
            elif self.epoch[e] > 0:
                toks.append(((e, self.epoch[e] - 1), self.EPOCH))
        toks += [(("d", j), self.dma_tot[j]) for j in range(self.ndma) if self.dma_tot[j] > 0]
        return toks

    def barrier(self):
        toks = self.all_tokens()
        for e in self.ENG:
            for t in toks:
                self._wait(e, t)

    def final_wait(self, e="sp"):
        for t in self.all_tokens():
            self._wait(e, t)


def build(NST=8, NL=2, TPS=4, dbg=False, phases="GSCMF"):
    TS = TPS * 128
    TMAX = TS + NMETA
    NTILE = TPS + 1
    nc = bass.Bass("TRN2", target_bir_lowering=False)
    dram = lambda n, s, k="ExternalInput": nc.dram_tensor(n, s, F32, kind=k).ap()
    x_d = dram("x", [SEQ, D])
    meta_d = dram("meta", [NMETA, D])
    win_d = dram("w_in", [2, D, IN_W])
    wpg_d = dram("w_pg", [2, D, D])
    wps_d = dram("w_ps", [2, D, D])
    wpc_d = dram("w_pc", [2, D, D])
    wout_d = dram("w_out", [2, D, D])
    wup_d = dram("w_up", [2, D, DFF])
    wdn_d = dram("w_dn", [2, DFF, D])
    pfm_d = dram("pfm", [2, 256, 128])
    ptm_d = dram("ptm", [2, PTM_W])
    fnw_d = dram("fnw", [1, D])
    out_d = dram("out", [NST * TS, D], "ExternalOutput")
    dbg_d = dram("dbg", [128, 8 * 528], "ExternalOutput") if dbg else None

    with ExitStack() as st:
        S = Sched(nc, st)
        sbytes = [0]

        def sb(name, shape, dt=F32, stack=None):
            sbytes[0] += 1
            t = (stack or st).enter_context(nc.sbuf_tensor(f"sb{sbytes[0]}_{name}", shape, dt))
            return t

        def op(e, fn, r=(), w=()):
            w = list(w) + [b for b in r if b.name.startswith("bank") and b not in w]
            return S.op(e, fn, reads=r, writes=w)

        banks = [st.enter_context(nc.psum_tensor(f"bank{i}", [128, 512], F32)) for i in range(8)]
        bbuf = [Buf(f"bank{i}") for i in range(8)]
        reserved = [False] * 8
        bank_rr = [0]

        def bank(reserve=False):
            for _ in range(8):
                i = bank_rr[0]
                bank_rr[0] = (i + 1) % 8
                if not reserved[i]:
                    if reserve:
                        reserved[i] = True
                    return i
            raise RuntimeError("no psum bank")

        def mm(bi, out_ap, lhsT, rhs, start, stop, r):
            op("pe", lambda e: e.matmul(out_ap, lhsT=lhsT, rhs=rhs, start=start, stop=stop), r, [bbuf[bi]])

        def tr(bi, out_ap, in_ap, kparts, r):
            op("pe", lambda e: e.transpose(out=out_ap, in_=in_ap, identity=ident[:kparts, :kparts]),
               list(r) + [b_const], [bbuf[bi]])

        ident = sb("ident", [128, 128])
        Ui = sb("Ui", [128, 128])
        Ls = sb("Ls", [128, 128])
        nUs = sb("nUs", [128, 128])
        ones = sb("ones", [128, 128])
        b_const = Buf("const")
        for t_, val in ((ident, 1.0), (Ui, 1.0), (Ls, 1.0), (nUs, -1.0), (ones, 1.0)):
            op("pool", lambda e, t_=t_, val=val: e.memset(t_[:], val), [], [b_const])
        sel = lambda t_, pat, cm, cmp: op("pool", lambda e: e.affine_select(
            out=t_[:], in_=t_[:], pattern=[[pat, 128]], compare_op=cmp, fill=0.0, base=0, channel_multiplier=cm),
            [b_const], [b_const])
        sel(ident, -1, 1, ALU.is_equal)
        sel(Ui, 1, -1, ALU.is_ge)
        sel(Ls, -1, 1, ALU.is_gt)
        sel(nUs, 1, -1, ALU.is_gt)

        pfmT = sb("pfmT", [128, 2, 256])
        ptm = sb("ptm", [128, 2, PTM_W])
        fnw = sb("fnw", [128, D])
        negA_g = sb("negA_g", [128, 2, 8])
        A_s = sb("A_s", [128, 2, 16])
        esink = sb("esink", [128, 2, 16])
        b_par = Buf("par")
        ptmp = sb("ptmp", [128, 2, 128])
        b_ptmp = Buf("ptmp")
        for l in range(2):
            S.dma([(ptmp[:, 0, :], pfm_d[l, 0:128, :]), (ptmp[:, 1, :], pfm_d[l, 128:256, :])],
                  writes=[b_ptmp], q="act")
            bi = bank()
            for hlf in range(2):
                tr(bi, banks[bi][:, hlf * 128:(hlf + 1) * 128], ptmp[:, hlf, :], 128, [b_ptmp])
            op("dve", lambda e: e.tensor_copy(out=pfmT[:, l, :], in_=banks[bi][:, 0:256]), [bbuf[bi]], [b_par])
            S.dma([(ptm[:, l, :], ptm_d[l:l + 1, :].partition_broadcast(128))], writes=[b_par], q="act")
        S.dma([(fnw[:], fnw_d[0:1, :].partition_broadcast(128))], writes=[b_par], q="act")
        for l in range(2):
            op("act", lambda e: e.activation(out=negA_g[:, l, :], in_=ptm[:, l, C_GAL:C_GAL + 8], func=AF.Exp), [b_par], [b_par])
            op("act", lambda e: e.activation(out=A_s[:, l, :], in_=ptm[:, l, C_SAL:C_SAL + 16], func=AF.Exp), [b_par], [b_par])
            op("act", lambda e: e.activation(out=esink[:, l, :], in_=ptm[:, l, C_SNK:C_SNK + 16], func=AF.Exp), [b_par], [b_par])
            op("dve", lambda e: e.tensor_scalar(out=negA_g[:, l, :], in0=negA_g[:, l, :], scalar1=-1.0, scalar2=None, op0=ALU.mult), [b_par], [b_par])
            op("dve", lambda e: e.tensor_scalar(out=A_s[:, l, :], in0=A_s[:, l, :], scalar1=-1.0, scalar2=None, op0=ALU.mult), [b_par], [b_par])
        pcol = lambda l, row: pfmT[:, l, row:row + 1]

        h = sb("h", [128, NTILE, D])
        b_h = [Buf(f"h{i}") for i in range(NTILE)]
        xnT = sb("xnT", [128, 8, TMAX], BF16)
        b_xnT = Buf("xnT")
        y_g = sb("y_g", [128, 8, TMAX], BF16)
        y_s = sb("y_s", [128, 8, TMAX], BF16)
        y_c = sb("y_c", [128, 8, TMAX], BF16)
        b_yg, b_ys, b_yc = Buf("yg"), Buf("ys"), Buf("yc")
        Sg = sb("Sg", [128, 2, 8, 128])
        b_Sg = [[Buf() for _ in range(8)] for _ in range(2)]
        Hs = sb("Hs", [128, 2, 4, 256])
        b_Hs = [[Buf() for _ in range(4)] for _ in range(2)]
        halo_g = sb("halo_g", [128, 2, 24, 3])
        halo_s = sb("halo_s", [128, 2, 16, 3])
        b_halo = Buf("halo")
        KW = NMETA + 128 + TS
        kTc = sb("kTc", [64, 2, 4, KW], BF16)
        b_kT = [Buf() for _ in range(2)]
        vA = sb("vA", [128, 2, 2 + TPS, 4, 65], BF16)
        b_vA = [Buf() for _ in range(2)]
        for t_ in (Sg, Hs, halo_g, halo_s):
            op("pool", lambda e, t_=t_: e.memset(t_[:], 0.0), [], [b_halo])
        op("pool", lambda e: e.memset(kTc[:], 0.0), [], [b_kT[0], b_kT[1]])
        op("pool", lambda e: e.memset(vA[:], 1.0), [], [b_vA[0], b_vA[1]])
        for l in range(2):
            for hh in range(8):
                b_Sg[l][hh].lw = b_halo.lw
            for g_ in range(4):
                b_Hs[l][g_].lw = b_halo.lw

        NSTG, NWB = 2, 2
        stg = [sb(f"stg{i}", [128, 2048]) for i in range(NSTG)]
        b_stg = [Buf() for _ in range(NSTG)]
        wbf = [sb(f"wbf{i}", [128, 4096], BF16) for i in range(NWB)]
        b_wbf = [Buf() for _ in range(NWB)]
        wrr = [0, 0]

        def load_w(parts, kc, cols):
            wi = wrr[1]
            wrr[1] = (wi + 1) % NWB
            wv = wbf[wi][:, 0:kc * cols].rearrange("p (k c) -> p k c", k=kc)
            nsplit = 2 if kc * cols > 2048 else 1
            assert kc % nsplit == 0 and kc * cols // nsplit <= 2048
            kh = kc // nsplit
            for hf in range(nsplit):
                si = wrr[0]
                wrr[0] = (si + 1) % NSTG
                sv = stg[si][:, 0:kh * cols].rearrange("p (k c) -> p k c", k=kh)
                pairs = []
                c0 = 0
                for d_ap in parts:
                    c = d_ap.shape[1]
                    pairs.append((sv[:, :, c0:c0 + c], d_ap[hf * kh * 128:(hf + 1) * kh * 128, :].rearrange("(k p) c -> p k c", p=128)))
                    c0 += c
                assert c0 == cols
                S.dma(pairs, writes=[b_stg[si]], q="sp")
                op("pool", lambda e: e.tensor_copy(out=wv[:, hf * kh:(hf + 1) * kh, :], in_=sv), [b_stg[si]], [b_wbf[wi]])
            return wv, b_wbf[wi]

        def softplus_inplace(x_ap, t_ap, bx, bt):
            op("act", lambda e: e.activation(out=t_ap, in_=x_ap, func=AF.Abs), [bx], [bt])
            op("act", lambda e: e.activation(out=t_ap, in_=t_ap, func=AF.Exp, scale=-1.0), [bt], [bt])
            op("act", lambda e: e.activation(out=t_ap, in_=t_ap, func=AF.Ln, bias=1.0, scale=1.0), [bt], [bt])
            op("dve", lambda e: e.scalar_tensor_tensor(out=x_ap, in0=x_ap, scalar=0.0, in1=t_ap, op0=ALU.max, op1=ALU.add), [bx, bt], [bx])

        def rsqrt_inplace(x_ap, bx, scale, eps):
            op("act", lambda e: e.activation(out=x_ap, in_=x_ap, func=AF.Sqrt, bias=eps, scale=scale), [bx], [bx])
            op("dve", lambda e: e.reciprocal(out=x_ap, in_=x_ap), [bx], [bx])

        nscr = sb("nscr", [128, D])
        b_nscr = Buf("nscr")
        nsm = sb("nsm", [128, NTILE])
        b_nsm = Buf("nsm")

        def norm_to_FM(tiles, l, row):
            for i, (off, n) in enumerate(tiles):
                op("act", lambda e: e.activation(out=nscr[:n, :], in_=h[:n, i, :], func=AF.Square, accum_out=nsm[:n, i:i + 1]),
                   [b_h[i]], [b_nscr, b_nsm])
                rsqrt_inplace(nsm[:n, i:i + 1], b_nsm, 1.0 / D, RMS_EPS)
                op("dve", lambda e: e.tensor_scalar(out=nscr[:n, :], in0=h[:n, i, :], scalar1=nsm[:n, i:i + 1], scalar2=None, op0=ALU.mult),
                   [b_h[i], b_nsm], [b_nscr])
                for half in range(2):
                    bi = bank()
                    pv = banks[bi][:, :].rearrange("p (c t) -> p c t", c=4)
                    for c in range(4):
                        cc = half * 4 + c
                        tr(bi, pv[:, c, 0:n], nscr[:n, cc * 128:(cc + 1) * 128], n, [b_nscr])
                    op("dve", lambda e: e.tensor_tensor(
                        out=xnT[:, half * 4:half * 4 + 4, off:off + n], in0=pv[:, :, 0:n],
                        in1=pfmT[:, l, row + half * 4:row + half * 4 + 4].unsqueeze(2).to_broadcast([128, 4, n]), op=ALU.mult),
                        [bbuf[bi], b_par], [b_xnT])

        def proj_FM(bi, wv, bw, c0, ncol, src, bsrc, s0, sn, kcs=8):
            for kc in range(kcs):
                mm(bi, banks[bi][:ncol, 0:sn], wv[:, kc, c0:c0 + ncol], src[:, kc, s0:s0 + sn], kc == 0, kc == kcs - 1, [bw, bsrc])

        def dbg_tap(src, bsrc):
            with ExitStack() as ph2:
                dbg_copy = sb("dbgc", [128, 8, TMAX], F32, ph2)
                b_dbg = Buf()
                op("dve", lambda e: e.tensor_copy(out=dbg_copy[:, :, :], in_=src[:, :, :]), [bsrc], [b_dbg])
                S.dma([(dbg_d.rearrange("p (c t) -> p c t", c=8), dbg_copy[:, :, :])], reads=[b_dbg], q="act")
                S.barrier()

        for s in range(NST):
            if s == 0:
                tiles = [(0, NMETA)] + [(NMETA + 128 * i, 128) for i in range(TPS)]
                segs = [(0, NMETA), (NMETA, TS)]
                T = TMAX
            else:
                tiles = [(128 * i, 128) for i in range(TPS)]
                segs = [(0, TS)]
                T = TS
            seq0 = s * TS
            if s == 0:
                cgroups = [([0], NMETA), ([NMETA + 64 * j for j in range(2 * TPS)], 64)]
            else:
                cgroups = [([64 * j for j in range(2 * TPS)], 64)]
            nchunk = sum(len(g[0]) for g in cgroups)
            pairs = []
            wl = []
            for i, (off, n) in enumerate(tiles):
                if s == 0 and i == 0:
                    pairs.append((h[:n, i, :], meta_d[:, :]))
                else:
                    r0 = seq0 + off - (NMETA if s == 0 else 0)
                    pairs.append((h[:n, i, :], x_d[r0:r0 + n, :]))
                wl.append(b_h[i])
            S.dma(pairs, writes=wl, q="act")

            for l in range(NL):
                norm_to_FM(tiles, l, R_N1)
                with ExitStack() as ph:
                  if "G" in phases:
                    psb = lambda n_, sh, dt=F32: sb(n_, sh, dt, ph)
                    NCH = 2 * TPS + 1
                    ba = psb("g_ba", [64, NCH, 16]); b_ba = Buf()
                    tsm = psb("g_tsm", [64, NCH, 8]); b_tsm = Buf()
                    beta = psb("g_beta", [64, NCH, 8]); gsm = psb("g_gsm", [64, NCH, 8])
                    bk = psb("g_bk", [64, NCH, 8]); etail = psb("g_etail", [64, NCH, 8])
                    eglast = psb("g_eglast", [128, NCH, 8]); b_sm = Buf()
                    wv, bw = load_w([win_d[l, :, O_GB:O_GB + 16]], 8, 16)
                    ci = 0
                    cinfo = []
                    for offs, cs in cgroups:
                        bi = bank()
                        for j, off in enumerate(offs):
                            for kc in range(8):
                                mm(bi, banks[bi][:cs, j * 16:(j + 1) * 16], xnT[:, kc, off:off + cs], wv[:, kc, 0:16], kc == 0, kc == 7, [b_xnT, bw])
                            cinfo.append((ci + j, off, cs))
                        nj = len(offs)
                        op("dve", lambda e: e.tensor_copy(out=ba[:cs, ci:ci + nj, :], in_=banks[bi][:cs, 0:nj * 16].rearrange("p (j c) -> p j c", c=16)), [bbuf[bi]], [b_ba])
                        ci += nj
                    assert ci == nchunk
                    NC_ = nchunk
                    op("act", lambda e: e.activation(out=beta[:, 0:NC_, :], in_=ba[:, 0:NC_, 0:8], func=AF.Sigmoid), [b_ba], [b_sm])
                    op("dve", lambda e: e.tensor_tensor(out=gsm[:, 0:NC_, :], in0=ba[:, 0:NC_, 8:16], in1=ptm[:64, l, C_GDB:C_GDB + 8].unsqueeze(1).to_broadcast([64, NC_, 8]), op=ALU.add), [b_ba, b_par], [b_sm])
                    softplus_inplace(gsm[:, 0:NC_, :].rearrange("p j c -> p (j c)"), tsm[:, 0:NC_, :].rearrange("p j c -> p (j c)"), b_sm, b_tsm)
                    op("dve", lambda e: e.tensor_tensor(out=gsm[:, 0:NC_, :], in0=gsm[:, 0:NC_, :], in1=negA_g[:64, l, :].unsqueeze(1).to_broadcast([64, NC_, 8]), op=ALU.mult), [b_sm, b_par], [b_sm])
                    ci = 0
                    for offs, cs in cgroups:
                        nj = len(offs)
                        rhs = gsm[:cs, ci:ci + nj, :].rearrange("p j c -> p (j c)")
                        bi = bank()
                        mm(bi, banks[bi][:cs, 0:nj * 8], Ui[:cs, :cs], rhs, True, True, [b_sm, b_const])
                        mm(bi, banks[bi][:, 128:128 + nj * 8], ones[:cs, :], rhs, True, True, [b_sm, b_const])
                        gam_v = banks[bi][:cs, 0:nj * 8].rearrange("p (j c) -> p j c", c=8)
                        gl_v = banks[bi][:, 128:128 + nj * 8].rearrange("p (j c) -> p j c", c=8)
                        op("act", lambda e: e.activation(out=bk[:cs, ci:ci + nj, :], in_=gam_v, func=AF.Exp), [bbuf[bi]], [b_sm])
                        op("dve", lambda e: e.tensor_tensor(out=bk[:cs, ci:ci + nj, :], in0=bk[:cs, ci:ci + nj, :], in1=beta[:cs, ci:ci + nj, :], op=ALU.mult), [b_sm], [b_sm])
                        op("act", lambda e: e.activation(out=eglast[:, ci:ci + nj, :], in_=gl_v, func=AF.Exp), [bbuf[bi]], [b_sm])
                        op("act", lambda e: e.activation(out=tsm[:cs, ci:ci + nj, :], in_=gl_v[:cs], func=AF.Copy), [bbuf[bi]], [b_tsm])
                        op("dve", lambda e: e.tensor_tensor(out=etail[:cs, ci:ci + nj, :], in0=tsm[:cs, ci:ci + nj, :], in1=gam_v, op=ALU.subtract), [b_tsm, bbuf[bi]], [b_sm])
                        op("act", lambda e: e.activation(out=etail[:cs, ci:ci + nj, :], in_=etail[:cs, ci:ci + nj, :], func=AF.Exp), [b_sm], [b_sm])
                        ci += nj
                    xq = psb("g_xq", [128, 3, TMAX + 3]); b_xq = Buf()
                    cq = psb("g_cq", [128, 3, TMAX]); b_cq = Buf()
                    sgt = psb("g_sgt", [128, TMAX]); b_sgt = Buf()
                    sq = psb("g_sq", [128, TMAX]); b_sq = Buf()
                    rin = psb("g_rin", [128, TMAX]); b_rin = Buf()
                    egb = psb("g_egb", [128, TMAX]); b_egb = Buf()
                    kTb = psb("g_kTb", [128, TMAX], BF16); qTb = psb("g_qTb", [128, TMAX], BF16); qdb = psb("g_qdb", [128, TMAX], BF16); b_qk = Buf()
                    m64 = [psb(f"g_m{i}", [64, 2 * TPS, 64]) for i in range(9)]
                    b_m = [Buf() for _ in range(9)]
                    E_, DT_, MB_, Bm, BT_, P_, PT_, M_, M2_ = m64
                    bE, bDT, bMB, bBm, bBT, bP, bPT, bM, bM2 = b_m
                    Rk = psb("g_Rk", [64, 2 * TPS, 128], BF16); Rv = psb("g_Rv", [64, 2 * TPS, 128], BF16); ktl = psb("g_ktl", [64, 2 * TPS, 128], BF16); b_R = Buf()
                    TTb = psb("g_TTb", [64, 2 * TPS, 64], BF16); aTb = psb("g_aTb", [64, 2 * TPS, 64], BF16); b_TT = Buf(); b_aT = Buf()
                    nWT = psb("g_nWT", [128, 2 * TPS, 64], BF16); b_nWT = Buf()
                    oT = psb("g_oT", [128, TMAX]); b_oT = Buf()
                    vnb = psb("g_vnb", [64, 128], BF16); b_vnb = Buf()
                    Sb = psb("g_Sb", [128, 128], BF16); b_Sb = Buf()
                    for hh in range(8):
                        wv, bw = load_w([win_d[l, :, O_GQ + hh * 128:O_GQ + (hh + 1) * 128], win_d[l, :, O_GK + hh * 128:O_GK + (hh + 1) * 128],
                                         win_d[l, :, O_GV + hh * 128:O_GV + (hh + 1) * 128], win_d[l, :, O_GG + hh * 128:O_GG + (hh + 1) * 128]], 8, 512)
                        for qi in range(3):
                            op("dve", lambda e: e.tensor_copy(out=xq[:, qi, 0:3], in_=halo_g[:, l, qi * 8 + hh, :]), [b_halo], [b_xq])
                        for qi in range(4):
                            for (s0, sn) in segs:
                                bi = bank()
                                proj_FM(bi, wv, bw, qi * 128, 128, xnT, b_xnT, s0, sn)
                                if qi < 3:
                                    op("act", lambda e: e.activation(out=xq[:, qi, 3 + s0:3 + s0 + sn], in_=banks[bi][:, 0:sn], func=AF.Copy), [bbuf[bi]], [b_xq])
                                else:
                                    op("act", lambda e: e.activation(out=sgt[:, s0:s0 + sn], in_=banks[bi][:, 0:sn], func=AF.Silu), [bbuf[bi]], [b_sgt])
                        for qi in range(3):
                            chn = qi * 8 + hh
                            cw = lambda tap: pcol(l, R_GCW + tap * 24 + chn)
                            op("dve", lambda e: e.tensor_scalar(out=cq[:, qi, 0:T], in0=xq[:, qi, 3:3 + T], scalar1=cw(3), scalar2=None, op0=ALU.mult), [b_xq, b_par], [b_cq])
                            for tap in range(3):
                                op("dve", lambda e: e.scalar_tensor_tensor(out=cq[:, qi, 0:T], in0=xq[:, qi, tap:tap + T], scalar=cw(tap), in1=cq[:, qi, 0:T], op0=ALU.mult, op1=ALU.add), [b_xq, b_par, b_cq], [b_cq])
                            op("dve", lambda e: e.tensor_copy(out=halo_g[:, l, chn, :], in_=xq[:, qi, T:T + 3]), [b_xq], [b_halo])
                            op("act", lambda e: e.activation(out=cq[:, qi, 0:T], in_=cq[:, qi, 0:T], func=AF.Silu), [b_cq], [b_cq])
                        for qi in range(2):
                            op("pool", lambda e: e.tensor_tensor(out=sq[:, 0:T], in0=cq[:, qi, 0:T], in1=cq[:, qi, 0:T], op=ALU.mult), [b_cq], [b_sq])
                            for (s0, sn) in segs:
                                bi = bank()
                                mm(bi, banks[bi][:, 0:sn], ones[:, :], sq[:, s0:s0 + sn], True, True, [b_sq, b_const])
                                op("act", lambda e: e.activation(out=rin[:, s0:s0 + sn], in_=banks[bi][:, 0:sn], func=AF.Sqrt, bias=L2_EPS, scale=1.0), [bbuf[bi]], [b_rin])
                            op("dve", lambda e: e.reciprocal(out=rin[:, 0:T], in_=rin[:, 0:T]), [b_rin], [b_rin])
                            if qi == 0:
                                op("dve", lambda e: e.scalar_tensor_tensor(out=cq[:, 0, 0:T], in0=cq[:, 0, 0:T], scalar=128.0 ** -0.5, in1=rin[:, 0:T], op0=ALU.mult, op1=ALU.mult), [b_cq, b_rin], [b_cq])
                            else:
                                op("dve", lambda e: e.tensor_tensor(out=cq[:, 1, 0:T], in0=cq[:, 1, 0:T], in1=rin[:, 0:T], op=ALU.mult), [b_cq, b_rin], [b_cq])
                        op("pool", lambda e: e.tensor_copy(out=kTb[:, 0:T], in_=cq[:, 1, 0:T]), [b_cq], [b_qk])
                        op("pool", lambda e: e.tensor_copy(out=qTb[:, 0:T], in_=cq[:, 0, 0:T]), [b_cq], [b_qk])
                        ci = 0
                        for offs, cs in cgroups:
                            nj = len(offs)
                            gs0 = offs[0]
                            gl_ = nj * cs
                            v3 = lambda t_: t_[:cs, 0:nj, 0:cs]
                            Uib = Ui[:cs, :cs].unsqueeze(1).to_broadcast([cs, nj, cs])
                            op("dve", lambda e: e.tensor_tensor(out=v3(DT_), in0=gsm[:cs, ci:ci + nj, hh].unsqueeze(2).to_broadcast([cs, nj, cs]), in1=Uib, op=ALU.mult), [b_sm, b_const], [bDT])
                            b1 = bank(); b2 = bank(); b3 = bank()
                            p3 = lambda b_: banks[b_][:cs, 0:nj * cs].rearrange("p (j c) -> p j c", c=cs)
                            for j in range(nj):
                                mm(b1, p3(b1)[:, j, :], Ls[:cs, :cs], DT_[:cs, j, 0:cs], True, True, [bDT, b_const])
                            op("act", lambda e: e.activation(out=v3(E_), in_=p3(b1), func=AF.Exp), [bbuf[b1]], [bE])
                            op("dve", lambda e: e.tensor_tensor(out=v3(MB_), in0=beta[:cs, ci:ci + nj, hh].unsqueeze(2).to_broadcast([cs, nj, cs]), in1=ident[:cs, :cs].unsqueeze(1).to_broadcast([cs, nj, cs]), op=ALU.mult), [b_sm, b_const], [bMB])
                            for j in range(nj):
                                mm(b2, p3(b2)[:, j, :], ones[:cs, :cs], MB_[:cs, j, 0:cs], True, True, [bMB, b_const])
                            op("dve", lambda e: e.tensor_tensor(out=v3(MB_), in0=v3(E_), in1=p3(b2), op=ALU.mult), [bE, bbuf[b2]], [bMB])
                            op("pool", lambda e: e.tensor_tensor(out=v3(MB_), in0=v3(MB_), in1=nUs[:cs, :cs].unsqueeze(1).to_broadcast([cs, nj, cs]), op=ALU.mult), [bMB, b_const], [bMB])
                            op("pool", lambda e: e.tensor_tensor(out=v3(DT_), in0=v3(E_), in1=Uib, op=ALU.mult), [bE, b_const], [bDT])
                            for j in range(nj):
                                mm(b3, banks[b3][:, j * cs:(j + 1) * cs], gsm[:cs, ci + j, hh:hh + 1].to_broadcast([cs, 128]), Ui[:cs, :cs], True, True, [b_sm, b_const])
                            op("act", lambda e: e.activation(out=egb[:, gs0:gs0 + gl_], in_=banks[b3][:, 0:gl_], func=AF.Exp), [bbuf[b3]], [b_egb])
                            op("dve", lambda e: e.tensor_tensor(out=qdb[:, gs0:gs0 + gl_], in0=cq[:, 0, gs0:gs0 + gl_], in1=egb[:, gs0:gs0 + gl_], op=ALU.mult), [b_cq, b_egb], [b_qk])
                            for j0 in range(0, nj, 4):
                                jn = min(4, nj - j0)
                                bi = bank()
                                pk = banks[bi][:cs, 0:jn * 128].rearrange("p (j c) -> p j c", c=128)
                                for j in range(jn):
                                    tr(bi, pk[:, j, :], cq[:, 1, offs[j0 + j]:offs[j0 + j] + cs], 128, [b_cq])
                                bcs = lambda t_: t_[:cs, ci + j0:ci + j0 + jn, hh].unsqueeze(2).to_broadcast([cs, jn, 128])
                                op("dve", lambda e: e.tensor_tensor(out=Rk[:cs, j0:j0 + jn, :], in0=pk, in1=bcs(bk), op=ALU.mult), [bbuf[bi], b_sm], [b_R])
                                op("dve", lambda e: e.tensor_tensor(out=ktl[:cs, j0:j0 + jn, :], in0=pk, in1=bcs(etail), op=ALU.mult), [bbuf[bi], b_sm], [b_R])
                                bi = bank()
                                pk2 = banks[bi][:cs, 0:jn * 128].rearrange("p (j c) -> p j c", c=128)
                                for j in range(jn):
                                    tr(bi, pk2[:, j, :], cq[:, 2, offs[j0 + j]:offs[j0 + j] + cs], 128, [b_cq])
                                op("dve", lambda e: e.tensor_tensor(out=Rv[:cs, j0:j0 + jn, :], in0=pk2, in1=bcs(beta), op=ALU.mult), [bbuf[bi], b_sm], [b_R])
                            b1 = bank()
                            for j in range(nj):
                                o_ = offs[j]
                                mm(b1, p3(b1)[:, j, :], kTb[:, o_:o_ + cs], kTb[:, o_:o_ + cs], True, True, [b_qk])
                            op("dve", lambda e: e.tensor_tensor(out=v3(Bm), in0=v3(MB_), in1=p3(b1), op=ALU.mult), [bMB, bbuf[b1]], [bBm])
                            b1 = bank()
                            for j in range(nj):
                                tr(b1, p3(b1)[:, j, :], Bm[:cs, j, 0:cs], cs, [bBm])
                            op("act", lambda e: e.activation(out=v3(BT_), in_=p3(b1), func=AF.Copy), [bbuf[b1]], [bBT])
                            op("pool", lambda e: e.tensor_tensor(out=v3(M_), in0=v3(Bm), in1=ident[:cs, :cs].unsqueeze(1).to_broadcast([cs, nj, cs]), op=ALU.add), [bBm, b_const], [bM])
                            nlev = 5 if cs == 64 else 3
                            Pc, PTc, bPc, bPTc = Bm, BT_, bBm, bBT
                            Pn, PTn, bPn, bPTn = P_, PT_, bP, bPT
                            Mc, Mn, bMc, bMn = M_, M2_, bM, bM2
                            for lev in range(nlev):
                                last = lev == nlev - 1
                                b2 = bank()
                                for j in range(nj):
                                    mm(b2, p3(b2)[:, j, :], Pc[:cs, j, 0:cs], PTc[:cs, j, 0:cs], True, True, [bPc, bPTc])
                                if not last:
                                    b1 = bank()
                                    for j in range(nj):
                                        mm(b1, p3(b1)[:, j, :], PTc[:cs, j, 0:cs], Pc[:cs, j, 0:cs], True, True, [bPc, bPTc])
                                op("act", lambda e: e.activation(out=v3(PTn), in_=p3(b2), func=AF.Copy), [bbuf[b2]], [bPTn])
                                if not last:
                                    op("dve", lambda e: e.tensor_copy(out=v3(Pn), in_=p3(b1)), [bbuf[b1]], [bPn])
                                b3 = bank()
                                for j in range(nj):
                                    mm(b3, p3(b3)[:, j, :], PTn[:cs, j, 0:cs], Mc[:cs, j, 0:cs], True, True, [bPTn, bMc])
                                op("dve", lambda e: e.tensor_tensor(out=v3(Mn), in0=v3(Mc), in1=p3(b3), op=ALU.add), [bMc, bbuf[b3]], [bMn])
                                Pc, Pn, bPc, bPn = Pn, Pc, bPn, bPc
                                PTc, PTn, bPTc, bPTn = PTn, PTc, bPTn, bPTc
                                Mc, Mn, bMc, bMn = Mn, Mc, bMn, bMc
                            op("pool", lambda e: e.tensor_copy(out=v3(TTb), in_=v3(Mc)), [bMc], [b_TT])
                            b1 = bank()
                            for j in range(nj):
                                o_ = offs[j]
                                mm(b1, p3(b1)[:, j, :], kTb[:, o_:o_ + cs], qTb[:, o_:o_ + cs], True, True, [b_qk])
                            op("dve", lambda e: e.tensor_tensor(out=v3(aTb), in0=v3(DT_), in1=p3(b1), op=ALU.mult), [bDT, bbuf[b1]], [b_aT])
                            b1 = bank()
                            for j in range(nj):
                                mm(b1, banks[b1][:, j * cs:(j + 1) * cs], Rk[:cs, j, :], TTb[:cs, j, 0:cs], True, True, [b_R, b_TT])
                            op("act", lambda e: e.activation(out=nWT[:, 0:nj, 0:cs], in_=banks[b1][:, 0:nj * cs].rearrange("p (j c) -> p j c", c=cs), func=AF.Copy, scale=-1.0), [bbuf[b1]], [b_nWT])
                            bo = bank(reserve=True)
                            for j in range(nj):
                                o_ = offs[j]
                                op("act", lambda e: e.activation(out=Sb[:, :], in_=Sg[:, l, hh, :], func=AF.Copy), [b_Sg[l][hh]], [b_Sb])
                                b1 = bank()
                                mm(b1, banks[b1][:cs, 0:128], TTb[:cs, j, 0:cs], Rv[:cs, j, :], True, False, [b_TT, b_R])
                                mm(b1, banks[b1][:cs, 0:128], nWT[:, j, 0:cs], Sb[:, :], False, True, [b_nWT, b_Sb])
                                op("act", lambda e: e.activation(out=vnb[:cs, :], in_=banks[b1][:cs, 0:128], func=AF.Copy), [bbuf[b1]], [b_vnb])
                                mm(bo, banks[bo][:, j * cs:(j + 1) * cs], Sb[:, :], qdb[:, o_:o_ + cs], True, False, [b_Sb, b_qk])
                                mm(bo, banks[bo][:, j * cs:(j + 1) * cs], vnb[:cs, :], aTb[:cs, j, 0:cs], False, True, [b_vnb, b_aT])
                                b2 = bank()
                                mm(b2, banks[b2][:, 0:128], ktl[:cs, j, :], vnb[:cs, :], True, True, [b_R, b_vnb])
                                op("dve", lambda e: e.scalar_tensor_tensor(out=Sg[:, l, hh, :], in0=Sg[:, l, hh, :], scalar=eglast[:, ci + j, hh:hh + 1], in1=banks[b2][:, 0:128], op0=ALU.mult, op1=ALU.add),
                                   [b_Sg[l][hh], b_sm, bbuf[b2]], [b_Sg[l][hh]])
                            op("act", lambda e: e.activation(out=oT[:, gs0:gs0 + gl_], in_=banks[bo][:, 0:gl_], func=AF.Copy), [bbuf[bo]], [b_oT])
                            reserved[bo] = False
                            ci += nj
                        op("pool", lambda e: e.tensor_tensor(out=sq[:, 0:T], in0=oT[:, 0:T], in1=oT[:, 0:T], op=ALU.mult), [b_oT], [b_sq])
                        for (s0, sn) in segs:
                            bi = bank()
                            mm(bi, banks[bi][:, 0:sn], ones[:, :], sq[:, s0:s0 + sn], True, True, [b_sq, b_const])
                            op("act", lambda e: e.activation(out=rin[:, s0:s0 + sn], in_=banks[bi][:, 0:sn], func=AF.Sqrt, bias=RMS_EPS, scale=1.0 / 128), [bbuf[bi]], [b_rin])
                        op("dve", lambda e: e.reciprocal(out=rin[:, 0:T], in_=rin[:, 0:T]), [b_rin], [b_rin])
                        op("dve", lambda e: e.tensor_tensor(out=oT[:, 0:T], in0=oT[:, 0:T], in1=rin[:, 0:T], op=ALU.mult), [b_oT, b_rin], [b_oT])
                        op("dve", lambda e: e.scalar_tensor_tensor(out=y_g[:, hh, 0:T], in0=oT[:, 0:T], scalar=pcol(l, R_GNW), in1=sgt[:, 0:T], op0=ALU.mult, op1=ALU.mult), [b_oT, b_par, b_sgt], [b_yg])
                    S.barrier()
                with ExitStack() as ph:
                  if "S" in phases:
                    psb = lambda n_, sh, dt=F32: sb(n_, sh, dt, ph)
                    NT_ = len(tiles)
                    dtp = psb("s_dtp", [128, NTILE, 16]); adt = psb("s_adt", [128, NTILE, 16]); tsm2 = psb("s_tsm", [128, NTILE, 16])
                    eacum = psb("s_eacum", [128, NTILE, 16]); edec = psb("s_edec", [128, NTILE, 16]); echk = psb("s_echk", [128, NTILE, 16])
                    dte = psb("s_dte", [128, NTILE, 16])
                    b_ss = Buf(); b_st = Buf()
                    wv, bw = load_w([win_d[l, :, O_SDT:O_SDT + 16]], 8, 16)
                    bi = bank()
                    for i, (off, n) in enumerate(tiles):
                        for kc in range(8):
                            mm(bi, banks[bi][:n, i * 16:(i + 1) * 16], xnT[:, kc, off:off + n], wv[:, kc, 0:16], kc == 0, kc == 7, [b_xnT, bw])
                    op("dve", lambda e: e.tensor_tensor(out=dtp[:, 0:NT_, :], in0=banks[bi][:, 0:NT_ * 16].rearrange("p (j c) -> p j c", c=16),
                                                        in1=ptm[:, l, C_SDB:C_SDB + 16].unsqueeze(1).to_broadcast([128, NT_, 16]), op=ALU.add), [bbuf[bi], b_par], [b_ss])
                    softplus_inplace(dtp[:, 0:NT_, :].rearrange("p j c -> p (j c)"), tsm2[:, 0:NT_, :].rearrange("p j c -> p (j c)"), b_ss, b_st)
                    op("dve", lambda e: e.tensor_tensor(out=adt[:, 0:NT_, :], in0=dtp[:, 0:NT_, :], in1=A_s[:, l, :].unsqueeze(1).to_broadcast([128, NT_, 16]), op=ALU.mult), [b_ss, b_par], [b_ss])
                    for i, (off, n) in enumerate(tiles):
                        bi = bank()
                        mm(bi, banks[bi][:n, 0:16], Ui[:n, :n], adt[:n, i, :], True, True, [b_ss, b_const])
                        mm(bi, banks[bi][:, 16:32], ones[:n, :], adt[:n, i, :], True, True, [b_ss, b_const])
                        op("act", lambda e: e.activation(out=eacum[:n, i, :], in_=banks[bi][:n, 0:16], func=AF.Exp), [bbuf[bi]], [b_ss])
                        op("act", lambda e: e.activation(out=echk[:, i, :], in_=banks[bi][:, 16:32], func=AF.Exp), [bbuf[bi]], [b_ss])
                        op("act", lambda e: e.activation(out=tsm2[:n, i, :], in_=banks[bi][:n, 16:32], func=AF.Copy), [bbuf[bi]], [b_st])
                        op("dve", lambda e: e.tensor_tensor(out=edec[:n, i, :], in0=tsm2[:n, i, :], in1=banks[bi][:n, 0:16], op=ALU.subtract), [b_st, bbuf[bi]], [b_ss])
                        op("act", lambda e: e.activation(out=edec[:n, i, :], in_=edec[:n, i, :], func=AF.Exp), [b_ss], [b_ss])
                        op("dve", lambda e: e.tensor_tensor(out=dte[:n, i, :], in0=dtp[:n, i, :], in1=edec[:n, i, :], op=ALU.mult), [b_ss], [b_ss])
                    SST = 99
                    xs4 = psb("s_xs4", [128, 4, TMAX + 3]); b_xs4 = Buf()
                    cs4 = psb("s_cs4", [128, 4, TMAX]); b_cs4 = Buf()
                    BTb = psb("s_BTb", [128, TMAX], BF16); CTb = psb("s_CTb", [128, TMAX], BF16); b_BC = Buf()
                    sz = psb("s_sz", [128, NTILE, 256]); b_sz = Buf()
                    xs_tm = psb("s_xstm", [128, 256]); b_xstm = Buf()
                    xdt = psb("s_xdt", [128, 4, 64], BF16); xdt2 = psb("s_xdt2", [128, 4, 64], BF16); b_xdt = Buf()
                    Btm = psb("s_Btm", [128, 128], BF16); b_Btm = Buf()
                    La = psb("s_La", [128, 4, 128]); b_La = Buf()
                    E4 = psb("s_E4", [128, 4, 128]); b_E4 = Buf()
                    MT = psb("s_MT", [128, 4, 128], BF16); b_MT = Buf()
                    t1_ = psb("s_t1", [128, 256]); t2_ = psb("s_t2", [128, 256]); b_t1 = Buf(); b_t2 = Buf()
                    t1 = t1_[:, :].rearrange("p (r c) -> p r c", c=64); t2 = t2_[:, :].rearrange("p (r c) -> p r c", c=64)
                    ssm = psb("s_ssm", [128, 1]); b_ssm = Buf()
                    Hb = psb("s_Hb", [128, 256], BF16); b_Hb = Buf()
                    for gi in range(4 if SST > 0 else 0):
                        wa, bwa = load_w([win_d[l, :, O_SX + gi * 256:O_SX + (gi + 1) * 256], win_d[l, :, O_SB + gi * 128:O_SB + (gi + 1) * 128],
                                          win_d[l, :, O_SC + gi * 128:O_SC + (gi + 1) * 128]], 8, 512)
                        chns = [2 * gi, 2 * gi + 1, 8 + gi, 12 + gi]
                        for qi in range(4):
                            op("dve", lambda e: e.tensor_copy(out=xs4[:, qi, 0:3], in_=halo_s[:, l, chns[qi], :]), [b_halo], [b_xs4])
                            for (s0, sn) in segs:
                                bi = bank()
                                proj_FM(bi, wa, bwa, qi * 128, 128, xnT, b_xnT, s0, sn)
                                op("act", lambda e: e.activation(out=xs4[:, qi, 3 + s0:3 + s0 + sn], in_=banks[bi][:, 0:sn], func=AF.Copy), [bbuf[bi]], [b_xs4])
                        for qi in range(4):
                            chn = chns[qi]
                            cw = lambda tap: pcol(l, R_SCW + tap * 16 + chn)
                            op("dve", lambda e: e.tensor_scalar(out=cs4[:, qi, 0:T], in0=xs4[:, qi, 3:3 + T], scalar1=cw(3), scalar2=pcol(l, R_SCB + chn), op0=ALU.mult, op1=ALU.add), [b_xs4, b_par], [b_cs4])
                            for tap in range(3):
                                op("dve", lambda e: e.scalar_tensor_tensor(out=cs4[:, qi, 0:T], in0=xs4[:, qi, tap:tap + T], scalar=cw(tap), in1=cs4[:, qi, 0:T], op0=ALU.mult, op1=ALU.add), [b_xs4, b_par, b_cs4], [b_cs4])
                            op("dve", lambda e: e.tensor_copy(out=halo_s[:, l, chn, :], in_=xs4[:, qi, T:T + 3]), [b_xs4], [b_halo])
                            op("act", lambda e: e.activation(out=cs4[:, qi, 0:T], in_=cs4[:, qi, 0:T], func=AF.Silu), [b_cs4], [b_cs4])
                        op("dve", lambda e: e.tensor_copy(out=BTb[:, 0:T], in_=cs4[:, 2, 0:T]), [b_cs4], [b_BC])
                        op("dve", lambda e: e.tensor_copy(out=CTb[:, 0:T], in_=cs4[:, 3, 0:T]), [b_cs4], [b_BC])
                        wz, bwz = load_w([win_d[l, :, O_SZ + gi * 256:O_SZ + (gi + 1) * 256]], 8, 256)
                        for i, (off, n) in enumerate(tiles):
                            bi = bank()
                            for kc in range(8):
                                mm(bi, banks[bi][:n, 0:256], xnT[:, kc, off:off + n], wz[:, kc, 0:256], kc == 0, kc == 7, [b_xnT, bwz])
                            op("act", lambda e: e.activation(out=sz[:n, i, :], in_=banks[bi][:n, 0:256], func=AF.Silu), [bbuf[bi]], [b_sz])
                        op("act", lambda e: e.activation(out=Hb[:, :], in_=Hs[:, l, gi, :], func=AF.Copy), [b_Hs[l][gi]], [b_Hb])
                        for i, (off, n) in enumerate(tiles if SST > 1 else []):
                            hd = slice(4 * gi, 4 * gi + 4)
                            bc4 = lambda ap_: ap_.unsqueeze(2).to_broadcast([n, 4, 64])
                            bi = bank()
                            tr(bi, banks[bi][:n, 0:128], cs4[:, 0, off:off + n], 128, [b_cs4])
                            tr(bi, banks[bi][:n, 128:256], cs4[:, 1, off:off + n], 128, [b_cs4])
                            px = banks[bi][:n, 0:256].rearrange("p (r c) -> p r c", c=64)
                            op("act", lambda e: e.activation(out=xs_tm[:n, :], in_=banks[bi][:n, 0:256], func=AF.Copy), [bbuf[bi]], [b_xstm])
                            op("dve", lambda e: e.tensor_tensor(out=xdt[:n], in0=px, in1=bc4(dtp[:n, i, hd]), op=ALU.mult), [bbuf[bi], b_ss], [b_xdt])
                            op("dve", lambda e: e.tensor_tensor(out=xdt2[:n], in0=px, in1=bc4(dte[:n, i, hd]), op=ALU.mult), [bbuf[bi], b_ss], [b_xdt])
                            if SST <= 2:
                                continue
                            bi = bank()
                            tr(bi, banks[bi][:n, 0:128], cs4[:, 2, off:off + n], 128, [b_cs4])
                            op("act", lambda e: e.activation(out=Btm[:n, :], in_=banks[bi][:n, 0:128], func=AF.Copy), [bbuf[bi]], [b_Btm])
                            if SST <= 3:
                                continue
                            b1 = bank()
                            mm(b1, banks[b1][:n, 0:n], BTb[:, off:off + n], CTb[:, off:off + n], True, True, [b_BC])
                            op("dve", lambda e: e.tensor_tensor(out=La[:n, :, 0:n], in0=adt[:n, i, hd].unsqueeze(2).to_broadcast([n, 4, n]), in1=Ui[:n, :n].unsqueeze(1).to_broadcast([n, 4, n]), op=ALU.mult), [b_ss, b_const], [b_La])
                            b2 = bank()
                            p2 = banks[b2][:n, 0:4 * n].rearrange("p (r c) -> p r c", c=n)
                            for r_ in range(4):
                                mm(b2, p2[:, r_, :], Ls[:n, :n], La[:n, r_, 0:n], True, True, [b_La, b_const])
                            op("act", lambda e: e.activation(out=E4[:n, :, 0:n], in_=p2, func=AF.Exp), [bbuf[b2]], [b_E4])
                            op("dve", lambda e: e.tensor_tensor(out=E4[:n, :, 0:n], in0=E4[:n, :, 0:n], in1=Ui[:n, :n].unsqueeze(1).to_broadcast([n, 4, n]), op=ALU.mult), [b_E4, b_const], [b_E4])
                            op("dve", lambda e: e.tensor_tensor(out=MT[:n, :, 0:n], in0=E4[:n, :, 0:n], in1=banks[b1][:n, 0:n].unsqueeze(1).to_broadcast([n, 4, n]), op=ALU.mult), [b_E4, bbuf[b1]], [b_MT])
                            if SST <= 4:
                                continue
                            b3 = bank()
                            for r_ in range(4):
                                mm(b3, banks[b3][:n, r_ * 64:(r_ + 1) * 64], MT[:n, r_, 0:n], xdt[:n, r_, :], True, True, [b_MT, b_xdt])
                            b4 = bank()
                            mm(b4, banks[b4][:n, 0:256], CTb[:, off:off + n], Hb[:, :], True, True, [b_BC, b_Hb])
                            b5 = bank()
                            mm(b5, banks[b5][:, 0:256], Btm[:n, :], xdt2[:n].rearrange("p r c -> p (r c)"), True, True, [b_Btm, b_xdt])
                            if SST <= 5:
                                continue
                            v4 = lambda b_: banks[b_][:n, 0:256].rearrange("p (r c) -> p r c", c=64)
                            op("dve", lambda e: e.tensor_tensor(out=t1[:n], in0=v4(b4), in1=bc4(eacum[:n, i, hd]), op=ALU.mult), [bbuf[b4], b_ss], [b_t1])
                            op("dve", lambda e: e.tensor_tensor(out=t1[:n], in0=t1[:n], in1=v4(b3), op=ALU.add), [b_t1, bbuf[b3]], [b_t1])
                            op("dve", lambda e: e.tensor_tensor(out=t2[:n], in0=xs_tm[:n, :].rearrange("p (r c) -> p r c", c=64), in1=bc4(ptm[:n, l, C_SD + 4 * gi:C_SD + 4 * gi + 4]), op=ALU.mult), [b_xstm, b_par], [b_t2])
                            op("dve", lambda e: e.tensor_tensor(out=t1[:n], in0=t1[:n], in1=t2[:n], op=ALU.add), [b_t1, b_t2], [b_t1])
                            op("dve", lambda e: e.tensor_tensor(out=t1[:n], in0=t1[:n], in1=sz[:n, i, :].rearrange("p (r c) -> p r c", c=64), op=ALU.mult), [b_t1, b_sz], [b_t1])
                            if SST <= 6:
                                continue
                            op("act", lambda e: e.activation(out=t2_[:n, :], in_=t1_[:n, :], func=AF.Square, accum_out=ssm[:n, 0:1]), [b_t1], [b_t2, b_ssm])
                            rsqrt_inplace(ssm[:n, 0:1], b_ssm, 1.0 / 256, RMS_EPS)
                            op("dve", lambda e: e.tensor_scalar(out=t1_[:n, :], in0=t1_[:n, :], scalar1=ssm[:n, 0:1], scalar2=None, op0=ALU.mult), [b_t1, b_ssm], [b_t1])
                            if SST <= 7:
                                continue
                            b6 = bank()
                            t1f = t1_[:n, :]
                            tr(b6, banks[b6][:, 0:n], t1f[:, 0:128], n, [b_t1])
                            tr(b6, banks[b6][:, 128:128 + n], t1f[:, 128:256], n, [b_t1])
                            for c_ in range(2):
                                op("dve", lambda e: e.tensor_scalar(out=y_s[:, 2 * gi + c_, off:off + n], in0=banks[b6][:, c_ * 128:c_ * 128 + n], scalar1=pcol(l, R_SNW + 2 * gi + c_), scalar2=None, op0=ALU.mult), [bbuf[b6], b_par], [b_ys])
                            if SST <= 8:
                                continue
                            hv = Hs[:, l, gi, :].rearrange("p (r c) -> p r c", c=64)
                            op("dve", lambda e: e.tensor_tensor(out=hv, in0=hv, in1=echk[:, i, hd].unsqueeze(2).to_broadcast([128, 4, 64]), op=ALU.mult), [b_Hs[l][gi], b_ss], [b_Hs[l][gi]])
                            op("dve", lambda e: e.tensor_tensor(out=Hs[:, l, gi, :], in0=Hs[:, l, gi, :], in1=banks[b5][:, 0:256], op=ALU.add), [b_Hs[l][gi], bbuf[b5]], [b_Hs[l][gi]])
                            op("act", lambda e: e.activation(out=Hb[:, :], in_=Hs[:, l, gi, :], func=AF.Copy), [b_Hs[l][gi]], [b_Hb])
                    S.barrier()
                with ExitStack() as ph:
                  if "C" in phases:
                    psb = lambda n_, sh, dt=F32: sb(n_, sh, dt, ph)
                    seqbase = NMETA if s == 0 else 0
                    qTb2 = psb("c_qTb", [64, 4, TMAX], BF16); b_qT2 = Buf()
                    eTs = [psb(f"c_eT{i_}", [128, 4, 128], BF16) for i_ in range(3)]; b_eT = [Buf() for _ in range(3)]
                    etmp = psb("c_etmp", [128, 4, 128]); b_etmp = Buf()
                    den = psb("c_den", [128, 4]); b_den = Buf()
                    otm_ = psb("c_otm", [128, 256]); b_otm = Buf()
                    otm = otm_[:, :].rearrange("p (r c) -> p r c", c=64)
                    wkv, bwkv = load_w([win_d[l, :, O_CK:O_CK + 256], win_d[l, :, O_CV:O_CV + 256]], 8, 512)
                    kcol = lambda t_: t_ if (s == 0 and t_ < NMETA) else NMETA + 128 + (t_ - seqbase)
                    for hk in range(4):
                        for (s0, sn) in segs:
                            bi = bank()
                            proj_FM(bi, wkv, bwkv, hk * 64, 64, xnT, b_xnT, s0, sn)
                            d0 = kcol(s0)
                            op("act", lambda e: e.activation(out=kTc[:, l, hk, d0:d0 + sn], in_=banks[bi][:64, 0:sn], func=AF.Copy), [bbuf[bi]], [b_kT[l]])
                    vslot = lambda i_: 0 if (s == 0 and i_ == 0) else 2 + i_ - (1 if s == 0 else 0)
                    for i, (off, n) in enumerate(tiles):
                        bi = bank()
                        for kc in range(8):
                            mm(bi, banks[bi][:n, 0:256], xnT[:, kc, off:off + n], wkv[:, kc, 256:512], kc == 0, kc == 7, [b_xnT, bwkv])
                        op("act", lambda e: e.activation(out=vA[:n, l, vslot(i), :, 0:64], in_=banks[bi][:n, 0:256].rearrange("p (r c) -> p r c", c=64), func=AF.Copy), [bbuf[bi]], [b_vA[l]])
                    for hk in range(4):
                        wq, bwq = load_w([win_d[l, :, O_CQ + hk * 256:O_CQ + (hk + 1) * 256]], 8, 256)
                        for r_ in range(4):
                            for (s0, sn) in segs:
                                bi = bank()
                                proj_FM(bi, wq, bwq, r_ * 64, 64, xnT, b_xnT, s0, sn)
                                op("act", lambda e: e.activation(out=qTb2[:, r_, s0:s0 + sn], in_=banks[bi][:64, 0:sn], func=AF.Copy), [bbuf[bi]], [b_qT2])
                        for i, (off, n) in enumerate(tiles):
                            is_meta = (s == 0 and i == 0)
                            if is_meta:
                                kbs = [(0, 0, NMETA, Ui)]
                            else:
                                k_ = i - (1 if s == 0 else 0)
                                kbs = [(NMETA + 128 + 128 * k_, 2 + k_, 128, Ui)]
                                if k_ > 0:
                                    kbs.append((NMETA + 128 + 128 * (k_ - 1), 2 + k_ - 1, 128, Ls))
                                elif s > 0:
                                    kbs.append((NMETA, 1, 128, Ls))
                                kbs.append((0, 0, NMETA, None))
                            for idx, (kc0, vs_, nk, msk) in enumerate(kbs):
                                bi = bank()
                                pq = banks[bi][:nk, 0:4 * n].rearrange("p (r c) -> p r c", c=n)
                                for r_ in range(4):
                                    mm(bi, pq[:, r_, :], kTc[:, l, hk, kc0:kc0 + nk], qTb2[:, r_, off:off + n], True, True, [b_kT[l], b_qT2])
                                if msk is None:
                                    op("act", lambda e: e.activation(out=eTs[idx][:nk, :, 0:n], in_=pq, func=AF.Exp, scale=0.125), [bbuf[bi]], [b_eT[idx]])
                                else:
                                    op("act", lambda e: e.activation(out=etmp[:nk, :, 0:n], in_=pq, func=AF.Exp, scale=0.125), [bbuf[bi]], [b_etmp])
                                    op("dve", lambda e: e.tensor_tensor(out=eTs[idx][:nk, :, 0:n], in0=etmp[:nk, :, 0:n], in1=msk[:nk, :n].unsqueeze(1).to_broadcast([nk, 4, n]), op=ALU.mult), [b_etmp, b_const], [b_eT[idx]])
                            bo = bank()
                            po = banks[bo][:n, 0:260].rearrange("p (r c) -> p r c", c=65)
                            for r_ in range(4):
                                for idx, (kc0, vs_, nk, msk) in enumerate(kbs):
                                    mm(bo, po[:, r_, :], eTs[idx][:nk, r_, 0:n], vA[:nk, l, vs_, hk, :], idx == 0, idx == len(kbs) - 1, [b_eT[idx], b_vA[l]])
                            op("dve", lambda e: e.tensor_tensor(out=den[:n, :], in0=po[:, :, 64], in1=esink[:n, l, 4 * hk:4 * hk + 4], op=ALU.add), [bbuf[bo], b_par], [b_den])
                            op("dve", lambda e: e.reciprocal(out=den[:n, :], in_=den[:n, :]), [b_den], [b_den])
                            op("dve", lambda e: e.tensor_tensor(out=otm[:n], in0=po[:, :, 0:64], in1=den[:n, :].unsqueeze(2).to_broadcast([n, 4, 64]), op=ALU.mult), [bbuf[bo], b_den], [b_otm])
                            b6 = bank()
                            of = otm_[:n, :]
                            tr(b6, banks[b6][:, 0:n], of[:, 0:128], n, [b_otm])
                            tr(b6, banks[b6][:, 128:128 + n], of[:, 128:256], n, [b_otm])
                            for c_ in range(2):
                                op("act", lambda e: e.activation(out=y_c[:, 2 * hk + c_, off:off + n], in_=banks[b6][:, c_ * 128:c_ * 128 + n], func=AF.Copy), [bbuf[b6]], [b_yc])
                    op("dve", lambda e: e.tensor_copy(out=kTc[:, l, :, NMETA:NMETA + 128], in_=kTc[:, l, :, NMETA + TS:NMETA + TS + 128]), [b_kT[l]], [b_kT[l]])
                    op("dve", lambda e: e.tensor_copy(out=vA[:, l, 1, :, 0:64], in_=vA[:, l, 1 + TPS, :, 0:64]), [b_vA[l]], [b_vA[l]])
                    S.barrier()
                with ExitStack() as ph:
                  if "M" in phases:
                    psb = lambda n_, sh, dt=F32: sb(n_, sh, dt, ph)
                    mrg = psb("m_mrg", [128, 8, TMAX], BF16); b_mrg = Buf()
                    sig = psb("m_sig", [128, 512]); b_sig = Buf()
                    tmpm = psb("m_tmp", [128, 512]); b_tmpm = Buf()
                    acc = psb("m_acc", [128, 512]); b_acc = Buf()
                    ysrc = [(y_g, b_yg, wpg_d), (y_s, b_ys, wps_d), (y_c, b_yc, wpc_d)]
                    for fc in range(8):
                        wg, bwg = load_w([win_d[l, :, O_GATE + br * 1024 + fc * 128:O_GATE + br * 1024 + (fc + 1) * 128] for br in range(3)], 8, 384)
                        wp, bwp = load_w([ysrc[br][2][l, :, fc * 128:(fc + 1) * 128] for br in range(3)], 8, 384)
                        for (s0, sn) in segs:
                            for br in range(3):
                                b1 = bank()
                                proj_FM(b1, wg, bwg, br * 128, 128, xnT, b_xnT, s0, sn)
                                op("act", lambda e: e.activation(out=sig[:, 0:sn], in_=banks[b1][:, 0:sn], func=AF.Sigmoid), [bbuf[b1]], [b_sig])
                                b2 = bank()
                                proj_FM(b2, wp, bwp, br * 128, 128, ysrc[br][0], ysrc[br][1], s0, sn)
                                if br == 0:
                                    op("dve", lambda e: e.tensor_tensor(out=acc[:, 0:sn], in0=sig[:, 0:sn], in1=banks[b2][:, 0:sn], op=ALU.mult), [b_sig, bbuf[b2]], [b_acc])
                                else:
                                    op("dve", lambda e: e.tensor_tensor(out=tmpm[:, 0:sn], in0=sig[:, 0:sn], in1=banks[b2][:, 0:sn], op=ALU.mult), [b_sig, bbuf[b2]], [b_tmpm])
                                    if br == 1:
                                        op("dve", lambda e: e.tensor_tensor(out=acc[:, 0:sn], in0=acc[:, 0:sn], in1=tmpm[:, 0:sn], op=ALU.add), [b_acc, b_tmpm], [b_acc])
                                    else:
                                        op("dve", lambda e: e.tensor_tensor(out=mrg[:, fc, s0:s0 + sn], in0=acc[:, 0:sn], in1=tmpm[:, 0:sn], op=ALU.add), [b_acc, b_tmpm], [b_mrg])
                    if dbg == "mrg" and l == 0 and s == 0:
                        dbg_tap(mrg, b_mrg)
                    for half in range(2):
                        wo, bwo = load_w([wout_d[l, :, half * 512:(half + 1) * 512]], 8, 512)
                        for i, (off, n) in enumerate(tiles):
                            bi = bank()
                            for kc in range(8):
                                mm(bi, banks[bi][:n, 0:512], mrg[:, kc, off:off + n], wo[:, kc, :], kc == 0, kc == 7, [b_mrg, bwo])
                            op("dve", lambda e: e.tensor_tensor(out=h[:n, i, half * 512:(half + 1) * 512], in0=h[:n, i, half * 512:(half + 1) * 512], in1=banks[bi][:n, 0:512], op=ALU.add), [b_h[i], bbuf[bi]], [b_h[i]])
                    S.barrier()
                if dbg and l == 0 and s == 0 and dbg in ("y_g", "y_s", "y_c"):
                    dbg_tap({"y_g": y_g, "y_s": y_s, "y_c": y_c}[dbg], {"y_g": b_yg, "y_s": b_ys, "y_c": b_yc}[dbg])
                norm_to_FM(tiles, l, R_N2)
                with ExitStack() as ph:
                  if "F" in phases:
                    psb = lambda n_, sh, dt=F32: sb(n_, sh, dt, ph)
                    actT = psb("f_act", [128, 4, TMAX], BF16); b_actT = Buf()
                    rl = psb("f_rl", [128, 512]); b_rl = Buf()
                    for dg in range(8):
                        wu, bwu = load_w([wup_d[l, :, dg * 512:(dg + 1) * 512]], 8, 512)
                        wd, bwd = load_w([wdn_d[l, dg * 512:(dg + 1) * 512, :]], 4, 1024)
                        for c_ in range(4):
                            for (s0, sn) in segs:
                                bi = bank()
                                proj_FM(bi, wu, bwu, c_ * 128, 128, xnT, b_xnT, s0, sn)
                                op("act", lambda e: e.activation(out=rl[:, 0:sn], in_=banks[bi][:, 0:sn], func=AF.Relu), [bbuf[bi]], [b_rl])
                                op("dve", lambda e: e.tensor_tensor(out=actT[:, c_, s0:s0 + sn], in0=rl[:, 0:sn], in1=rl[:, 0:sn], op=ALU.mult), [b_rl], [b_actT])
                        for i, (off, n) in enumerate(tiles):
                            for half in range(2):
                                bi = bank()
                                for c_ in range(4):
                                    mm(bi, banks[bi][:n, 0:512], actT[:, c_, off:off + n], wd[:, c_, half * 512:(half + 1) * 512], c_ == 0, c_ == 3, [b_actT, bwd])
                                op("dve", lambda e: e.tensor_tensor(out=h[:n, i, half * 512:(half + 1) * 512], in0=h[:n, i, half * 512:(half + 1) * 512], in1=banks[bi][:n, 0:512], op=ALU.add), [b_h[i], bbuf[bi]], [b_h[i]])
                    S.barrier()

            with ExitStack() as ph:
                ot = sb("f_ot", [128, 2, D], F32, ph)
                b_ot = [Buf(), Buf()]
                k_ = 0
                for i, (off, n) in enumerate(tiles):
                    if s == 0 and i == 0:
                        continue
                    op("act", lambda e: e.activation(out=nscr[:n, :], in_=h[:n, i, :], func=AF.Square, accum_out=nsm[:n, i:i + 1]), [b_h[i]], [b_nscr, b_nsm])
                    rsqrt_inplace(nsm[:n, i:i + 1], b_nsm, 1.0 / D, RMS_EPS)
                    op("dve", lambda e: e.scalar_tensor_tensor(out=ot[:n, k_ % 2, :], in0=h[:n, i, :], scalar=nsm[:n, i:i + 1], in1=fnw[:n, :], op0=ALU.mult, op1=ALU.mult),
                       [b_h[i], b_nsm, b_par], [b_ot[k_ % 2]])
                    r0 = seq0 + off - (NMETA if s == 0 else 0)
                    S.dma([(out_d[r0:r0 + n, :], ot[:n, k_ % 2, :])], reads=[b_ot[k_ % 2]], q="act")
                    k_ += 1
                S.barrier()
        S.final_wait("sp")
        S.final_wait("act")
        print("instructions", S.n_ins, "waits", S.n_wait)
    return nc


def pack_params(inp):
    pfm = np.zeros((2, 256, 128), np.float32)
    ptm = np.zeros((2, PTM_W), np.float32)
    for l in range(2):
        pfm[l, R_N1:R_N1 + 8] = np.asarray(inp["norm1_w"][l]).reshape(8, 128)
        pfm[l, R_N2:R_N2 + 8] = np.asarray(inp["norm2_w"][l]).reshape(8, 128)
        pfm[l, R_GCW:R_GCW + 96] = np.asarray(inp["gdn_conv_w"][l]).reshape(96, 128)
        pfm[l, R_SCW:R_SCW + 64] = np.asarray(inp["ssd_conv_w"][l]).reshape(64, 128)
        pfm[l, R_SCB:R_SCB + 16] = np.asarray(inp["ssd_conv_b"][l]).reshape(16, 128)
        pfm[l, R_GNW] = np.asarray(inp["gdn_norm_w"][l])
        pfm[l, R_SNW:R_SNW + 8] = np.asarray(inp["ssd_norm_w"][l]).reshape(8, 128)
        ptm[l, C_GAL:C_GAL + 8] = np.asarray(inp["gdn_a_log"][l])
        ptm[l, C_GDB:C_GDB + 8] = np.asarray(inp["gdn_dt_bias"][l])
        ptm[l, C_SDB:C_SDB + 16] = np.asarray(inp["ssd_dt_bias"][l])
        ptm[l, C_SAL:C_SAL + 16] = np.asarray(inp["ssd_a_log"][l])
        ptm[l, C_SD:C_SD + 16] = np.asarray(inp["ssd_d"][l])
        ptm[l, C_SNK:C_SNK + 16] = np.asarray(inp["swa_sinks"][l])
    return pfm, ptm


_NC_CACHE = {}


def kernel(**inputs):
    inp = {k: np.asarray(v) for k, v in inputs.items()}
    n = 8
    if "nc" not in _NC_CACHE:
        _NC_CACHE["nc"] = build(NST=8, NL=2, TPS=4)
    nc = _NC_CACHE["nc"]
    pfm, ptm = pack_params(inp)
    f32 = lambda a: np.ascontiguousarray(a, dtype=np.float32)
    shared = dict(meta=f32(inp["meta_tokens"]), w_in=f32(inp["w_in"]), w_pg=f32(inp["w_proj_gdn"]), w_ps=f32(inp["w_proj_ssd"]),
                  w_pc=f32(inp["w_proj_swa"]), w_out=f32(inp["w_out"]), w_up=f32(inp["w_up"]), w_dn=f32(inp["w_down"]),
                  pfm=pfm, ptm=ptm, fnw=f32(inp["final_norm_w"]).reshape(1, -1))
    in_maps = [dict(shared, x=f32(inp["x"][i])) for i in range(n)]
    res = run_bass_kernel_spmd(nc, in_maps, core_ids=list(range(n)))
    return np.stack([np.asarray(r["out"], dtype=np.float32) for r in res.results], axis=0)
```

```python
import numpy as np
from contextlib import ExitStack
import concourse.bass as bass
import concourse.mybir as mybir
from concourse.bass_utils import run_bass_kernel_spmd

F32 = mybir.dt.float32
BF16 = mybir.dt.bfloat16
AF = mybir.ActivationFunctionType
ALU = mybir.AluOpType
AX = mybir.AxisListType

D = 1024
SEQ = 4096
NMETA = 16
DFF = 4096
IN_W = 11808
O_GQ, O_GK, O_GV, O_GG, O_GB, O_GA = 0, 1024, 2048, 3072, 4096, 4104
O_SZ, O_SX, O_SB, O_SC, O_SDT = 4112, 5136, 6160, 6672, 7184
O_CQ, O_CK, O_CV, O_GATE = 7200, 8224, 8480, 8736
RMS_EPS = 1e-6
L2_EPS = 1e-6
R_N1, R_N2, R_GCW, R_SCW, R_SCB, R_GNW, R_SNW = 0, 8, 16, 112, 176, 192, 193
C_GAL, C_GDB, C_SDB, C_SAL, C_SD, C_SNK, PTM_W = 0, 8, 16, 32, 48, 64, 80


class Buf:
    __slots__ = ("name", "lw", "rd")

    def __init__(self, name=""):
        self.name = name
        self.lw = None
        self.rd = {}


class Sched:
    ENG = ("pe", "act", "dve", "pool", "sp")
    EPOCH = 30000

    def __init__(self, nc, stack, n_dma_sems=24):
        self.nc = nc
        self.stack = stack
        self.eng = {"pe": nc.tensor, "act": nc.scalar, "dve": nc.vector,
                    "pool": nc.gpsimd, "sp": nc.sync}
        self.semh = {}
        self.cnt = {}
        self.epoch = {}
        for e in self.ENG:
            self.epoch[e] = 0
            self.cnt[e] = 0
            self.semh[(e, 0)] = stack.enter_context(nc.semaphore(f"s_{e}_0"))
        self.waited = {e: {} for e in self.ENG}
        self.ndma = n_dma_sems
        self.dma_tot = [0] * n_dma_sems
        for j in range(n_dma_sems):
            self.semh[("d", j)] = stack.enter_context(nc.semaphore(f"s_dma_{j}"))
        self.dma_next = 0
        self.n_ins = 0
        self.n_wait = 0

    def _wait(self, e, tok):
        key, val = tok
        if self.waited[e].get(key, 0) >= val:
            return
        if key[0] == e and e == "pe":
            return
        self.eng[e].wait_ge(self.semh[key], val)
        self.waited[e][key] = val
        self.n_wait += 1

    def _deps(self, reads, writes):
        deps = {}

        def add(k, v):
            if deps.get(k, 0) < v:
                deps[k] = v
        for b in reads:
            if b.lw is not None:
                add(*b.lw)
        for b in writes:
            if b.lw is not None:
                add(*b.lw)
            for k, v in b.rd.items():
                add(k, v)
        return deps

    def _mark(self, tok, reads, writes):
        k, v = tok
        for b in reads:
            if b.rd.get(k, 0) < v:
                b.rd[k] = v
        for b in writes:
            b.lw = tok
            b.rd = {}

    def op(self, e, fn, reads=(), writes=()):
        deps = self._deps(reads, writes)
        for k, v in deps.items():
            self._wait(e, (k, v))
        if self.cnt[e] >= self.EPOCH:
            self.epoch[e] += 1
            self.cnt[e] = 0
            self.semh[(e, self.epoch[e])] = self.stack.enter_context(
                self.nc.semaphore(f"s_{e}_{self.epoch[e]}"))
        ins = fn(self.eng[e])
        self.cnt[e] += 1
        key = (e, self.epoch[e])
        ins.then_inc(self.semh[key], 1)
        tok = (key, self.cnt[e])
        self._mark(tok, reads, writes)
        self.n_ins += 1
        return tok

    def dma(self, pairs, reads=(), writes=(), q="sp"):
        j = self.dma_next
        self.dma_next = (self.dma_next + 1) % self.ndma
        key = ("d", j)
        if self.dma_tot[j] > 0:
            self._wait(q, (key, self.dma_tot[j]))
        deps = self._deps(reads, writes)
        for k, v in deps.items():
            self._wait(q, (k, v))
        for (o, i) in pairs:
            self.eng[q].dma_start(out=o, in_=i).then_inc(self.semh[key], 16)
            self.dma_tot[j] += 16
            self.n_ins += 1
        tok = (key, self.dma_tot[j])
        self._mark(tok, reads, writes)
        return tok

    def all_tokens(self):
        toks = []
        for e in self.ENG:
            if self.cnt[e] > 0:
                toks.append(((e, self.epoch[e]), self.cnt[e]))
            elif self.epoch[e] > 0:
                toks.append(((e, self.epoch[e] - 1), self.EPOCH))
        toks += [(("d", j), self.dma_tot[j]) for j in range(self.ndma) if self.dma_tot[j] > 0]
        return toks

    def barrier(self):
        toks = self.all_tokens()
        for e in self.ENG:
            for t in toks:
                self._wait(e, t)

    def final_wait(self, e="sp"):
        for t in self.all_tokens():
            self._wait(e, t)


def build(NST=8, NL=2, TPS=4, dbg=False, phases="GSCMF"):
    TS = TPS * 128
    TMAX = TS + NMETA
    NTILE = TPS + 1
    nc = bass.Bass("TRN2", target_bir_lowering=False)
    dram = lambda n, s, k="ExternalInput": nc.dram_tensor(n, s, F32, kind=k).ap()
    x_d = dram("x", [SEQ, D])
    meta_d = dram("meta", [NMETA, D])
    win_d = dram("w_in", [2, D, IN_W])
    wpg_d = dram("w_pg", [2, D, D])
    wps_d = dram("w_ps", [2, D, D])
    wpc_d = dram("w_pc", [2, D, D])
    wout_d = dram("w_out", [2, D, D])
    wup_d = dram("w_up", [2, D, DFF])
    wdn_d = dram("w_dn", [2, DFF, D])
    pfm_d = dram("pfm", [2, 256, 128])
    ptm_d = dram("ptm", [2, PTM_W])
    fnw_d = dram("fnw", [1, D])
    out_d = dram("out", [NST * TS, D], "ExternalOutput")
    dbg_d = dram("dbg", [128, 8 * 528], "ExternalOutput") if dbg else None

    with ExitStack() as st:
        S = Sched(nc, st)
        sbytes = [0]

        def sb(name, shape, dt=F32, stack=None):
            sbytes[0] += 1
            t = (stack or st).enter_context(nc.sbuf_tensor(f"sb{sbytes[0]}_{name}", shape, dt))
            return t

        def op(e, fn, r=(), w=()):
            w = list(w) + [b for b in r if b.name.startswith("bank") and b not in w]
            return S.op(e, fn, reads=r, writes=w)

        banks = [st.enter_context(nc.psum_tensor(f"bank{i}", [128, 512], F32)) for i in range(8)]
        bbuf = [Buf(f"bank{i}") for i in range(8)]
        reserved = [False] * 8
        bank_rr = [0]

        def bank(reserve=False):
            for _ in range(8):
                i = bank_rr[0]
                bank_rr[0] = (i + 1) % 8
                if not reserved[i]:
                    if reserve:
                        reserved[i] = True
                    return i
            raise RuntimeError("no psum bank")

        def mm(bi, out_ap, lhsT, rhs, start, stop, r):
            op("pe", lambda e: e.matmul(out_ap, lhsT=lhsT, rhs=rhs, start=start, stop=stop), r, [bbuf[bi]])

        def tr(bi, out_ap, in_ap, kparts, r):
            op("pe", lambda e: e.transpose(out=out_ap, in_=in_ap, identity=ident[:kparts, :kparts]),
               list(r) + [b_const], [bbuf[bi]])

        ident = sb("ident", [128, 128])
        Ui = sb("Ui", [128, 128])
        Ls = sb("Ls", [128, 128])
        nUs = sb("nUs", [128, 128])
        ones = sb("ones", [128, 128])
        b_const = Buf("const")
        for t_, val in ((ident, 1.0), (Ui, 1.0), (Ls, 1.0), (nUs, -1.0), (ones, 1.0)):
            op("pool", lambda e, t_=t_, val=val: e.memset(t_[:], val), [], [b_const])
        sel = lambda t_, pat, cm, cmp: op("pool", lambda e: e.affine_select(
            out=t_[:], in_=t_[:], pattern=[[pat, 128]], compare_op=cmp, fill=0.0, base=0, channel_multiplier=cm),
            [b_const], [b_const])
        sel(ident, -1, 1, ALU.is_equal)
        sel(Ui, 1, -1, ALU.is_ge)
        sel(Ls, -1, 1, ALU.is_gt)
        sel(nUs, 1, -1, ALU.is_gt)

        pfmT = sb("pfmT", [128, 2, 256])
        ptm = sb("ptm", [128, 2, PTM_W])
        fnw = sb("fnw", [128, D])
        negA_g = sb("negA_g", [128, 2, 8])
        A_s = sb("A_s", [128, 2, 16])
        esink = sb("esink", [128, 2, 16])
        b_par = Buf("par")
        ptmp = sb("ptmp", [128, 2, 128])
        b_ptmp = Buf("ptmp")
        for l in range(2):
            S.dma([(ptmp[:, 0, :], pfm_d[l, 0:128, :]), (ptmp[:, 1, :], pfm_d[l, 128:256, :])],
                  writes=[b_ptmp], q="act")
            bi = bank()
            for hlf in range(2):
                tr(bi, banks[bi][:, hlf * 128:(hlf + 1) * 128], ptmp[:, hlf, :], 128, [b_ptmp])
            op("dve", lambda e: e.tensor_copy(out=pfmT[:, l, :], in_=banks[bi][:, 0:256]), [bbuf[bi]], [b_par])
            S.dma([(ptm[:, l, :], ptm_d[l:l + 1, :].partition_broadcast(128))], writes=[b_par], q="act")
        S.dma([(fnw[:], fnw_d[0:1, :].partition_broadcast(128))], writes=[b_par], q="act")
        for l in range(2):
            op("act", lambda e: e.activation(out=negA_g[:, l, :], in_=ptm[:, l, C_GAL:C_GAL + 8], func=AF.Exp), [b_par], [b_par])
            op("act", lambda e: e.activation(out=A_s[:, l, :], in_=ptm[:, l, C_SAL:C_SAL + 16], func=AF.Exp), [b_par], [b_par])
            op("act", lambda e: e.activation(out=esink[:, l, :], in_=ptm[:, l, C_SNK:C_SNK + 16], func=AF.Exp), [b_par], [b_par])
            op("dve", lambda e: e.tensor_scalar(out=negA_g[:, l, :], in0=negA_g[:, l, :], scalar1=-1.0, scalar2=None, op0=ALU.mult), [b_par], [b_par])
            op("dve", lambda e: e.tensor_scalar(out=A_s[:, l, :], in0=A_s[:, l, :], scalar1=-1.0, scalar2=None, op0=ALU.mult), [b_par], [b_par])
        pcol = lambda l, row: pfmT[:, l, row:row + 1]

        h = sb("h", [128, NTILE, D])
        b_h = [Buf(f"h{i}") for i in range(NTILE)]
        xnT = sb("xnT", [128, 8, TMAX], BF16)
        b_xnT = Buf("xnT")
        y_g = sb("y_g", [128, 8, TMAX], BF16)
        y_s = sb("y_s", [128, 8, TMAX], BF16)
        y_c = sb("y_c", [128, 8, TMAX], BF16)
        b_yg, b_ys, b_yc = Buf("yg"), Buf("ys"), Buf("yc")
        Sg = sb("Sg", [128, 2, 8, 128])
        b_Sg = [[Buf() for _ in range(8)] for _ in range(2)]
        Hs = sb("Hs", [128, 2, 4, 256])
        b_Hs = [[Buf() for _ in range(4)] for _ in range(2)]
        halo_g = sb("halo_g", [128, 2, 24, 3])
        halo_s = sb("halo_s", [128, 2, 16, 3])
        b_halo = Buf("halo")
        KW = NMETA + 128 + TS
        kTc = sb("kTc", [64, 2, 4, KW], BF16)
        b_kT = [Buf() for _ in range(2)]
        vA = sb("vA", [128, 2, 2 + TPS, 4, 65], BF16)
        b_vA = [Buf() for _ in range(2)]
        for t_ in (Sg, Hs, halo_g, halo_s):
            op("pool", lambda e, t_=t_: e.memset(t_[:], 0.0), [], [b_halo])
        op("pool", lambda e: e.memset(kTc[:], 0.0), [], [b_kT[0], b_kT[1]])
        op("pool", lambda e: e.memset(vA[:], 1.0), [], [b_vA[0], b_vA[1]])
        for l in range(2):
            for hh in range(8):
                b_Sg[l][hh].lw = b_halo.lw
            for g_ in range(4):
                b_Hs[l][g_].lw = b_halo.lw

        NSTG, NWB = 2, 2
        stg = [sb(f"stg{i}", [128, 2048]) for i in range(NSTG)]
        b_stg = [Buf() for _ in range(NSTG)]
        wbf = [sb(f"wbf{i}", [128, 4096], BF16) for i in range(NWB)]
        b_wbf = [Buf() for _ in range(NWB)]
        wrr = [0, 0]

        def load_w(parts, kc, cols):
            wi = wrr[1]
            wrr[1] = (wi + 1) % NWB
            wv = wbf[wi][:, 0:kc * cols].rearrange("p (k c) -> p k c", k=kc)
            nsplit = 2 if kc * cols > 2048 else 1
            assert kc % nsplit == 0 and kc * cols // nsplit <= 2048
            kh = kc // nsplit
            for hf in range(nsplit):
                si = wrr[0]
                wrr[0] = (si + 1) % NSTG
                sv = stg[si][:, 0:kh * cols].rearrange("p (k c) -> p k c", k=kh)
                pairs = []
                c0 = 0
                for d_ap in parts:
                    c = d_ap.shape[1]
                    pairs.append((sv[:, :, c0:c0 + c], d_ap[hf * kh * 128:(hf + 1) * kh * 128, :].rearrange("(k p) c -> p k c", p=128)))
                    c0 += c
                assert c0 == cols
                S.dma(pairs, writes=[b_stg[si]], q="sp")
                op("pool", lambda e: e.tensor_copy(out=wv[:, hf * kh:(hf + 1) * kh, :], in_=sv), [b_stg[si]], [b_wbf[wi]])
            return wv, b_wbf[wi]

        def softplus_inplace(x_ap, t_ap, bx, bt):
            op("act", lambda e: e.activation(out=t_ap, in_=x_ap, func=AF.Abs), [bx], [bt])
            op("act", lambda e: e.activation(out=t_ap, in_=t_ap, func=AF.Exp, scale=-1.0), [bt], [bt])
            op("act", lambda e: e.activation(out=t_ap, in_=t_ap, func=AF.Ln, bias=1.0, scale=1.0), [bt], [bt])
            op("dve", lambda e: e.scalar_tensor_tensor(out=x_ap, in0=x_ap, scalar=0.0, in1=t_ap, op0=ALU.max, op1=ALU.add), [bx, bt], [bx])

        def rsqrt_inplace(x_ap, bx, scale, eps):
            op("act", lambda e: e.activation(out=x_ap, in_=x_ap, func=AF.Sqrt, bias=eps, scale=scale), [bx], [bx])
            op("dve", lambda e: e.reciprocal(out=x_ap, in_=x_ap), [bx], [bx])

        nscr = sb("nscr", [128, D])
        b_nscr = Buf("nscr")
        nsm = sb("nsm", [128, NTILE])
        b_nsm = Buf("nsm")

        def norm_to_FM(tiles, l, row):
            for i, (off, n) in enumerate(tiles):
                op("act", lambda e: e.activation(out=nscr[:n, :], in_=h[:n, i, :], func=AF.Square, accum_out=nsm[:n, i:i + 1]),
                   [b_h[i]], [b_nscr, b_nsm])
                rsqrt_inplace(nsm[:n, i:i + 1], b_nsm, 1.0 / D, RMS_EPS)
                op("dve", lambda e: e.tensor_scalar(out=nscr[:n, :], in0=h[:n, i, :], scalar1=nsm[:n, i:i + 1], scalar2=None, op0=ALU.mult),
                   [b_h[i], b_nsm], [b_nscr])
                for half in range(2):
                    bi = bank()
                    pv = banks[bi][:, :].rearrange("p (c t) -> p c t", c=4)
                    for c in range(4):
                        cc = half * 4 + c
                        tr(bi, pv[:, c, 0:n], nscr[:n, cc * 128:(cc + 1) * 128], n, [b_nscr])
                    op("dve", lambda e: e.tensor_tensor(
                        out=xnT[:, half * 4:half * 4 + 4, off:off + n], in0=pv[:, :, 0:n],
                        in1=pfmT[:, l, row + half * 4:row + half * 4 + 4].unsqueeze(2).to_broadcast([128, 4, n]), op=ALU.mult),
                        [bbuf[bi], b_par], [b_xnT])

        def proj_FM(bi, wv, bw, c0, ncol, src, bsrc, s0, sn, kcs=8):
            for kc in range(kcs):
                mm(bi, banks[bi][:ncol, 0:sn], wv[:, kc, c0:c0 + ncol], src[:, kc, s0:s0 + sn], kc == 0, kc == kcs - 1, [bw, bsrc])

        def dbg_tap(src, bsrc):
            with ExitStack() as ph2:
                dbg_copy = sb("dbgc", [128, 8, TMAX], F32, ph2)
                b_dbg = Buf()
                op("dve", lambda e: e.tensor_copy(out=dbg_copy[:, :, :], in_=src[:, :, :]), [bsrc], [b_dbg])
                S.dma([(dbg_d.rearrange("p (c t) -> p c t", c=8), dbg_copy[:, :, :])], reads=[b_dbg], q="act")
                S.barrier()

        for s in range(NST):
            if s == 0:
                tiles = [(0, NMETA)] + [(NMETA + 128 * i, 128) for i in range(TPS)]
                segs = [(0, NMETA), (NMETA, TS)]
                T = TMAX
            else:
                tiles = [(128 * i, 128) for i in range(TPS)]
                segs = [(0, TS)]
                T = TS
            seq0 = s * TS
            if s == 0:
                cgroups = [([0], NMETA), ([NMETA + 64 * j for j in range(2 * TPS)], 64)]
            else:
                cgroups = [([64 * j for j in range(2 * TPS)], 64)]
            nchunk = sum(len(g[0]) for g in cgroups)
            pairs = []
            wl = []
            for i, (off, n) in enumerate(tiles):
                if s == 0 and i == 0:
                    pairs.append((h[:n, i, :], meta_d[:, :]))
                else:
                    r0 = seq0 + off - (NMETA if s == 0 else 0)
                    pairs.append((h[:n, i, :], x_d[r0:r0 + n, :]))
                wl.append(b_h[i])
            S.dma(pairs, writes=wl, q="act")

            for l in range(NL):
                norm_to_FM(tiles, l, R_N1)
                with ExitStack() as ph:
                  if "G" in phases:
                    psb = lambda n_, sh, dt=F32: sb(n_, sh, dt, ph)
                    NCH = 2 * TPS + 1
                    ba = psb("g_ba", [64, NCH, 16]); b_ba = Buf()
                    tsm = psb("g_tsm", [64, NCH, 8]); b_tsm = Buf()
                    beta = psb("g_beta", [64, NCH, 8]); gsm = psb("g_gsm", [64, NCH, 8])
                    bk = psb("g_bk", [64, NCH, 8]); etail = psb("g_etail", [64, NCH, 8])
                    eglast = psb("g_eglast", [128, NCH, 8]); b_sm = Buf()
                    wv, bw = load_w([win_d[l, :, O_GB:O_GB + 16]], 8, 16)
                    ci = 0
                    cinfo = []
                    for offs, cs in cgroups:
                        bi = bank()
                        for j, off in enumerate(offs):
                            for kc in range(8):
                                mm(bi, banks[bi][:cs, j * 16:(j + 1) * 16], xnT[:, kc, off:off + cs], wv[:, kc, 0:16], kc == 0, kc == 7, [b_xnT, bw])
                            cinfo.append((ci + j, off, cs))
                        nj = len(offs)
                        op("dve", lambda e: e.tensor_copy(out=ba[:cs, ci:ci + nj, :], in_=banks[bi][:cs, 0:nj * 16].rearrange("p (j c) -> p j c", c=16)), [bbuf[bi]], [b_ba])
                        ci += nj
                    assert ci == nchunk
                    NC_ = nchunk
                    op("act", lambda e: e.activation(out=beta[:, 0:NC_, :], in_=ba[:, 0:NC_, 0:8], func=AF.Sigmoid), [b_ba], [b_sm])
                    op("dve", lambda e: e.tensor_tensor(out=gsm[:, 0:NC_, :], in0=ba[:, 0:NC_, 8:16], in1=ptm[:64, l, C_GDB:C_GDB + 8].unsqueeze(1).to_broadcast([64, NC_, 8]), op=ALU.add), [b_ba, b_par], [b_sm])
                    softplus_inplace(gsm[:, 0:NC_, :].rearrange("p j c -> p (j c)"), tsm[:, 0:NC_, :].rearrange("p j c -> p (j c)"), b_sm, b_tsm)
                    op("dve", lambda e: e.tensor_tensor(out=gsm[:, 0:NC_, :], in0=gsm[:, 0:NC_, :], in1=negA_g[:64, l, :].unsqueeze(1).to_broadcast([64, NC_, 8]), op=ALU.mult), [b_sm, b_par], [b_sm])
                    ci = 0
                    for offs, cs in cgroups:
                        nj = len(offs)
                        rhs = gsm[:cs, ci:ci + nj, :].rearrange("p j c -> p (j c)")
                        bi = bank()
                        mm(bi, banks[bi][:cs, 0:nj * 8], Ui[:cs, :cs], rhs, True, True, [b_sm, b_const])
                        mm(bi, banks[bi][:, 128:128 + nj * 8], ones[:cs, :], rhs, True, True, [b_sm, b_const])
                        gam_v = banks[bi][:cs, 0:nj * 8].rearrange("p (j c) -> p j c", c=8)
                        gl_v = banks[bi][:, 128:128 + nj * 8].rearrange("p (j c) -> p j c", c=8)
                        op("act", lambda e: e.activation(out=bk[:cs, ci:ci + nj, :], in_=gam_v, func=AF.Exp), [bbuf[bi]], [b_sm])
                        op("dve", lambda e: e.tensor_tensor(out=bk[:cs, ci:ci + nj, :], in0=bk[:cs, ci:ci + nj, :], in1=beta[:cs, ci:ci + nj, :], op=ALU.mult), [b_sm], [b_sm])
                        op("act", lambda e: e.activation(out=eglast[:, ci:ci + nj, :], in_=gl_v, func=AF.Exp), [bbuf[bi]], [b_sm])
                        op("act", lambda e: e.activation(out=tsm[:cs, ci:ci + nj, :], in_=gl_v[:cs], func=AF.Copy), [bbuf[bi]], [b_tsm])
                        op("dve", lambda e: e.tensor_tensor(out=etail[:cs, ci:ci + nj, :], in0=tsm[:cs, ci:ci + nj, :], in1=gam_v, op=ALU.subtract), [b_tsm, bbuf[bi]], [b_sm])
                        op("act", lambda e: e.activation(out=etail[:cs, ci:ci + nj, :], in_=etail[:cs, ci:ci + nj, :], func=AF.Exp), [b_sm], [b_sm])
                        ci += nj
                    GST = 99
                    xq = psb("g_xq", [128, 3, TMAX + 3]); b_xq = Buf()
                    cq = psb("g_cq", [128, 3, TMAX]); b_cq = Buf()
                    sgt = psb("g_sgt", [128, TMAX]); b_sgt = Buf()
                    sq = psb("g_sq", [128, TMAX]); b_sq = Buf()
                    rin = psb("g_rin", [128, TMAX]); b_rin = Buf()
                    egb = psb("g_egb", [128, TMAX]); b_egb = Buf()
                    kTb = psb("g_kTb", [128, TMAX], BF16); qTb = psb("g_qTb", [128, TMAX], BF16); qdb = psb("g_qdb", [128, TMAX], BF16); b_qk = Buf()
                    m64 = [psb(f"g_m{i}", [64, 2 * TPS, 64]) for i in range(9)]
                    b_m = [Buf() for _ in range(9)]
                    E_, DT_, MB_, Bm, BT_, P_, PT_, M_, M2_ = m64
                    bE, bDT, bMB, bBm, bBT, bP, bPT, bM, bM2 = b_m
                    Rk = psb("g_Rk", [64, 2 * TPS, 128], BF16); Rv = psb("g_Rv", [64, 2 * TPS, 128], BF16); ktl = psb("g_ktl", [64, 2 * TPS, 128], BF16); b_R = Buf()
                    TTb = psb("g_TTb", [64, 2 * TPS, 64], BF16); aTb = psb("g_aTb", [64, 2 * TPS, 64], BF16); b_TT = Buf(); b_aT = Buf()
                    nWT = psb("g_nWT", [128, 2 * TPS, 64], BF16); b_nWT = Buf()
                    oT = psb("g_oT", [128, TMAX]); b_oT = Buf()
                    vnb = psb("g_vnb", [64, 128], BF16); b_vnb = Buf()
                    Sb = psb("g_Sb", [128, 128], BF16); b_Sb = Buf()
                    for hh in range(8 if GST > 0 else 0):
                        wv, bw = load_w([win_d[l, :, O_GQ + hh * 128:O_GQ + (hh + 1) * 128], win_d[l, :, O_GK + hh * 128:O_GK + (hh + 1) * 128],
                                         win_d[l, :, O_GV + hh * 128:O_GV + (hh + 1) * 128], win_d[l, :, O_GG + hh * 128:O_GG + (hh + 1) * 128]], 8, 512)
                        for qi in range(3):
                            op("dve", lambda e: e.tensor_copy(out=xq[:, qi, 0:3], in_=halo_g[:, l, qi * 8 + hh, :]), [b_halo], [b_xq])
                        for qi in range(4):
                            for (s0, sn) in segs:
                                bi = bank()
                                proj_FM(bi, wv, bw, qi * 128, 128, xnT, b_xnT, s0, sn)
                                if qi < 3:
                                    op("act", lambda e: e.activation(out=xq[:, qi, 3 + s0:3 + s0 + sn], in_=banks[bi][:, 0:sn], func=AF.Copy), [bbuf[bi]], [b_xq])
                                else:
                                    op("act", lambda e: e.activation(out=sgt[:, s0:s0 + sn], in_=banks[bi][:, 0:sn], func=AF.Silu), [bbuf[bi]], [b_sgt])
                        for qi in range(3):
                            chn = qi * 8 + hh
                            cw = lambda tap: pcol(l, R_GCW + tap * 24 + chn)
                            op("dve", lambda e: e.tensor_scalar(out=cq[:, qi, 0:T], in0=xq[:, qi, 3:3 + T], scalar1=cw(3), scalar2=None, op0=ALU.mult), [b_xq, b_par], [b_cq])
                            for tap in range(3):
                                op("dve", lambda e: e.scalar_tensor_tensor(out=cq[:, qi, 0:T], in0=xq[:, qi, tap:tap + T], scalar=cw(tap), in1=cq[:, qi, 0:T], op0=ALU.mult, op1=ALU.add), [b_xq, b_par, b_cq], [b_cq])
                            op("dve", lambda e: e.tensor_copy(out=halo_g[:, l, chn, :], in_=xq[:, qi, T:T + 3]), [b_xq], [b_halo])
                            op("act", lambda e: e.activation(out=cq[:, qi, 0:T], in_=cq[:, qi, 0:T], func=AF.Silu), [b_cq], [b_cq])
                        for qi in range(2):
                            op("pool", lambda e: e.tensor_tensor(out=sq[:, 0:T], in0=cq[:, qi, 0:T], in1=cq[:, qi, 0:T], op=ALU.mult), [b_cq], [b_sq])
                            for (s0, sn) in segs:
                                bi = bank()
                                mm(bi, banks[bi][:, 0:sn], ones[:, :], sq[:, s0:s0 + sn], True, True, [b_sq, b_const])
                                op("act", lambda e: e.activation(out=rin[:, s0:s0 + sn], in_=banks[bi][:, 0:sn], func=AF.Sqrt, bias=L2_EPS, scale=1.0), [bbuf[bi]], [b_rin])
                            op("dve", lambda e: e.reciprocal(out=rin[:, 0:T], in_=rin[:, 0:T]), [b_rin], [b_rin])
                            if qi == 0:
                                op("dve", lambda e: e.scalar_tensor_tensor(out=cq[:, 0, 0:T], in0=cq[:, 0, 0:T], scalar=128.0 ** -0.5, in1=rin[:, 0:T], op0=ALU.mult, op1=ALU.mult), [b_cq, b_rin], [b_cq])
                            else:
                                op("dve", lambda e: e.tensor_tensor(out=cq[:, 1, 0:T], in0=cq[:, 1, 0:T], in1=rin[:, 0:T], op=ALU.mult), [b_cq, b_rin], [b_cq])
                        op("pool", lambda e: e.tensor_copy(out=kTb[:, 0:T], in_=cq[:, 1, 0:T]), [b_cq], [b_qk])
                        op("pool", lambda e: e.tensor_copy(out=qTb[:, 0:T], in_=cq[:, 0, 0:T]), [b_cq], [b_qk])
                        ci = 0
                        for offs, cs in (cgroups if GST > 1 else []):
                            nj = len(offs)
                            gs0 = offs[0]
                            gl_ = nj * cs
                            v3 = lambda t_: t_[:cs, 0:nj, 0:cs]
                            Uib = Ui[:cs, :cs].unsqueeze(1).to_broadcast([cs, nj, cs])
                            op("dve", lambda e: e.tensor_tensor(out=v3(DT_), in0=gsm[:cs, ci:ci + nj, hh].unsqueeze(2).to_broadcast([cs, nj, cs]), in1=Uib, op=ALU.mult), [b_sm, b_const], [bDT])
                            b1 = bank(); b2 = bank(); b3 = bank()
                            p3 = lambda b_: banks[b_][:cs, 0:nj * cs].rearrange("p (j c) -> p j c", c=cs)
                            for j in range(nj):
                                mm(b1, p3(b1)[:, j, :], Ls[:cs, :cs], DT_[:cs, j, 0:cs], True, True, [bDT, b_const])
                            op("act", lambda e: e.activation(out=v3(E_), in_=p3(b1), func=AF.Exp), [bbuf[b1]], [bE])
                            op("dve", lambda e: e.tensor_tensor(out=v3(MB_), in0=beta[:cs, ci:ci + nj, hh].unsqueeze(2).to_broadcast([cs, nj, cs]), in1=ident[:cs, :cs].unsqueeze(1).to_broadcast([cs, nj, cs]), op=ALU.mult), [b_sm, b_const], [bMB])
                            for j in range(nj):
                                mm(b2, p3(b2)[:, j, :], ones[:cs, :cs], MB_[:cs, j, 0:cs], True, True, [bMB, b_const])
                            op("dve", lambda e: e.tensor_tensor(out=v3(MB_), in0=v3(E_), in1=p3(b2), op=ALU.mult), [bE, bbuf[b2]], [bMB])
                            op("pool", lambda e: e.tensor_tensor(out=v3(MB_), in0=v3(MB_), in1=nUs[:cs, :cs].unsqueeze(1).to_broadcast([cs, nj, cs]), op=ALU.mult), [bMB, b_const], [bMB])
                            op("pool", lambda e: e.tensor_tensor(out=v3(DT_), in0=v3(E_), in1=Uib, op=ALU.mult), [bE, b_const], [bDT])
                            for j in range(nj):
                                mm(b3, banks[b3][:, j * cs:(j + 1) * cs], gsm[:cs, ci + j, hh:hh + 1].to_broadcast([cs, 128]), Ui[:cs, :cs], True, True, [b_sm, b_const])
                            op("act", lambda e: e.activation(out=egb[:, gs0:gs0 + gl_], in_=banks[b3][:, 0:gl_], func=AF.Exp), [bbuf[b3]], [b_egb])
                            op("dve", lambda e: e.tensor_tensor(out=qdb[:, gs0:gs0 + gl_], in0=cq[:, 0, gs0:gs0 + gl_], in1=egb[:, gs0:gs0 + gl_], op=ALU.mult), [b_cq, b_egb], [b_qk])
                            if GST <= 2:
                                ci += nj
                                continue
                            for j0 in range(0, nj, 4):
                                jn = min(4, nj - j0)
                                bi = bank()
                                pk = banks[bi][:cs, 0:jn * 128].rearrange("p (j c) -> p j c", c=128)
                                for j in range(jn):
                                    tr(bi, pk[:, j, :], cq[:, 1, offs[j0 + j]:offs[j0 + j] + cs], 128, [b_cq])
                                bcs = lambda t_: t_[:cs, ci + j0:ci + j0 + jn, hh].unsqueeze(2).to_broadcast([cs, jn, 128])
                                op("dve", lambda e: e.tensor_tensor(out=Rk[:cs, j0:j0 + jn, :], in0=pk, in1=bcs(bk), op=ALU.mult), [bbuf[bi], b_sm], [b_R])
                                op("dve", lambda e: e.tensor_tensor(out=ktl[:cs, j0:j0 + jn, :], in0=pk, in1=bcs(etail), op=ALU.mult), [bbuf[bi], b_sm], [b_R])
                                bi = bank()
                                pk2 = banks[bi][:cs, 0:jn * 128].rearrange("p (j c) -> p j c", c=128)
                                for j in range(jn):
                                    tr(bi, pk2[:, j, :], cq[:, 2, offs[j0 + j]:offs[j0 + j] + cs], 128, [b_cq])
                                op("dve", lambda e: e.tensor_tensor(out=Rv[:cs, j0:j0 + jn, :], in0=pk2, in1=bcs(beta), op=ALU.mult), [bbuf[bi], b_sm], [b_R])
                            if GST <= 3:
                                ci += nj
                                continue
                            b1 = bank()
                            for j in range(nj):
                                o_ = offs[j]
                                mm(b1, p3(b1)[:, j, :], kTb[:, o_:o_ + cs], kTb[:, o_:o_ + cs], True, True, [b_qk])
                            op("dve", lambda e: e.tensor_tensor(out=v3(Bm), in0=v3(MB_), in1=p3(b1), op=ALU.mult), [bMB, bbuf[b1]], [bBm])
                            b1 = bank()
                            for j in range(nj):
                                tr(b1, p3(b1)[:, j, :], Bm[:cs, j, 0:cs], cs, [bBm])
                            op("act", lambda e: e.activation(out=v3(BT_), in_=p3(b1), func=AF.Copy), [bbuf[b1]], [bBT])
                            op("pool", lambda e: e.tensor_tensor(out=v3(M_), in0=v3(Bm), in1=ident[:cs, :cs].unsqueeze(1).to_broadcast([cs, nj, cs]), op=ALU.add), [bBm, b_const], [bM])
                            if GST <= 4:
                                ci += nj
                                continue
                            nlev = 5 if cs == 64 else 3
                            Pc, PTc, bPc, bPTc = Bm, BT_, bBm, bBT
                            Pn, PTn, bPn, bPTn = P_, PT_, bP, bPT
                            Mc, Mn, bMc, bMn = M_, M2_, bM, bM2
                            for lev in range(nlev):
                                last = lev == nlev - 1
                                b2 = bank()
                                for j in range(nj):
                                    mm(b2, p3(b2)[:, j, :], Pc[:cs, j, 0:cs], PTc[:cs, j, 0:cs], True, True, [bPc, bPTc])
                                if not last:
                                    b1 = bank()
                                    for j in range(nj):
                                        mm(b1, p3(b1)[:, j, :], PTc[:cs, j, 0:cs], Pc[:cs, j, 0:cs], True, True, [bPc, bPTc])
                                op("act", lambda e: e.activation(out=v3(PTn), in_=p3(b2), func=AF.Copy), [bbuf[b2]], [bPTn])
                                if not last:
                                    op("dve", lambda e: e.tensor_copy(out=v3(Pn), in_=p3(b1)), [bbuf[b1]], [bPn])
                                b3 = bank()
                                for j in range(nj):
                                    mm(b3, p3(b3)[:, j, :], PTn[:cs, j, 0:cs], Mc[:cs, j, 0:cs], True, True, [bPTn, bMc])
                                op("dve", lambda e: e.tensor_tensor(out=v3(Mn), in0=v3(Mc), in1=p3(b3), op=ALU.add), [bMc, bbuf[b3]], [bMn])
                                Pc, Pn, bPc, bPn = Pn, Pc, bPn, bPc
                                PTc, PTn, bPTc, bPTn = PTn, PTc, bPTn, bPTc
                                Mc, Mn, bMc, bMn = Mn, Mc, bMn, bMc
                            op("pool", lambda e: e.tensor_copy(out=v3(TTb), in_=v3(Mc)), [bMc], [b_TT])
                            if GST <= 5:
                                ci += nj
                                continue
                            b1 = bank()
                            for j in range(nj):
                                o_ = offs[j]
                                mm(b1, p3(b1)[:, j, :], kTb[:, o_:o_ + cs], qTb[:, o_:o_ + cs], True, True, [b_qk])
                            op("dve", lambda e: e.tensor_tensor(out=v3(aTb), in0=v3(DT_), in1=p3(b1), op=ALU.mult), [bDT, bbuf[b1]], [b_aT])
                            b1 = bank()
                            for j in range(nj):
                                mm(b1, banks[b1][:, j * cs:(j + 1) * cs], Rk[:cs, j, :], TTb[:cs, j, 0:cs], True, True, [b_R, b_TT])
                            op("act", lambda e: e.activation(out=nWT[:, 0:nj, 0:cs], in_=banks[b1][:, 0:nj * cs].rearrange("p (j c) -> p j c", c=cs), func=AF.Copy, scale=-1.0), [bbuf[b1]], [b_nWT])
                            if GST <= 6:
                                ci += nj
                                continue
                            bo = bank(reserve=True)
                            for j in range(nj):
                                o_ = offs[j]
                                op("act", lambda e: e.activation(out=Sb[:, :], in_=Sg[:, l, hh, :], func=AF.Copy), [b_Sg[l][hh]], [b_Sb])
                                b1 = bank()
                                mm(b1, banks[b1][:cs, 0:128], TTb[:cs, j, 0:cs], Rv[:cs, j, :], True, False, [b_TT, b_R])
                                mm(b1, banks[b1][:cs, 0:128], nWT[:, j, 0:cs], Sb[:, :], False, True, [b_nWT, b_Sb])
                                op("act", lambda e: e.activation(out=vnb[:cs, :], in_=banks[b1][:cs, 0:128], func=AF.Copy), [bbuf[b1]], [b_vnb])
                                if GST <= 7:
                                    continue
                                mm(bo, banks[bo][:, j * cs:(j + 1) * cs], Sb[:, :], qdb[:, o_:o_ + cs], True, False, [b_Sb, b_qk])
                                mm(bo, banks[bo][:, j * cs:(j + 1) * cs], vnb[:cs, :], aTb[:cs, j, 0:cs], False, True, [b_vnb, b_aT])
                                if GST <= 8:
                                    continue
                                b2 = bank()
                                mm(b2, banks[b2][:, 0:128], ktl[:cs, j, :], vnb[:cs, :], True, True, [b_R, b_vnb])
                                op("dve", lambda e: e.scalar_tensor_tensor(out=Sg[:, l, hh, :], in0=Sg[:, l, hh, :], scalar=eglast[:, ci + j, hh:hh + 1], in1=banks[b2][:, 0:128], op0=ALU.mult, op1=ALU.add),
                                   [b_Sg[l][hh], b_sm, bbuf[b2]], [b_Sg[l][hh]])
                            if GST > 9:
                                op("dve", lambda e: e.tensor_copy(out=oT[:, gs0:gs0 + gl_], in_=banks[bo][:, 0:gl_]), [bbuf[bo]], [b_oT])
                            reserved[bo] = False
                            ci += nj
                        op("pool", lambda e: e.tensor_tensor(out=sq[:, 0:T], in0=oT[:, 0:T], in1=oT[:, 0:T], op=ALU.mult), [b_oT], [b_sq])
                        for (s0, sn) in segs:
                            bi = bank()
                            mm(bi, banks[bi][:, 0:sn], ones[:, :], sq[:, s0:s0 + sn], True, True, [b_sq, b_const])
                            op("act", lambda e: e.activation(out=rin[:, s0:s0 + sn], in_=banks[bi][:, 0:sn], func=AF.Sqrt, bias=RMS_EPS, scale=1.0 / 128), [bbuf[bi]], [b_rin])
                        op("dve", lambda e: e.reciprocal(out=rin[:, 0:T], in_=rin[:, 0:T]), [b_rin], [b_rin])
                        op("dve", lambda e: e.tensor_tensor(out=oT[:, 0:T], in0=oT[:, 0:T], in1=rin[:, 0:T], op=ALU.mult), [b_oT, b_rin], [b_oT])
                        op("dve", lambda e: e.scalar_tensor_tensor(out=y_g[:, hh, 0:T], in0=oT[:, 0:T], scalar=pcol(l, R_GNW), in1=sgt[:, 0:T], op0=ALU.mult, op1=ALU.mult), [b_oT, b_par, b_sgt], [b_yg])
                    S.barrier()
                with ExitStack() as ph:
                  if "S" in phases:
                    psb = lambda n_, sh, dt=F32: sb(n_, sh, dt, ph)
                    NT_ = len(tiles)
                    dtp = psb("s_dtp", [128, NTILE, 16]); adt = psb("s_adt", [128, NTILE, 16]); tsm2 = psb("s_tsm", [128, NTILE, 16])
                    eacum = psb("s_eacum", [128, NTILE, 16]); edec = psb("s_edec", [128, NTILE, 16]); echk = psb("s_echk", [128, NTILE, 16])
                    dte = psb("s_dte", [128, NTILE, 16])
                    b_ss = Buf(); b_st = Buf()
                    wv, bw = load_w([win_d[l, :, O_SDT:O_SDT + 16]], 8, 16)
                    bi = bank()
                    for i, (off, n) in enumerate(tiles):
                        for kc in range(8):
                            mm(bi, banks[bi][:n, i * 16:(i + 1) * 16], xnT[:, kc, off:off + n], wv[:, kc, 0:16], kc == 0, kc == 7, [b_xnT, bw])
                    op("dve", lambda e: e.tensor_tensor(out=dtp[:, 0:NT_, :], in0=banks[bi][:, 0:NT_ * 16].rearrange("p (j c) -> p j c", c=16),
                                                        in1=ptm[:, l, C_SDB:C_SDB + 16].unsqueeze(1).to_broadcast([128, NT_, 16]), op=ALU.add), [bbuf[bi], b_par], [b_ss])
                    softplus_inplace(dtp[:, 0:NT_, :].rearrange("p j c -> p (j c)"), tsm2[:, 0:NT_, :].rearrange("p j c -> p (j c)"), b_ss, b_st)
                    op("dve", lambda e: e.tensor_tensor(out=adt[:, 0:NT_, :], in0=dtp[:, 0:NT_, :], in1=A_s[:, l, :].unsqueeze(1).to_broadcast([128, NT_, 16]), op=ALU.mult), [b_ss, b_par], [b_ss])
                    for i, (off, n) in enumerate(tiles):
                        bi = bank()
                        mm(bi, banks[bi][:n, 0:16], Ui[:n, :n], adt[:n, i, :], True, True, [b_ss, b_const])
                        mm(bi, banks[bi][:, 16:32], ones[:n, :], adt[:n, i, :], True, True, [b_ss, b_const])
                        op("act", lambda e: e.activation(out=eacum[:n, i, :], in_=banks[bi][:n, 0:16], func=AF.Exp), [bbuf[bi]], [b_ss])
                        op("act", lambda e: e.activation(out=echk[:, i, :], in_=banks[bi][:, 16:32], func=AF.Exp), [bbuf[bi]], [b_ss])
                        op("act", lambda e: e.activation(out=tsm2[:n, i, :], in_=banks[bi][:n, 16:32], func=AF.Copy), [bbuf[bi]], [b_st])
                        op("dve", lambda e: e.tensor_tensor(out=edec[:n, i, :], in0=tsm2[:n, i, :], in1=banks[bi][:n, 0:16], op=ALU.subtract), [b_st, bbuf[bi]], [b_ss])
                        op("act", lambda e: e.activation(out=edec[:n, i, :], in_=edec[:n, i, :], func=AF.Exp), [b_ss], [b_ss])
                        op("dve", lambda e: e.tensor_tensor(out=dte[:n, i, :], in0=dtp[:n, i, :], in1=edec[:n, i, :], op=ALU.mult), [b_ss], [b_ss])
                    SST = 99
                    xs4 = psb("s_xs4", [128, 4, TMAX + 3]); b_xs4 = Buf()
                    cs4 = psb("s_cs4", [128, 4, TMAX]); b_cs4 = Buf()
                    BTb = psb("s_BTb", [128, TMAX], BF16); CTb = psb("s_CTb", [128, TMAX], BF16); b_BC = Buf()
                    sz = psb("s_sz", [128, NTILE, 256]); b_sz = Buf()
                    xs_tm = psb("s_xstm", [128, 256]); b_xstm = Buf()
                    xdt = psb("s_xdt", [128, 4, 64], BF16); xdt2 = psb("s_xdt2", [128, 4, 64], BF16); b_xdt = Buf()
                    Btm = psb("s_Btm", [128, 128], BF16); b_Btm = Buf()
                    La = psb("s_La", [128, 4, 128]); b_La = Buf()
                    E4 = psb("s_E4", [128, 4, 128]); b_E4 = Buf()
                    MT = psb("s_MT", [128, 4, 128], BF16); b_MT = Buf()
                    t1_ = psb("s_t1", [128, 256]); t2_ = psb("s_t2", [128, 256]); b_t1 = Buf(); b_t2 = Buf()
                    t1 = t1_[:, :].rearrange("p (r c) -> p r c", c=64); t2 = t2_[:, :].rearrange("p (r c) -> p r c", c=64)
                    ssm = psb("s_ssm", [128, 1]); b_ssm = Buf()
                    Hb = psb("s_Hb", [128, 256], BF16); b_Hb = Buf()
                    for gi in range(4 if SST > 0 else 0):
                        wa, bwa = load_w([win_d[l, :, O_SX + gi * 256:O_SX + (gi + 1) * 256], win_d[l, :, O_SB + gi * 128:O_SB + (gi + 1) * 128],
                                          win_d[l, :, O_SC + gi * 128:O_SC + (gi + 1) * 128]], 8, 512)
                        chns = [2 * gi, 2 * gi + 1, 8 + gi, 12 + gi]
                        for qi in range(4):
                            op("dve", lambda e: e.tensor_copy(out=xs4[:, qi, 0:3], in_=halo_s[:, l, chns[qi], :]), [b_halo], [b_xs4])
                            for (s0, sn) in segs:
                                bi = bank()
                                proj_FM(bi, wa, bwa, qi * 128, 128, xnT, b_xnT, s0, sn)
                                op("act", lambda e: e.activation(out=xs4[:, qi, 3 + s0:3 + s0 + sn], in_=banks[bi][:, 0:sn], func=AF.Copy), [bbuf[bi]], [b_xs4])
                        for qi in range(4):
                            chn = chns[qi]
                            cw = lambda tap: pcol(l, R_SCW + tap * 16 + chn)
                            op("dve", lambda e: e.tensor_scalar(out=cs4[:, qi, 0:T], in0=xs4[:, qi, 3:3 + T], scalar1=cw(3), scalar2=pcol(l, R_SCB + chn), op0=ALU.mult, op1=ALU.add), [b_xs4, b_par], [b_cs4])
                            for tap in range(3):
                                op("dve", lambda e: e.scalar_tensor_tensor(out=cs4[:, qi, 0:T], in0=xs4[:, qi, tap:tap + T], scalar=cw(tap), in1=cs4[:, qi, 0:T], op0=ALU.mult, op1=ALU.add), [b_xs4, b_par, b_cs4], [b_cs4])
                            op("dve", lambda e: e.tensor_copy(out=halo_s[:, l, chn, :], in_=xs4[:, qi, T:T + 3]), [b_xs4], [b_halo])
                            op("act", lambda e: e.activation(out=cs4[:, qi, 0:T], in_=cs4[:, qi, 0:T], func=AF.Silu), [b_cs4], [b_cs4])
                        op("dve", lambda e: e.tensor_copy(out=BTb[:, 0:T], in_=cs4[:, 2, 0:T]), [b_cs4], [b_BC])
                        op("dve", lambda e: e.tensor_copy(out=CTb[:, 0:T], in_=cs4[:, 3, 0:T]), [b_cs4], [b_BC])
                        wz, bwz = load_w([win_d[l, :, O_SZ + gi * 256:O_SZ + (gi + 1) * 256]], 8, 256)
                        for i, (off, n) in enumerate(tiles):
                            bi = bank()
                            for kc in range(8):
                                mm(bi, banks[bi][:n, 0:256], xnT[:, kc, off:off + n], wz[:, kc, 0:256], kc == 0, kc == 7, [b_xnT, bwz])
                            op("act", lambda e: e.activation(out=sz[:n, i, :], in_=banks[bi][:n, 0:256], func=AF.Silu), [bbuf[bi]], [b_sz])
                        op("act", lambda e: e.activation(out=Hb[:, :], in_=Hs[:, l, gi, :], func=AF.Copy), [b_Hs[l][gi]], [b_Hb])
                        for i, (off, n) in enumerate(tiles if SST > 1 else []):
                            hd = slice(4 * gi, 4 * gi + 4)
                            bc4 = lambda ap_: ap_.unsqueeze(2).to_broadcast([n, 4, 64])
                            bi = bank()
                            tr(bi, banks[bi][:n, 0:128], cs4[:, 0, off:off + n], 128, [b_cs4])
                            tr(bi, banks[bi][:n, 128:256], cs4[:, 1, off:off + n], 128, [b_cs4])
                            px = banks[bi][:n, 0:256].rearrange("p (r c) -> p r c", c=64)
                            op("act", lambda e: e.activation(out=xs_tm[:n, :], in_=banks[bi][:n, 0:256], func=AF.Copy), [bbuf[bi]], [b_xstm])
                            op("dve", lambda e: e.tensor_tensor(out=xdt[:n], in0=px, in1=bc4(dtp[:n, i, hd]), op=ALU.mult), [bbuf[bi], b_ss], [b_xdt])
                            op("dve", lambda e: e.tensor_tensor(out=xdt2[:n], in0=px, in1=bc4(dte[:n, i, hd]), op=ALU.mult), [bbuf[bi], b_ss], [b_xdt])
                            if SST <= 2:
                                continue
                            bi = bank()
                            tr(bi, banks[bi][:n, 0:128], cs4[:, 2, off:off + n], 128, [b_cs4])
                            op("act", lambda e: e.activation(out=Btm[:n, :], in_=banks[bi][:n, 0:128], func=AF.Copy), [bbuf[bi]], [b_Btm])
                            if SST <= 3:
                                continue
                            b1 = bank()
                            mm(b1, banks[b1][:n, 0:n], BTb[:, off:off + n], CTb[:, off:off + n], True, True, [b_BC])
                            op("dve", lambda e: e.tensor_tensor(out=La[:n, :, 0:n], in0=adt[:n, i, hd].unsqueeze(2).to_broadcast([n, 4, n]), in1=Ui[:n, :n].unsqueeze(1).to_broadcast([n, 4, n]), op=ALU.mult), [b_ss, b_const], [b_La])
                            b2 = bank()
                            p2 = banks[b2][:n, 0:4 * n].rearrange("p (r c) -> p r c", c=n)
                            for r_ in range(4):
                                mm(b2, p2[:, r_, :], Ls[:n, :n], La[:n, r_, 0:n], True, True, [b_La, b_const])
                            op("act", lambda e: e.activation(out=E4[:n, :, 0:n], in_=p2, func=AF.Exp), [bbuf[b2]], [b_E4])
                            op("dve", lambda e: e.tensor_tensor(out=E4[:n, :, 0:n], in0=E4[:n, :, 0:n], in1=Ui[:n, :n].unsqueeze(1).to_broadcast([n, 4, n]), op=ALU.mult), [b_E4, b_const], [b_E4])
                            op("dve", lambda e: e.tensor_tensor(out=MT[:n, :, 0:n], in0=E4[:n, :, 0:n], in1=banks[b1][:n, 0:n].unsqueeze(1).to_broadcast([n, 4, n]), op=ALU.mult), [b_E4, bbuf[b1]], [b_MT])
                            if SST <= 4:
                                continue
                            b3 = bank()
                            for r_ in range(4):
                                mm(b3, banks[b3][:n, r_ * 64:(r_ + 1) * 64], MT[:n, r_, 0:n], xdt[:n, r_, :], True, True, [b_MT, b_xdt])
                            b4 = bank()
                            mm(b4, banks[b4][:n, 0:256], CTb[:, off:off + n], Hb[:, :], True, True, [b_BC, b_Hb])
                            b5 = bank()
                            mm(b5, banks[b5][:, 0:256], Btm[:n, :], xdt2[:n].rearrange("p r c -> p (r c)"), True, True, [b_Btm, b_xdt])
                            if SST <= 5:
                                continue
                            v4 = lambda b_: banks[b_][:n, 0:256].rearrange("p (r c) -> p r c", c=64)
                            op("dve", lambda e: e.tensor_tensor(out=t1[:n], in0=v4(b4), in1=bc4(eacum[:n, i, hd]), op=ALU.mult), [bbuf[b4], b_ss], [b_t1])
                            op("dve", lambda e: e.tensor_tensor(out=t1[:n], in0=t1[:n], in1=v4(b3), op=ALU.add), [b_t1, bbuf[b3]], [b_t1])
                            op("dve", lambda e: e.tensor_tensor(out=t2[:n], in0=xs_tm[:n, :].rearrange("p (r c) -> p r c", c=64), in1=bc4(ptm[:n, l, C_SD + 4 * gi:C_SD + 4 * gi + 4]), op=ALU.mult), [b_xstm, b_par], [b_t2])
                            op("dve", lambda e: e.tensor_tensor(out=t1[:n], in0=t1[:n], in1=t2[:n], op=ALU.add), [b_t1, b_t2], [b_t1])
                            op("dve", lambda e: e.tensor_tensor(out=t1[:n], in0=t1[:n], in1=sz[:n, i, :].rearrange("p (r c) -> p r c", c=64), op=ALU.mult), [b_t1, b_sz], [b_t1])
                            if SST <= 6:
                                continue
                            op("act", lambda e: e.activation(out=t2_[:n, :], in_=t1_[:n, :], func=AF.Square, accum_out=ssm[:n, 0:1]), [b_t1], [b_t2, b_ssm])
                            rsqrt_inplace(ssm[:n, 0:1], b_ssm, 1.0 / 256, RMS_EPS)
                            op("dve", lambda e: e.tensor_scalar(out=t1_[:n, :], in0=t1_[:n, :], scalar1=ssm[:n, 0:1], scalar2=None, op0=ALU.mult), [b_t1, b_ssm], [b_t1])
                            if SST <= 7:
                                continue
                            b6 = bank()
                            t1f = t1_[:n, :]
                            tr(b6, banks[b6][:, 0:n], t1f[:, 0:128], n, [b_t1])
                            tr(b6, banks[b6][:, 128:128 + n], t1f[:, 128:256], n, [b_t1])
                            for c_ in range(2):
                                op("dve", lambda e: e.tensor_scalar(out=y_s[:, 2 * gi + c_, off:off + n], in0=banks[b6][:, c_ * 128:c_ * 128 + n], scalar1=pcol(l, R_SNW + 2 * gi + c_), scalar2=None, op0=ALU.mult), [bbuf[b6], b_par], [b_ys])
                            if SST <= 8:
                                continue
                            hv = Hs[:, l, gi, :].rearrange("p (r c) -> p r c", c=64)
                            op("dve", lambda e: e.tensor_tensor(out=hv, in0=hv, in1=echk[:, i, hd].unsqueeze(2).to_broadcast([128, 4, 64]), op=ALU.mult), [b_Hs[l][gi], b_ss], [b_Hs[l][gi]])
                            op("dve", lambda e: e.tensor_tensor(out=Hs[:, l, gi, :], in0=Hs[:, l, gi, :], in1=banks[b5][:, 0:256], op=ALU.add), [b_Hs[l][gi], bbuf[b5]], [b_Hs[l][gi]])
                            op("act", lambda e: e.activation(out=Hb[:, :], in_=Hs[:, l, gi, :], func=AF.Copy), [b_Hs[l][gi]], [b_Hb])
                    S.barrier()
                with ExitStack() as ph:
                  if "C" in phases:
                    psb = lambda n_, sh, dt=F32: sb(n_, sh, dt, ph)
                    seqbase = NMETA if s == 0 else 0
                    qTb2 = psb("c_qTb", [64, 4, TMAX], BF16); b_qT2 = Buf()
                    eTs = [psb(f"c_eT{i_}", [128, 4, 128], BF16) for i_ in range(3)]; b_eT = [Buf() for _ in range(3)]
                    etmp = psb("c_etmp", [128, 4, 128]); b_etmp = Buf()
                    den = psb("c_den", [128, 4]); b_den = Buf()
                    otm_ = psb("c_otm", [128, 256]); b_otm = Buf()
                    otm = otm_[:, :].rearrange("p (r c) -> p r c", c=64)
                    wkv, bwkv = load_w([win_d[l, :, O_CK:O_CK + 256], win_d[l, :, O_CV:O_CV + 256]], 8, 512)
                    kcol = lambda t_: t_ if (s == 0 and t_ < NMETA) else NMETA + 128 + (t_ - seqbase)
                    for hk in range(4):
                        for (s0, sn) in segs:
                            bi = bank()
                            proj_FM(bi, wkv, bwkv, hk * 64, 64, xnT, b_xnT, s0, sn)
                            d0 = kcol(s0)
                            op("act", lambda e: e.activation(out=kTc[:, l, hk, d0:d0 + sn], in_=banks[bi][:64, 0:sn], func=AF.Copy), [bbuf[bi]], [b_kT[l]])
                    vslot = lambda i_: 0 if (s == 0 and i_ == 0) else 2 + i_ - (1 if s == 0 else 0)
                    for i, (off, n) in enumerate(tiles):
                        bi = bank()
                        for kc in range(8):
                            mm(bi, banks[bi][:n, 0:256], xnT[:, kc, off:off + n], wkv[:, kc, 256:512], kc == 0, kc == 7, [b_xnT, bwkv])
                        op("act", lambda e: e.activation(out=vA[:n, l, vslot(i), :, 0:64], in_=banks[bi][:n, 0:256].rearrange("p (r c) -> p r c", c=64), func=AF.Copy), [bbuf[bi]], [b_vA[l]])
                    for hk in range(4):
                        wq, bwq = load_w([win_d[l, :, O_CQ + hk * 256:O_CQ + (hk + 1) * 256]], 8, 256)
                        for r_ in range(4):
                            for (s0, sn) in segs:
                                bi = bank()
                                proj_FM(bi, wq, bwq, r_ * 64, 64, xnT, b_xnT, s0, sn)
                                op("act", lambda e: e.activation(out=qTb2[:, r_, s0:s0 + sn], in_=banks[bi][:64, 0:sn], func=AF.Copy), [bbuf[bi]], [b_qT2])
                        for i, (off, n) in enumerate(tiles):
                            is_meta = (s == 0 and i == 0)
                            if is_meta:
                                kbs = [(0, 0, NMETA, Ui)]
                            else:
                                k_ = i - (1 if s == 0 else 0)
                                kbs = [(NMETA + 128 + 128 * k_, 2 + k_, 128, Ui)]
                                if k_ > 0:
                                    kbs.append((NMETA + 128 + 128 * (k_ - 1), 2 + k_ - 1, 128, Ls))
                                elif s > 0:
                                    kbs.append((NMETA, 1, 128, Ls))
                                kbs.append((0, 0, NMETA, None))
                            for idx, (kc0, vs_, nk, msk) in enumerate(kbs):
                                bi = bank()
                                pq = banks[bi][:nk, 0:4 * n].rearrange("p (r c) -> p r c", c=n)
                                for r_ in range(4):
                                    mm(bi, pq[:, r_, :], kTc[:, l, hk, kc0:kc0 + nk], qTb2[:, r_, off:off + n], True, True, [b_kT[l], b_qT2])
                                if msk is None:
                                    op("act", lambda e: e.activation(out=eTs[idx][:nk, :, 0:n], in_=pq, func=AF.Exp, scale=0.125), [bbuf[bi]], [b_eT[idx]])
                                else:
                                    op("act", lambda e: e.activation(out=etmp[:nk, :, 0:n], in_=pq, func=AF.Exp, scale=0.125), [bbuf[bi]], [b_etmp])
                                    op("dve", lambda e: e.tensor_tensor(out=eTs[idx][:nk, :, 0:n], in0=etmp[:nk, :, 0:n], in1=msk[:nk, :n].unsqueeze(1).to_broadcast([nk, 4, n]), op=ALU.mult), [b_etmp, b_const], [b_eT[idx]])
                            bo = bank()
                            po = banks[bo][:n, 0:260].rearrange("p (r c) -> p r c", c=65)
                            for r_ in range(4):
                                for idx, (kc0, vs_, nk, msk) in enumerate(kbs):
                                    mm(bo, po[:, r_, :], eTs[idx][:nk, r_, 0:n], vA[:nk, l, vs_, hk, :], idx == 0, idx == len(kbs) - 1, [b_eT[idx], b_vA[l]])
                            op("dve", lambda e: e.tensor_tensor(out=den[:n, :], in0=po[:, :, 64], in1=esink[:n, l, 4 * hk:4 * hk + 4], op=ALU.add), [bbuf[bo], b_par], [b_den])
                            op("dve", lambda e: e.reciprocal(out=den[:n, :], in_=den[:n, :]), [b_den], [b_den])
                            op("dve", lambda e: e.tensor_tensor(out=otm[:n], in0=po[:, :, 0:64], in1=den[:n, :].unsqueeze(2).to_broadcast([n, 4, 64]), op=ALU.mult), [bbuf[bo], b_den], [b_otm])
                            b6 = bank()
                            of = otm_[:n, :]
                            tr(b6, banks[b6][:, 0:n], of[:, 0:128], n, [b_otm])
                            tr(b6, banks[b6][:, 128:128 + n], of[:, 128:256], n, [b_otm])
                            for c_ in range(2):
                                op("act", lambda e: e.activation(out=y_c[:, 2 * hk + c_, off:off + n], in_=banks[b6][:, c_ * 128:c_ * 128 + n], func=AF.Copy), [bbuf[b6]], [b_yc])
                    op("dve", lambda e: e.tensor_copy(out=kTc[:, l, :, NMETA:NMETA + 128], in_=kTc[:, l, :, NMETA + TS:NMETA + TS + 128]), [b_kT[l]], [b_kT[l]])
                    op("dve", lambda e: e.tensor_copy(out=vA[:, l, 1, :, 0:64], in_=vA[:, l, 1 + TPS, :, 0:64]), [b_vA[l]], [b_vA[l]])
                    S.barrier()
                with ExitStack() as ph:
                  if "M" in phases:
                    psb = lambda n_, sh, dt=F32: sb(n_, sh, dt, ph)
                    mrg = psb("m_mrg", [128, 8, TMAX], BF16); b_mrg = Buf()
                    sig = psb("m_sig", [128, 512]); b_sig = Buf()
                    tmpm = psb("m_tmp", [128, 512]); b_tmpm = Buf()
                    acc = psb("m_acc", [128, 512]); b_acc = Buf()
                    ysrc = [(y_g, b_yg, wpg_d), (y_s, b_ys, wps_d), (y_c, b_yc, wpc_d)]
                    for fc in range(8):
                        wg, bwg = load_w([win_d[l, :, O_GATE + br * 1024 + fc * 128:O_GATE + br * 1024 + (fc + 1) * 128] for br in range(3)], 8, 384)
                        wp, bwp = load_w([ysrc[br][2][l, :, fc * 128:(fc + 1) * 128] for br in range(3)], 8, 384)
                        for (s0, sn) in segs:
                            for br in range(3):
                                b1 = bank()
                                proj_FM(b1, wg, bwg, br * 128, 128, xnT, b_xnT, s0, sn)
                                op("act", lambda e: e.activation(out=sig[:, 0:sn], in_=banks[b1][:, 0:sn], func=AF.Sigmoid), [bbuf[b1]], [b_sig])
                                b2 = bank()
                                proj_FM(b2, wp, bwp, br * 128, 128, ysrc[br][0], ysrc[br][1], s0, sn)
                                if br == 0:
                                    op("dve", lambda e: e.tensor_tensor(out=acc[:, 0:sn], in0=sig[:, 0:sn], in1=banks[b2][:, 0:sn], op=ALU.mult), [b_sig, bbuf[b2]], [b_acc])
                                else:
                                    op("dve", lambda e: e.tensor_tensor(out=tmpm[:, 0:sn], in0=sig[:, 0:sn], in1=banks[b2][:, 0:sn], op=ALU.mult), [b_sig, bbuf[b2]], [b_tmpm])
                                    if br == 1:
                                        op("dve", lambda e: e.tensor_tensor(out=acc[:, 0:sn], in0=acc[:, 0:sn], in1=tmpm[:, 0:sn], op=ALU.add), [b_acc, b_tmpm], [b_acc])
                                    else:
                                        op("dve", lambda e: e.tensor_tensor(out=mrg[:, fc, s0:s0 + sn], in0=acc[:, 0:sn], in1=tmpm[:, 0:sn], op=ALU.add), [b_acc, b_tmpm], [b_mrg])
                    if dbg == "mrg" and l == 0 and s == 0:
                        dbg_tap(mrg, b_mrg)
                    for half in range(2):
                        wo, bwo = load_w([wout_d[l, :, half * 512:(half + 1) * 512]], 8, 512)
                        for i, (off, n) in enumerate(tiles):
                            bi = bank()
                            for kc in range(8):
                                mm(bi, banks[bi][:n, 0:512], mrg[:, kc, off:off + n], wo[:, kc, :], kc == 0, kc == 7, [b_mrg, bwo])
                            op("dve", lambda e: e.tensor_tensor(out=h[:n, i, half * 512:(half + 1) * 512], in0=h[:n, i, half * 512:(half + 1) * 512], in1=banks[bi][:n, 0:512], op=ALU.add), [b_h[i], bbuf[bi]], [b_h[i]])
                    S.barrier()
                if dbg and l == 0 and s == 0 and dbg in ("y_g", "y_s", "y_c"):
                    dbg_tap({"y_g": y_g, "y_s": y_s, "y_c": y_c}[dbg], {"y_g": b_yg, "y_s": b_ys, "y_c": b_yc}[dbg])
                norm_to_FM(tiles, l, R_N2)
                with ExitStack() as ph:
                  if "F" in phases:
                    psb = lambda n_, sh, dt=F32: sb(n_, sh, dt, ph)
                    actT = psb("f_act", [128, 4, TMAX], BF16); b_actT = Buf()
                    rl = psb("f_rl", [128, 512]); b_rl = Buf()
                    for dg in range(8):
                        wu, bwu = load_w([wup_d[l, :, dg * 512:(dg + 1) * 512]], 8, 512)
                        wd, bwd = load_w([wdn_d[l, dg * 512:(dg + 1) * 512, :]], 4, 1024)
                        for c_ in range(4):
                            for (s0, sn) in segs:
                                bi = bank()
                                proj_FM(bi, wu, bwu, c_ * 128, 128, xnT, b_xnT, s0, sn)
                                op("act", lambda e: e.activation(out=rl[:, 0:sn], in_=banks[bi][:, 0:sn], func=AF.Relu), [bbuf[bi]], [b_rl])
                                op("dve", lambda e: e.tensor_tensor(out=actT[:, c_, s0:s0 + sn], in0=rl[:, 0:sn], in1=rl[:, 0:sn], op=ALU.mult), [b_rl], [b_actT])
                        for i, (off, n) in enumerate(tiles):
                            for half in range(2):
                                bi = bank()
                                for c_ in range(4):
                                    mm(bi, banks[bi][:n, 0:512], actT[:, c_, off:off + n], wd[:, c_, half * 512:(half + 1) * 512], c_ == 0, c_ == 3, [b_actT, bwd])
                                op("dve", lambda e: e.tensor_tensor(out=h[:n, i, half * 512:(half + 1) * 512], in0=h[:n, i, half * 512:(half + 1) * 512], in1=banks[bi][:n, 0:512], op=ALU.add), [b_h[i], bbuf[bi]], [b_h[i]])
                    S.barrier()

            with ExitStack() as ph:
                ot = sb("f_ot", [128, 2, D], F32, ph)
                b_ot = [Buf(), Buf()]
                k_ = 0
                for i, (off, n) in enumerate(tiles):
                    if s == 0 and i == 0:
                        continue
                    op("act", lambda e: e.activation(out=nscr[:n, :], in_=h[:n, i, :], func=AF.Square, accum_out=nsm[:n, i:i + 1]), [b_h[i]], [b_nscr, b_nsm])
                    rsqrt_inplace(nsm[:n, i:i + 1], b_nsm, 1.0 / D, RMS_EPS)
                    op("dve", lambda e: e.scalar_tensor_tensor(out=ot[:n, k_ % 2, :], in0=h[:n, i, :], scalar=nsm[:n, i:i + 1], in1=fnw[:n, :], op0=ALU.mult, op1=ALU.mult),
                       [b_h[i], b_nsm, b_par], [b_ot[k_ % 2]])
                    r0 = seq0 + off - (NMETA if s == 0 else 0)
                    S.dma([(out_d[r0:r0 + n, :], ot[:n, k_ % 2, :])], reads=[b_ot[k_ % 2]], q="act")
                    k_ += 1
                S.barrier()
        S.final_wait("sp")
        S.final_wait("act")
        print("instructions", S.n_ins, "waits", S.n_wait)
    return nc


def pack_params(inp):
    pfm = np.zeros((2, 256, 128), np.float32)
    ptm = np.zeros((2, PTM_W), np.float32)
    for l in range(2):
        pfm[l, R_N1:R_N1 + 8] = np.asarray(inp["norm1_w"][l]).reshape(8, 128)
        pfm[l, R_N2:R_N2 + 8] = np.asarray(inp["norm2_w"][l]).reshape(8, 128)
        pfm[l, R_GCW:R_GCW + 96] = np.asarray(inp["gdn_conv_w"][l]).reshape(96, 128)
        pfm[l, R_SCW:R_SCW + 64] = np.asarray(inp["ssd_conv_w"][l]).reshape(64, 128)
        pfm[l, R_SCB:R_SCB + 16] = np.asarray(inp["ssd_conv_b"][l]).reshape(16, 128)
        pfm[l, R_GNW] = np.asarray(inp["gdn_norm_w"][l])
        pfm[l, R_SNW:R_SNW + 8] = np.asarray(inp["ssd_norm_w"][l]).reshape(8, 128)
        ptm[l, C_GAL:C_GAL + 8] = np.asarray(inp["gdn_a_log"][l])
        ptm[l, C_GDB:C_GDB + 8] = np.asarray(inp["gdn_dt_bias"][l])
        ptm[l, C_SDB:C_SDB + 16] = np.asarray(inp["ssd_dt_bias"][l])
        ptm[l, C_SAL:C_SAL + 16] = np.asarray(inp["ssd_a_log"][l])
        ptm[l, C_SD:C_SD + 16] = np.asarray(inp["ssd_d"][l])
        ptm[l, C_SNK:C_SNK + 16] = np.asarray(inp["swa_sinks"][l])
    return pfm, ptm


_NC_CACHE = {}


def kernel(**inputs):
    inp = {k: np.asarray(v) for k, v in inputs.items()}
    n = 8
    if "nc" not in _NC_CACHE:
        _NC_CACHE["nc"] = build(NST=8, NL=2, TPS=4)
    nc = _NC_CACHE["nc"]
    pfm, ptm = pack_params(inp)
    f32 = lambda a: np.ascontiguousarray(a, dtype=np.float32)
    shared = dict(meta=f32(inp["meta_tokens"]), w_in=f32(inp["w_in"]), w_pg=f32(inp["w_proj_gdn"]), w_ps=f32(inp["w_proj_ssd"]),
                  w_pc=f32(inp["w_proj_swa"]), w_out=f32(inp["w_out"]), w_up=f32(inp["w_up"]), w_dn=f32(inp["w_down"]),
                  pfm=pfm, ptm=ptm, fnw=f32(inp["final_norm_w"]).reshape(1, -1))
    in_maps = [dict(shared, x=f32(inp["x"][i])) for i in range(n)]
    res = run_bass_kernel_spmd(nc, in_maps, core_ids=list(range(n)))
    return np.stack([np.asarray(r["out"], dtype=np.float32) for r in res.results], axis=0)
```

```python
import numpy as np
from contextlib import ExitStack
import concourse.bass as bass
import concourse.mybir as mybir
from concourse.bass_utils import run_bass_kernel_spmd

F32 = mybir.dt.float32
BF16 = mybir.dt.bfloat16
AF = mybir.ActivationFunctionType
ALU = mybir.AluOpType
AX = mybir.AxisListType

D = 1024
SEQ = 4096
NMETA = 16
DFF = 4096
IN_W = 11808
O_GQ, O_GK, O_GV, O_GG, O_GB, O_GA = 0, 1024, 2048, 3072, 4096, 4104
O_SZ, O_SX, O_SB, O_SC, O_SDT = 4112, 5136, 6160, 6672, 7184
O_CQ, O_CK, O_CV, O_GATE = 7200, 8224, 8480, 8736
RMS_EPS = 1e-6
L2_EPS = 1e-6
R_N1, R_N2, R_GCW, R_SCW, R_SCB, R_GNW, R_SNW = 0, 8, 16, 112, 176, 192, 193
C_GAL, C_GDB, C_SDB, C_SAL, C_SD, C_SNK, PTM_W = 0, 8, 16, 32, 48, 64, 80


class Buf:
    __slots__ = ("name", "lw", "rd")

    def __init__(self, name=""):
        self.name = name
        self.lw = None
        self.rd = {}


class Sched:
    ENG = ("pe", "act", "dve", "pool", "sp")
    EPOCH = 30000

    def __init__(self, nc, stack, n_dma_sems=24):
        self.nc = nc
        self.stack = stack
        self.eng = {"pe": nc.tensor, "act": nc.scalar, "dve": nc.vector,
                    "pool": nc.gpsimd, "sp": nc.sync}
        self.semh = {}
        self.cnt = {}
        self.epoch = {}
        for e in self.ENG:
            self.epoch[e] = 0
            self.cnt[e] = 0
            self.semh[(e, 0)] = stack.enter_context(nc.semaphore(f"s_{e}_0"))
        self.waited = {e: {} for e in self.ENG}
        self.ndma = n_dma_sems
        self.dma_tot = [0] * n_dma_sems
        for j in range(n_dma_sems):
            self.semh[("d", j)] = stack.enter_context(nc.semaphore(f"s_dma_{j}"))
        self.dma_next = 0
        self.n_ins = 0
        self.n_wait = 0

    def _wait(self, e, tok):
        key, val = tok
        if self.waited[e].get(key, 0) >= val:
            return
        if key[0] == e and e == "pe":
            return
        self.eng[e].wait_ge(self.semh[key], val)
        self.waited[e][key] = val
        self.n_wait += 1

    def _deps(self, reads, writes):
        deps = {}

        def add(k, v):
            if deps.get(k, 0) < v:
                deps[k] = v
        for b in reads:
            if b.lw is not None:
                add(*b.lw)
        for b in writes:
            if b.lw is not None:
                add(*b.lw)
            for k, v in b.rd.items():
                add(k, v)
        return deps

    def _mark(self, tok, reads, writes):
        k, v = tok
        for b in reads:
            if b.rd.get(k, 0) < v:
                b.rd[k] = v
        for b in writes:
            b.lw = tok
            b.rd = {}

    def op(self, e, fn, reads=(), writes=()):
        deps = self._deps(reads, writes)
        for k, v in deps.items():
            self._wait(e, (k, v))
        if self.cnt[e] >= self.EPOCH:
            self.epoch[e] += 1
            self.cnt[e] = 0
            self.semh[(e, self.epoch[e])] = self.stack.enter_context(
                self.nc.semaphore(f"s_{e}_{self.epoch[e]}"))
        ins = fn(self.eng[e])
        self.cnt[e] += 1
        key = (e, self.epoch[e])
        ins.then_inc(self.semh[key], 1)
        tok = (key, self.cnt[e])
        self._mark(tok, reads, writes)
        self.n_ins += 1
        return tok

    def dma(self, pairs, reads=(), writes=(), q="sp"):
        j = self.dma_next
        self.dma_next = (self.dma_next + 1) % self.ndma
        key = ("d", j)
        if self.dma_tot[j] > 0:
            self._wait(q, (key, self.dma_tot[j]))
        deps = self._deps(reads, writes)
        for k, v in deps.items():
            self._wait(q, (k, v))
        for (o, i) in pairs:
            self.eng[q].dma_start(out=o, in_=i).then_inc(self.semh[key], 16)
            self.dma_tot[j] += 16
            self.n_ins += 1
        tok = (key, self.dma_tot[j])
        self._mark(tok, reads, writes)
        return tok

    def all_tokens(self):
        toks = []
        for e in self.ENG:
            if self.cnt[e] > 0:
                toks.append(((e, self.epoch[e]), self.cnt[e]))
            elif self.epoch[e] > 0:
                toks.append(((e, self.epoch[e] - 1), self.EPOCH))
        toks += [(("d", j), self.dma_tot[j]) for j in range(self.ndma) if self.dma_tot[j] > 0]
        return toks

    def barrier(self):
        toks = self.all_tokens()
        for e in self.ENG:
            if e in ("sp", "pool"):
                continue
            for t in toks:
                self._wait(e, t)

    def final_wait(self, e="sp"):
        for t in self.all_tokens():
            self._wait(e, t)


def build(NST=8, NL=2, TPS=4, dbg=False, phases="GSCMF"):
    TS = TPS * 128
    TMAX = TS + NMETA
    NTILE = TPS + 1
    nc = bass.Bass("TRN2", target_bir_lowering=False)
    dram = lambda n, s, k="ExternalInput": nc.dram_tensor(n, s, F32, kind=k).ap()
    x_d = dram("x", [SEQ, D])
    meta_d = dram("meta", [NMETA, D])
    win_d = dram("w_in", [2, D, IN_W])
    wpg_d = dram("w_pg", [2, D, D])
    wps_d = dram("w_ps", [2, D, D])
    wpc_d = dram("w_pc", [2, D, D])
    wout_d = dram("w_out", [2, D, D])
    wup_d = dram("w_up", [2, D, DFF])
    wdn_d = dram("w_dn", [2, DFF, D])
    pfm_d = dram("pfm", [2, 256, 128])
    ptm_d = dram("ptm", [2, PTM_W])
    fnw_d = dram("fnw", [1, D])
    out_d = dram("out", [NST * TS, D], "ExternalOutput")
    dbg_d = dram("dbg", [128, 8 * 528], "ExternalOutput") if dbg else None

    with ExitStack() as st:
        S = Sched(nc, st)
        sbytes = [0]

        def sb(name, shape, dt=F32, stack=None):
            sbytes[0] += 1
            t = (stack or st).enter_context(nc.sbuf_tensor(f"sb{sbytes[0]}_{name}", shape, dt))
            return t

        def op(e, fn, r=(), w=()):
            w = list(w) + [b for b in r if b.name.startswith("bank") and b not in w]
            return S.op(e, fn, reads=r, writes=w)

        banks = [st.enter_context(nc.psum_tensor(f"bank{i}", [128, 512], F32)) for i in range(8)]
        bbuf = [Buf(f"bank{i}") for i in range(8)]
        reserved = [False] * 8
        bank_rr = [0]

        def bank(reserve=False):
            for _ in range(8):
                i = bank_rr[0]
                bank_rr[0] = (i + 1) % 8
                if not reserved[i]:
                    if reserve:
                        reserved[i] = True
                    return i
            raise RuntimeError("no psum bank")

        def mm(bi, out_ap, lhsT, rhs, start, stop, r):
            op("pe", lambda e: e.matmul(out_ap, lhsT=lhsT, rhs=rhs, start=start, stop=stop), r, [bbuf[bi]])

        def tr(bi, out_ap, in_ap, kparts, r):
            op("pe", lambda e: e.transpose(out=out_ap, in_=in_ap, identity=ident[:kparts, :kparts]),
               list(r) + [b_const], [bbuf[bi]])

        ident = sb("ident", [128, 128])
        Ui = sb("Ui", [128, 128])
        Ls = sb("Ls", [128, 128])
        nUs = sb("nUs", [128, 128])
        ones = sb("ones", [128, 128])
        b_const = Buf("const")
        for t_, val in ((ident, 1.0), (Ui, 1.0), (Ls, 1.0), (nUs, -1.0), (ones, 1.0)):
            op("pool", lambda e, t_=t_, val=val: e.memset(t_[:], val), [], [b_const])
        sel = lambda t_, pat, cm, cmp: op("pool", lambda e: e.affine_select(
            out=t_[:], in_=t_[:], pattern=[[pat, 128]], compare_op=cmp, fill=0.0, base=0, channel_multiplier=cm),
            [b_const], [b_const])
        sel(ident, -1, 1, ALU.is_equal)
        sel(Ui, 1, -1, ALU.is_ge)
        sel(Ls, -1, 1, ALU.is_gt)
        sel(nUs, 1, -1, ALU.is_gt)

        pfmT = sb("pfmT", [128, 2, 256])
        ptm = sb("ptm", [128, 2, PTM_W])
        fnw = sb("fnw", [128, D])
        negA_g = sb("negA_g", [128, 2, 8])
        A_s = sb("A_s", [128, 2, 16])
        esink = sb("esink", [128, 2, 16])
        b_par = Buf("par")
        ptmp = sb("ptmp", [128, 2, 128])
        b_ptmp = Buf("ptmp")
        for l in range(2):
            S.dma([(ptmp[:, 0, :], pfm_d[l, 0:128, :]), (ptmp[:, 1, :], pfm_d[l, 128:256, :])],
                  writes=[b_ptmp], q="act")
            bi = bank()
            for hlf in range(2):
                tr(bi, banks[bi][:, hlf * 128:(hlf + 1) * 128], ptmp[:, hlf, :], 128, [b_ptmp])
            op("dve", lambda e: e.tensor_copy(out=pfmT[:, l, :], in_=banks[bi][:, 0:256]), [bbuf[bi]], [b_par])
            S.dma([(ptm[:, l, :], ptm_d[l:l + 1, :].partition_broadcast(128))], writes=[b_par], q="act")
        S.dma([(fnw[:], fnw_d[0:1, :].partition_broadcast(128))], writes=[b_par], q="act")
        for l in range(2):
            op("act", lambda e: e.activation(out=negA_g[:, l, :], in_=ptm[:, l, C_GAL:C_GAL + 8], func=AF.Exp), [b_par], [b_par])
            op("act", lambda e: e.activation(out=A_s[:, l, :], in_=ptm[:, l, C_SAL:C_SAL + 16], func=AF.Exp), [b_par], [b_par])
            op("act", lambda e: e.activation(out=esink[:, l, :], in_=ptm[:, l, C_SNK:C_SNK + 16], func=AF.Exp), [b_par], [b_par])
            op("dve", lambda e: e.tensor_scalar(out=negA_g[:, l, :], in0=negA_g[:, l, :], scalar1=-1.0, scalar2=None, op0=ALU.mult), [b_par], [b_par])
            op("dve", lambda e: e.tensor_scalar(out=A_s[:, l, :], in0=A_s[:, l, :], scalar1=-1.0, scalar2=None, op0=ALU.mult), [b_par], [b_par])
        pcol = lambda l, row: pfmT[:, l, row:row + 1]

        h = sb("h", [128, NTILE, D])
        b_h = [Buf(f"h{i}") for i in range(NTILE)]
        xnT = sb("xnT", [128, 8, TMAX], BF16)
        b_xnT = Buf("xnT")
        y_g = sb("y_g", [128, 8, TMAX], BF16)
        y_s = sb("y_s", [128, 8, TMAX], BF16)
        y_c = sb("y_c", [128, 8, TMAX], BF16)
        b_yg, b_ys, b_yc = Buf("yg"), Buf("ys"), Buf("yc")
        Sg = sb("Sg", [128, 2, 8, 128])
        b_Sg = [[Buf() for _ in range(8)] for _ in range(2)]
        Hs = sb("Hs", [128, 2, 4, 256])
        b_Hs = [[Buf() for _ in range(4)] for _ in range(2)]
        halo_g = sb("halo_g", [128, 2, 24, 3])
        halo_s = sb("halo_s", [128, 2, 16, 3])
        b_halo = Buf("halo")
        KW = NMETA + 128 + TS
        kTc = sb("kTc", [64, 2, 4, KW], BF16)
        b_kT = [Buf() for _ in range(2)]
        vA = sb("vA", [128, 2, 2 + TPS, 4, 65], BF16)
        b_vA = [Buf() for _ in range(2)]
        for t_ in (Sg, Hs, halo_g, halo_s):
            op("pool", lambda e, t_=t_: e.memset(t_[:], 0.0), [], [b_halo])
        op("pool", lambda e: e.memset(kTc[:], 0.0), [], [b_kT[0], b_kT[1]])
        op("pool", lambda e: e.memset(vA[:], 1.0), [], [b_vA[0], b_vA[1]])
        for l in range(2):
            for hh in range(8):
                b_Sg[l][hh].lw = b_halo.lw
            for g_ in range(4):
                b_Hs[l][g_].lw = b_halo.lw

        NSTG, NWB = 2, 3
        stg = [sb(f"stg{i}", [128, 2048]) for i in range(NSTG)]
        b_stg = [Buf() for _ in range(NSTG)]
        wbf = [sb(f"wbf{i}", [128, 4096], BF16) for i in range(NWB)]
        b_wbf = [Buf() for _ in range(NWB)]
        wrr = [0, 0]

        def issue_w(parts, kc, cols):
            wi = wrr[1]
            wrr[1] = (wi + 1) % NWB
            wv = wbf[wi][:, 0:kc * cols].rearrange("p (k c) -> p k c", k=kc)
            nsplit = 2 if kc * cols > 2048 else 1
            assert kc % nsplit == 0 and kc * cols // nsplit <= 2048
            kh = kc // nsplit
            for hf in range(nsplit):
                si = wrr[0]
                wrr[0] = (si + 1) % NSTG
                sv = stg[si][:, 0:kh * cols].rearrange("p (k c) -> p k c", k=kh)
                pairs = []
                c0 = 0
                for d_ap in parts:
                    c = d_ap.shape[1]
                    pairs.append((sv[:, :, c0:c0 + c], d_ap[hf * kh * 128:(hf + 1) * kh * 128, :].rearrange("(k p) c -> p k c", p=128)))
                    c0 += c
                assert c0 == cols
                S.dma(pairs, writes=[b_stg[si]], q="sp")
                if hf == 0:
                    op("pool", lambda e: e.tensor_copy(out=wv[:, hf * kh:(hf + 1) * kh, :], in_=sv), [b_stg[si]], [b_wbf[wi]])
                else:
                    op("act", lambda e: e.activation(out=wv[:, hf * kh:(hf + 1) * kh, :], in_=sv, func=AF.Copy), [b_stg[si]], [b_wbf[wi]])
            return wv, b_wbf[wi]

        def layer_specs(l):
            sp = []
            if "G" in phases:
                sp.append(([win_d[l, :, O_GB:O_GB + 16]], 8, 16))
                for hh in range(8):
                    sp.append(([win_d[l, :, O_GQ + hh * 128:O_GQ + (hh + 1) * 128], win_d[l, :, O_GK + hh * 128:O_GK + (hh + 1) * 128],
                                win_d[l, :, O_GV + hh * 128:O_GV + (hh + 1) * 128], win_d[l, :, O_GG + hh * 128:O_GG + (hh + 1) * 128]], 8, 512))
            if "S" in phases:
                sp.append(([win_d[l, :, O_SDT:O_SDT + 16]], 8, 16))
                for gi in range(4):
                    sp.append(([win_d[l, :, O_SX + gi * 256:O_SX + (gi + 1) * 256], win_d[l, :, O_SB + gi * 128:O_SB + (gi + 1) * 128],
                                win_d[l, :, O_SC + gi * 128:O_SC + (gi + 1) * 128]], 8, 512))
                    sp.append(([win_d[l, :, O_SZ + gi * 256:O_SZ + (gi + 1) * 256]], 8, 256))
            if "C" in phases:
                sp.append(([win_d[l, :, O_CK:O_CK + 256], win_d[l, :, O_CV:O_CV + 256]], 8, 512))
                for hk in range(4):
                    sp.append(([win_d[l, :, O_CQ + hk * 256:O_CQ + (hk + 1) * 256]], 8, 256))
            if "M" in phases:
                wps_ = (wpg_d, wps_d, wpc_d)
                for fc in range(8):
                    sp.append(([win_d[l, :, O_GATE + br * 1024 + fc * 128:O_GATE + br * 1024 + (fc + 1) * 128] for br in range(3)], 8, 384))
                    sp.append(([wps_[br][l, :, fc * 128:(fc + 1) * 128] for br in range(3)], 8, 384))
                for half in range(2):
                    sp.append(([wout_d[l, :, half * 512:(half + 1) * 512]], 8, 512))
            if "F" in phases:
                for dg in range(8):
                    sp.append(([wup_d[l, :, dg * 512:(dg + 1) * 512]], 8, 512))
                    sp.append(([wdn_d[l, dg * 512:(dg + 1) * 512, :]], 4, 1024))
            return sp

        all_specs = [sp_ for _s in range(NST) for l_ in range(NL) for sp_ in layer_specs(l_)]
        wqs = {"i": 0, "pend": None}

        def load_w(parts, kc, cols):
            i = wqs["i"]
            if wqs["pend"] is None:
                wqs["pend"] = issue_w(*all_specs[i])
            spec = all_specs[i]
            assert spec[1] == kc and spec[2] == cols and len(spec[0]) == len(parts), (i, spec[1:], kc, cols)
            cur = wqs["pend"]
            wqs["i"] = i + 1
            wqs["pend"] = issue_w(*all_specs[i + 1]) if i + 1 < len(all_specs) else None
            return cur

        def softplus_inplace(x_ap, t_ap, bx, bt):
            op("act", lambda e: e.activation(out=t_ap, in_=x_ap, func=AF.Abs), [bx], [bt])
            op("act", lambda e: e.activation(out=t_ap, in_=t_ap, func=AF.Exp, scale=-1.0), [bt], [bt])
            op("act", lambda e: e.activation(out=t_ap, in_=t_ap, func=AF.Ln, bias=1.0, scale=1.0), [bt], [bt])
            op("dve", lambda e: e.scalar_tensor_tensor(out=x_ap, in0=x_ap, scalar=0.0, in1=t_ap, op0=ALU.max, op1=ALU.add), [bx, bt], [bx])

        def rsqrt_inplace(x_ap, bx, scale, eps):
            op("act", lambda e: e.activation(out=x_ap, in_=x_ap, func=AF.Sqrt, bias=eps, scale=scale), [bx], [bx])
            op("dve", lambda e: e.reciprocal(out=x_ap, in_=x_ap), [bx], [bx])

        nscr = sb("nscr", [128, D])
        b_nscr = Buf("nscr")
        nsm = sb("nsm", [128, NTILE])
        b_nsm = Buf("nsm")

        def norm_to_FM(tiles, l, row):
            for i, (off, n) in enumerate(tiles):
                op("act", lambda e: e.activation(out=nscr[:n, :], in_=h[:n, i, :], func=AF.Square, accum_out=nsm[:n, i:i + 1]),
                   [b_h[i]], [b_nscr, b_nsm])
                rsqrt_inplace(nsm[:n, i:i + 1], b_nsm, 1.0 / D, RMS_EPS)
                op("dve", lambda e: e.tensor_scalar(out=nscr[:n, :], in0=h[:n, i, :], scalar1=nsm[:n, i:i + 1], scalar2=None, op0=ALU.mult),
                   [b_h[i], b_nsm], [b_nscr])
                for half in range(2):
                    bi = bank()
                    pv = banks[bi][:, :].rearrange("p (c t) -> p c t", c=4)
                    for c in range(4):
                        cc = half * 4 + c
                        tr(bi, pv[:, c, 0:n], nscr[:n, cc * 128:(cc + 1) * 128], n, [b_nscr])
                    op("dve", lambda e: e.tensor_tensor(
                        out=xnT[:, half * 4:half * 4 + 4, off:off + n], in0=pv[:, :, 0:n],
                        in1=pfmT[:, l, row + half * 4:row + half * 4 + 4].unsqueeze(2).to_broadcast([128, 4, n]), op=ALU.mult),
                        [bbuf[bi], b_par], [b_xnT])

        def proj_FM(bi, wv, bw, c0, ncol, src, bsrc, s0, sn, kcs=8):
            for kc in range(kcs):
                mm(bi, banks[bi][:ncol, 0:sn], wv[:, kc, c0:c0 + ncol], src[:, kc, s0:s0 + sn], kc == 0, kc == kcs - 1, [bw, bsrc])

        def dbg_tap(src, bsrc):
            with ExitStack() as ph2:
                dbg_copy = sb("dbgc", [128, 8, TMAX], F32, ph2)
                b_dbg = Buf()
                op("dve", lambda e: e.tensor_copy(out=dbg_copy[:, :, :], in_=src[:, :, :]), [bsrc], [b_dbg])
                S.dma([(dbg_d.rearrange("p (c t) -> p c t", c=8), dbg_copy[:, :, :])], reads=[b_dbg], q="act")
                S.barrier()

        for s in range(NST):
            if s == 0:
                tiles = [(0, NMETA)] + [(NMETA + 128 * i, 128) for i in range(TPS)]
                segs = [(0, NMETA), (NMETA, TS)]
                T = TMAX
            else:
                tiles = [(128 * i, 128) for i in range(TPS)]
                segs = [(0, TS)]
                T = TS
            seq0 = s * TS
            if s == 0:
                cgroups = [([0], NMETA)] + [([NMETA + 64 * j for j in range(4 * g_, 4 * g_ + 4)], 64) for g_ in range(2 * TPS // 4)]
            else:
                cgroups = [([64 * j for j in range(4 * g_, 4 * g_ + 4)], 64) for g_ in range(2 * TPS // 4)]
            nchunk = sum(len(g[0]) for g in cgroups)
            pairs = []
            wl = []
            for i, (off, n) in enumerate(tiles):
                if s == 0 and i == 0:
                    pairs.append((h[:n, i, :], meta_d[:, :]))
                else:
                    r0 = seq0 + off - (NMETA if s == 0 else 0)
                    pairs.append((h[:n, i, :], x_d[r0:r0 + n, :]))
                wl.append(b_h[i])
            S.dma(pairs, writes=wl, q="act")

            for l in range(NL):
                norm_to_FM(tiles, l, R_N1)
                with ExitStack() as ph:
                  if "G" in phases:
                    psb = lambda n_, sh, dt=F32: sb(n_, sh, dt, ph)
                    NCH = 2 * TPS + 1
                    ba = psb("g_ba", [64, NCH, 16]); b_ba = Buf()
                    tsm = psb("g_tsm", [64, NCH, 8]); b_tsm = Buf()
                    beta = psb("g_beta", [64, NCH, 8]); gsm = psb("g_gsm", [64, NCH, 8])
                    bk = psb("g_bk", [64, NCH, 8]); etail = psb("g_etail", [64, NCH, 8])
                    eglast = psb("g_eglast", [128, NCH, 8]); b_sm = Buf()
                    wv, bw = load_w([win_d[l, :, O_GB:O_GB + 16]], 8, 16)
                    ci = 0
                    cinfo = []
                    for offs, cs in cgroups:
                        bi = bank()
                        for j, off in enumerate(offs):
                            for kc in range(8):
                                mm(bi, banks[bi][:cs, j * 16:(j + 1) * 16], xnT[:, kc, off:off + cs], wv[:, kc, 0:16], kc == 0, kc == 7, [b_xnT, bw])
                            cinfo.append((ci + j, off, cs))
                        nj = len(offs)
                        op("dve", lambda e: e.tensor_copy(out=ba[:cs, ci:ci + nj, :], in_=banks[bi][:cs, 0:nj * 16].rearrange("p (j c) -> p j c", c=16)), [bbuf[bi]], [b_ba])
                        ci += nj
                    assert ci == nchunk
                    NC_ = nchunk
                    op("act", lambda e: e.activation(out=beta[:, 0:NC_, :], in_=ba[:, 0:NC_, 0:8], func=AF.Sigmoid), [b_ba], [b_sm])
                    op("dve", lambda e: e.tensor_tensor(out=gsm[:, 0:NC_, :], in0=ba[:, 0:NC_, 8:16], in1=ptm[:64, l, C_GDB:C_GDB + 8].unsqueeze(1).to_broadcast([64, NC_, 8]), op=ALU.add), [b_ba, b_par], [b_sm])
                    softplus_inplace(gsm[:, 0:NC_, :].rearrange("p j c -> p (j c)"), tsm[:, 0:NC_, :].rearrange("p j c -> p (j c)"), b_sm, b_tsm)
                    op("dve", lambda e: e.tensor_tensor(out=gsm[:, 0:NC_, :], in0=gsm[:, 0:NC_, :], in1=negA_g[:64, l, :].unsqueeze(1).to_broadcast([64, NC_, 8]), op=ALU.mult), [b_sm, b_par], [b_sm])
                    ci = 0
                    for offs, cs in cgroups:
                        nj = len(offs)
                        rhs = gsm[:cs, ci:ci + nj, :].rearrange("p j c -> p (j c)")
                        bi = bank()
                        mm(bi, banks[bi][:cs, 0:nj * 8], Ui[:cs, :cs], rhs, True, True, [b_sm, b_const])
                        mm(bi, banks[bi][:, 128:128 + nj * 8], ones[:cs, :], rhs, True, True, [b_sm, b_const])
                        gam_v = banks[bi][:cs, 0:nj * 8].rearrange("p (j c) -> p j c", c=8)
                        gl_v = banks[bi][:, 128:128 + nj * 8].rearrange("p (j c) -> p j c", c=8)
                        op("act", lambda e: e.activation(out=bk[:cs, ci:ci + nj, :], in_=gam_v, func=AF.Exp), [bbuf[bi]], [b_sm])
                        op("dve", lambda e: e.tensor_tensor(out=bk[:cs, ci:ci + nj, :], in0=bk[:cs, ci:ci + nj, :], in1=beta[:cs, ci:ci + nj, :], op=ALU.mult), [b_sm], [b_sm])
                        op("act", lambda e: e.activation(out=eglast[:, ci:ci + nj, :], in_=gl_v, func=AF.Exp), [bbuf[bi]], [b_sm])
                        op("act", lambda e: e.activation(out=tsm[:cs, ci:ci + nj, :], in_=gl_v[:cs], func=AF.Copy), [bbuf[bi]], [b_tsm])
                        op("dve", lambda e: e.tensor_tensor(out=etail[:cs, ci:ci + nj, :], in0=tsm[:cs, ci:ci + nj, :], in1=gam_v, op=ALU.subtract), [b_tsm, bbuf[bi]], [b_sm])
                        op("act", lambda e: e.activation(out=etail[:cs, ci:ci + nj, :], in_=etail[:cs, ci:ci + nj, :], func=AF.Exp), [b_sm], [b_sm])
                        ci += nj
                    GST = 99
                    xq = psb("g_xq", [128, 3, TMAX + 3]); b_xq = Buf()
                    cq = psb("g_cq", [128, 3, TMAX]); b_cq = Buf()
                    sgt = psb("g_sgt", [128, TMAX]); b_sgt = Buf()
                    sq = psb("g_sq", [128, TMAX]); b_sq = Buf()
                    rin = psb("g_rin", [128, TMAX]); b_rin = Buf()
                    egb = psb("g_egb", [128, TMAX]); b_egb = Buf()
                    kTb = psb("g_kTb", [128, TMAX], BF16); qTb = psb("g_qTb", [128, TMAX], BF16); qdb = psb("g_qdb", [128, TMAX], BF16); b_qk = Buf()
                    m64 = [psb(f"g_m{i}", [64, NCH, 64]) for i in range(9)]
                    b_m = [Buf() for _ in range(9)]
                    E_, DT_, MB_, Bm, BT_, P_, PT_, M_, M2_ = m64
                    bE, bDT, bMB, bBm, bBT, bP, bPT, bM, bM2 = b_m
                    Rk = psb("g_Rk", [64, NCH, 128], BF16); Rv = psb("g_Rv", [64, NCH, 128], BF16); ktl = psb("g_ktl", [64, NCH, 128], BF16)
                    TTb = psb("g_TTb", [64, NCH, 64], BF16); aTb = psb("g_aTb", [64, NCH, 64], BF16)
                    nWT = psb("g_nWT", [128, NCH, 64], BF16)
                    oT = psb("g_oT", [128, TMAX]); b_oT = Buf()
                    vnb = psb("g_vnb", [64, 128], BF16); b_vnb = Buf()
                    Sb = psb("g_Sb", [128, 128], BF16); b_Sb = Buf()
                    for hh in range(8 if GST > 0 else 0):
                        wv, bw = load_w([win_d[l, :, O_GQ + hh * 128:O_GQ + (hh + 1) * 128], win_d[l, :, O_GK + hh * 128:O_GK + (hh + 1) * 128],
                                         win_d[l, :, O_GV + hh * 128:O_GV + (hh + 1) * 128], win_d[l, :, O_GG + hh * 128:O_GG + (hh + 1) * 128]], 8, 512)
                        for qi in range(3):
                            op("dve", lambda e: e.tensor_copy(out=xq[:, qi, 0:3], in_=halo_g[:, l, qi * 8 + hh, :]), [b_halo], [b_xq])
                        for qi in range(4):
                            for (s0, sn) in segs:
                                bi = bank()
                                proj_FM(bi, wv, bw, qi * 128, 128, xnT, b_xnT, s0, sn)
                                if qi < 3:
                                    op("act", lambda e: e.activation(out=xq[:, qi, 3 + s0:3 + s0 + sn], in_=banks[bi][:, 0:sn], func=AF.Copy), [bbuf[bi]], [b_xq])
                                else:
                                    op("act", lambda e: e.activation(out=sgt[:, s0:s0 + sn], in_=banks[bi][:, 0:sn], func=AF.Silu), [bbuf[bi]], [b_sgt])
                        for qi in range(3):
                            chn = qi * 8 + hh
                            cw = lambda tap: pcol(l, R_GCW + tap * 24 + chn)
                            op("dve", lambda e: e.tensor_scalar(out=cq[:, qi, 0:T], in0=xq[:, qi, 3:3 + T], scalar1=cw(3), scalar2=None, op0=ALU.mult), [b_xq, b_par], [b_cq])
                            for tap in range(3):
                                op("dve", lambda e: e.scalar_tensor_tensor(out=cq[:, qi, 0:T], in0=xq[:, qi, tap:tap + T], scalar=cw(tap), in1=cq[:, qi, 0:T], op0=ALU.mult, op1=ALU.add), [b_xq, b_par, b_cq], [b_cq])
                            op("dve", lambda e: e.tensor_copy(out=halo_g[:, l, chn, :], in_=xq[:, qi, T:T + 3]), [b_xq], [b_halo])
                            op("act", lambda e: e.activation(out=cq[:, qi, 0:T], in_=cq[:, qi, 0:T], func=AF.Silu), [b_cq], [b_cq])
                        for qi in range(2):
                            op("dve", lambda e: e.tensor_tensor(out=sq[:, 0:T], in0=cq[:, qi, 0:T], in1=cq[:, qi, 0:T], op=ALU.mult), [b_cq], [b_sq])
                            for (s0, sn) in segs:
                                bi = bank()
                                mm(bi, banks[bi][:, 0:sn], ones[:, :], sq[:, s0:s0 + sn], True, True, [b_sq, b_const])
                                op("act", lambda e: e.activation(out=rin[:, s0:s0 + sn], in_=banks[bi][:, 0:sn], func=AF.Sqrt, bias=L2_EPS, scale=1.0), [bbuf[bi]], [b_rin])
                            op("dve", lambda e: e.reciprocal(out=rin[:, 0:T], in_=rin[:, 0:T]), [b_rin], [b_rin])
                            if qi == 0:
                                op("dve", lambda e: e.scalar_tensor_tensor(out=cq[:, 0, 0:T], in0=cq[:, 0, 0:T], scalar=128.0 ** -0.5, in1=rin[:, 0:T], op0=ALU.mult, op1=ALU.mult), [b_cq, b_rin], [b_cq])
                            else:
                                op("dve", lambda e: e.tensor_tensor(out=cq[:, 1, 0:T], in0=cq[:, 1, 0:T], in1=rin[:, 0:T], op=ALU.mult), [b_cq, b_rin], [b_cq])
                        op("act", lambda e: e.activation(out=kTb[:, 0:T], in_=cq[:, 1, 0:T], func=AF.Copy), [b_cq], [b_qk])
                        op("act", lambda e: e.activation(out=qTb[:, 0:T], in_=cq[:, 0, 0:T], func=AF.Copy), [b_cq], [b_qk])
                        def pre_gen(offs, cs, ci, G):
                            nj = len(offs)
                            gs0 = offs[0]
                            gl_ = nj * cs
                            v3 = lambda t_: t_[:cs, ci:ci + nj, 0:cs]
                            Uib = Ui[:cs, :cs].unsqueeze(1).to_broadcast([cs, nj, cs])
                            p3 = lambda b_: banks[b_][:cs, 0:nj * cs].rearrange("p (j c) -> p j c", c=cs)
                            op("dve", lambda e: e.tensor_tensor(out=v3(DT_), in0=gsm[:cs, ci:ci + nj, hh].unsqueeze(2).to_broadcast([cs, nj, cs]), in1=Uib, op=ALU.mult), [b_sm, b_const], [G["DT"]])
                            b1 = bank()
                            for j in range(nj):
                                mm(b1, p3(b1)[:, j, :], Ls[:cs, :cs], DT_[:cs, ci + j, 0:cs], True, True, [G["DT"], b_const])
                            op("act", lambda e: e.activation(out=v3(E_), in_=p3(b1), func=AF.Exp), [bbuf[b1]], [G["E"]])
                            yield
                            op("dve", lambda e: e.tensor_tensor(out=v3(MB_), in0=beta[:cs, ci:ci + nj, hh].unsqueeze(2).to_broadcast([cs, nj, cs]), in1=ident[:cs, :cs].unsqueeze(1).to_broadcast([cs, nj, cs]), op=ALU.mult), [b_sm, b_const], [G["MB"]])
                            b2 = bank()
                            for j in range(nj):
                                mm(b2, p3(b2)[:, j, :], ones[:cs, :cs], MB_[:cs, ci + j, 0:cs], True, True, [G["MB"], b_const])
                            op("dve", lambda e: e.tensor_tensor(out=v3(MB_), in0=v3(E_), in1=p3(b2), op=ALU.mult), [G["E"], bbuf[b2]], [G["MB"]])
                            op("dve", lambda e: e.tensor_tensor(out=v3(MB_), in0=v3(MB_), in1=nUs[:cs, :cs].unsqueeze(1).to_broadcast([cs, nj, cs]), op=ALU.mult), [G["MB"], b_const], [G["MB"]])
                            op("dve", lambda e: e.tensor_tensor(out=v3(DT_), in0=v3(E_), in1=Uib, op=ALU.mult), [G["E"], b_const], [G["DT"]])
                            yield
                            b1 = bank()
                            for j in range(nj):
                                o_ = offs[j]
                                mm(b1, p3(b1)[:, j, :], kTb[:, o_:o_ + cs], kTb[:, o_:o_ + cs], True, True, [b_qk])
                            op("dve", lambda e: e.tensor_tensor(out=v3(Bm), in0=v3(MB_), in1=p3(b1), op=ALU.mult), [G["MB"], bbuf[b1]], [G["Bm"]])
                            yield
                            b1 = bank()
                            for j in range(nj):
                                tr(b1, p3(b1)[:, j, :], Bm[:cs, ci + j, 0:cs], cs, [G["Bm"]])
                            op("act", lambda e: e.activation(out=v3(BT_), in_=p3(b1), func=AF.Copy), [bbuf[b1]], [G["BT"]])
                            op("dve", lambda e: e.tensor_tensor(out=v3(M_), in0=v3(Bm), in1=ident[:cs, :cs].unsqueeze(1).to_broadcast([cs, nj, cs]), op=ALU.add), [G["Bm"], b_const], [G["M"]])
                            yield
                            b3 = bank()
                            for j in range(nj):
                                mm(b3, banks[b3][:, j * cs:(j + 1) * cs], gsm[:cs, ci + j, hh:hh + 1].to_broadcast([cs, 128]), Ui[:cs, :cs], True, True, [b_sm, b_const])
                            op("act", lambda e: e.activation(out=egb[:, gs0:gs0 + gl_], in_=banks[b3][:, 0:gl_], func=AF.Exp), [bbuf[b3]], [G["egb"]])
                            op("dve", lambda e: e.tensor_tensor(out=qdb[:, gs0:gs0 + gl_], in0=cq[:, 0, gs0:gs0 + gl_], in1=egb[:, gs0:gs0 + gl_], op=ALU.mult), [b_cq, G["egb"]], [G["qdb"]])
                            nlev = 5 if cs == 64 else 3
                            Pc, PTc, bPc, bPTc = Bm, BT_, G["Bm"], G["BT"]
                            Pn, PTn, bPn, bPTn = P_, PT_, G["P"], G["PT"]
                            Mc, Mn, bMc, bMn = M_, M2_, G["M"], G["M2"]
                            def side_tr():
                                for j0 in range(0, nj, 4):
                                    jn = min(4, nj - j0)
                                    bi = bank()
                                    pk = banks[bi][:cs, 0:jn * 128].rearrange("p (j c) -> p j c", c=128)
                                    for j in range(jn):
                                        tr(bi, pk[:, j, :], cq[:, 1, offs[j0 + j]:offs[j0 + j] + cs], 128, [b_cq])
                                    bcs = lambda t_: t_[:cs, ci + j0:ci + j0 + jn, hh].unsqueeze(2).to_broadcast([cs, jn, 128])
                                    op("dve", lambda e: e.tensor_tensor(out=Rk[:cs, ci + j0:ci + j0 + jn, :], in0=pk, in1=bcs(bk), op=ALU.mult), [bbuf[bi], b_sm], [G["R"]])
                                    op("dve", lambda e: e.tensor_tensor(out=ktl[:cs, ci + j0:ci + j0 + jn, :], in0=pk, in1=bcs(etail), op=ALU.mult), [bbuf[bi], b_sm], [G["R"]])
                                    yield
                                    bi = bank()
                                    pk2 = banks[bi][:cs, 0:jn * 128].rearrange("p (j c) -> p j c", c=128)
                                    for j in range(jn):
                                        tr(bi, pk2[:, j, :], cq[:, 2, offs[j0 + j]:offs[j0 + j] + cs], 128, [b_cq])
                                    op("dve", lambda e: e.tensor_tensor(out=Rv[:cs, ci + j0:ci + j0 + jn, :], in0=pk2, in1=bcs(beta), op=ALU.mult), [bbuf[bi], b_sm], [G["R"]])
                                    yield
                                b1_ = bank()
                                for j in range(nj):
                                    o_ = offs[j]
                                    mm(b1_, p3(b1_)[:, j, :], kTb[:, o_:o_ + cs], qTb[:, o_:o_ + cs], True, True, [b_qk])
                                op("dve", lambda e: e.tensor_tensor(out=v3(aTb), in0=v3(DT_), in1=p3(b1_), op=ALU.mult), [G["DT"], bbuf[b1_]], [G["aT"]])
                                yield
                            side = side_tr()
                            for lev in range(nlev):
                                last = lev == nlev - 1
                                b2 = bank()
                                for j in range(nj):
                                    mm(b2, p3(b2)[:, j, :], Pc[:cs, ci + j, 0:cs], PTc[:cs, ci + j, 0:cs], True, True, [bPc, bPTc])
                                if not last:
                                    b1 = bank()
                                    for j in range(nj):
                                        mm(b1, p3(b1)[:, j, :], PTc[:cs, ci + j, 0:cs], Pc[:cs, ci + j, 0:cs], True, True, [bPc, bPTc])
                                op("act", lambda e: e.activation(out=v3(PTn), in_=p3(b2), func=AF.Copy), [bbuf[b2]], [bPTn])
                                if not last:
                                    op("dve", lambda e: e.tensor_copy(out=v3(Pn), in_=p3(b1)), [bbuf[b1]], [bPn])
                                yield
                                next(side, None)
                                b3 = bank()
                                for j in range(nj):
                                    mm(b3, p3(b3)[:, j, :], PTn[:cs, ci + j, 0:cs], Mc[:cs, ci + j, 0:cs], True, True, [bPTn, bMc])
                                op("dve", lambda e: e.tensor_tensor(out=v3(Mn), in0=v3(Mc), in1=p3(b3), op=ALU.add), [bMc, bbuf[b3]], [bMn])
                                Pc, Pn, bPc, bPn = Pn, Pc, bPn, bPc
                                PTc, PTn, bPTc, bPTn = PTn, PTc, bPTn, bPTc
                                Mc, Mn, bMc, bMn = Mn, Mc, bMn, bMc
                                yield
                            for _ in side:
                                yield
                            op("act", lambda e: e.activation(out=v3(TTb), in_=v3(Mc), func=AF.Copy), [bMc], [G["TT"]])
                            yield
                            b1 = bank()
                            for j in range(nj):
                                mm(b1, banks[b1][:, j * cs:(j + 1) * cs], Rk[:cs, ci + j, :], TTb[:cs, ci + j, 0:cs], True, True, [G["R"], G["TT"]])
                            op("act", lambda e: e.activation(out=nWT[:, ci:ci + nj, 0:cs], in_=banks[b1][:, 0:nj * cs].rearrange("p (j c) -> p j c", c=cs), func=AF.Copy, scale=-1.0), [bbuf[b1]], [G["nWT"]])
                            yield

                        def chain(offs, cs, ci, G):
                            nj = len(offs)
                            gs0 = offs[0]
                            gl_ = nj * cs
                            bo = bank(reserve=True)
                            for j in range(nj):
                                o_ = offs[j]
                                op("act", lambda e: e.activation(out=Sb[:, :], in_=Sg[:, l, hh, :], func=AF.Copy), [b_Sg[l][hh]], [b_Sb])
                                b1 = bank()
                                mm(b1, banks[b1][:cs, 0:128], TTb[:cs, ci + j, 0:cs], Rv[:cs, ci + j, :], True, False, [G["TT"], G["R"]])
                                mm(b1, banks[b1][:cs, 0:128], nWT[:, ci + j, 0:cs], Sb[:, :], False, True, [G["nWT"], b_Sb])
                                op("act", lambda e: e.activation(out=vnb[:cs, :], in_=banks[b1][:cs, 0:128], func=AF.Copy), [bbuf[b1]], [b_vnb])
                                mm(bo, banks[bo][:, j * cs:(j + 1) * cs], Sb[:, :], qdb[:, o_:o_ + cs], True, False, [b_Sb, G["qdb"]])
                                mm(bo, banks[bo][:, j * cs:(j + 1) * cs], vnb[:cs, :], aTb[:cs, ci + j, 0:cs], False, True, [b_vnb, G["aT"]])
                                b2 = bank()
                                mm(b2, banks[b2][:, 0:128], ktl[:cs, ci + j, :], vnb[:cs, :], True, True, [G["R"], b_vnb])
                                op("dve", lambda e: e.scalar_tensor_tensor(out=Sg[:, l, hh, :], in0=Sg[:, l, hh, :], scalar=eglast[:, ci + j, hh:hh + 1], in1=banks[b2][:, 0:128], op0=ALU.mult, op1=ALU.add),
                                   [b_Sg[l][hh], b_sm, bbuf[b2]], [b_Sg[l][hh]])
                            op("dve", lambda e: e.tensor_copy(out=oT[:, gs0:gs0 + gl_], in_=banks[bo][:, 0:gl_]), [bbuf[bo]], [G["oT"]])
                            reserved[bo] = False

                        grp = []
                        ci_ = 0
                        for offs, cs in cgroups:
                            Gd = {k_: Buf() for k_ in ("DT", "E", "MB", "Bm", "BT", "P", "PT", "M", "M2", "R", "TT", "aT", "nWT", "egb", "qdb", "oT")}
                            grp.append((offs, cs, ci_, Gd))
                            ci_ += len(offs)
                        gens = [pre_gen(*g_) for g_ in grp]
                        while gens:
                            for g_ in list(gens):
                                try:
                                    next(g_)
                                except StopIteration:
                                    gens.remove(g_)
                        for g_ in grp:
                            chain(*g_)
                        b_oTs = [g_[3]["oT"] for g_ in grp]
                        op("dve", lambda e: e.tensor_tensor(out=sq[:, 0:T], in0=oT[:, 0:T], in1=oT[:, 0:T], op=ALU.mult), b_oTs, [b_sq])
                        for (s0, sn) in segs:
                            bi = bank()
                            mm(bi, banks[bi][:, 0:sn], ones[:, :], sq[:, s0:s0 + sn], True, True, [b_sq, b_const])
                            op("act", lambda e: e.activation(out=rin[:, s0:s0 + sn], in_=banks[bi][:, 0:sn], func=AF.Sqrt, bias=RMS_EPS, scale=1.0 / 128), [bbuf[bi]], [b_rin])
                        op("dve", lambda e: e.reciprocal(out=rin[:, 0:T], in_=rin[:, 0:T]), [b_rin], [b_rin])
                        op("dve", lambda e: e.tensor_tensor(out=oT[:, 0:T], in0=oT[:, 0:T], in1=rin[:, 0:T], op=ALU.mult), b_oTs + [b_rin], b_oTs)
                        op("dve", lambda e: e.scalar_tensor_tensor(out=y_g[:, hh, 0:T], in0=oT[:, 0:T], scalar=pcol(l, R_GNW), in1=sgt[:, 0:T], op0=ALU.mult, op1=ALU.mult), b_oTs + [b_par, b_sgt], [b_yg])
                    S.barrier()
                ph = ExitStack()
                if True:
                  if "S" in phases:
                    psb = lambda n_, sh, dt=F32: sb(n_, sh, dt, ph)
                    NT_ = len(tiles)
                    dtp = psb("s_dtp", [128, NTILE, 16]); adt = psb("s_adt", [128, NTILE, 16]); tsm2 = psb("s_tsm", [128, NTILE, 16])
                    eacum = psb("s_eacum", [128, NTILE, 16]); edec = psb("s_edec", [128, NTILE, 16]); echk = psb("s_echk", [128, NTILE, 16])
                    dte = psb("s_dte", [128, NTILE, 16])
                    b_ss = Buf(); b_st = Buf()
                    wv, bw = load_w([win_d[l, :, O_SDT:O_SDT + 16]], 8, 16)
                    bi = bank()
                    for i, (off, n) in enumerate(tiles):
                        for kc in range(8):
                            mm(bi, banks[bi][:n, i * 16:(i + 1) * 16], xnT[:, kc, off:off + n], wv[:, kc, 0:16], kc == 0, kc == 7, [b_xnT, bw])
                    op("dve", lambda e: e.tensor_tensor(out=dtp[:, 0:NT_, :], in0=banks[bi][:, 0:NT_ * 16].rearrange("p (j c) -> p j c", c=16),
                                                        in1=ptm[:, l, C_SDB:C_SDB + 16].unsqueeze(1).to_broadcast([128, NT_, 16]), op=ALU.add), [bbuf[bi], b_par], [b_ss])
                    softplus_inplace(dtp[:, 0:NT_, :].rearrange("p j c -> p (j c)"), tsm2[:, 0:NT_, :].rearrange("p j c -> p (j c)"), b_ss, b_st)
                    op("dve", lambda e: e.tensor_tensor(out=adt[:, 0:NT_, :], in0=dtp[:, 0:NT_, :], in1=A_s[:, l, :].unsqueeze(1).to_broadcast([128, NT_, 16]), op=ALU.mult), [b_ss, b_par], [b_ss])
                    for i, (off, n) in enumerate(tiles):
                        bi = bank()
                        mm(bi, banks[bi][:n, 0:16], Ui[:n, :n], adt[:n, i, :], True, True, [b_ss, b_const])
                        mm(bi, banks[bi][:, 16:32], ones[:n, :], adt[:n, i, :], True, True, [b_ss, b_const])
                        op("act", lambda e: e.activation(out=eacum[:n, i, :], in_=banks[bi][:n, 0:16], func=AF.Exp), [bbuf[bi]], [b_ss])
                        op("act", lambda e: e.activation(out=echk[:, i, :], in_=banks[bi][:, 16:32], func=AF.Exp), [bbuf[bi]], [b_ss])
                        op("act", lambda e: e.activation(out=tsm2[:n, i, :], in_=banks[bi][:n, 16:32], func=AF.Copy), [bbuf[bi]], [b_st])
                        op("dve", lambda e: e.tensor_tensor(out=edec[:n, i, :], in0=tsm2[:n, i, :], in1=banks[bi][:n, 0:16], op=ALU.subtract), [b_st, bbuf[bi]], [b_ss])
                        op("act", lambda e: e.activation(out=edec[:n, i, :], in_=edec[:n, i, :], func=AF.Exp), [b_ss], [b_ss])
                        op("dve", lambda e: e.tensor_tensor(out=dte[:n, i, :], in0=dtp[:n, i, :], in1=edec[:n, i, :], op=ALU.mult), [b_ss], [b_ss])
                    SST = 99
                    xs4 = psb("s_xs4", [128, 4, TMAX + 3]); b_xs4 = Buf()
                    cs4 = psb("s_cs4", [128, 4, TMAX]); b_cs4 = Buf()
                    BTb = psb("s_BTb", [128, TMAX], BF16); CTb = psb("s_CTb", [128, TMAX], BF16); b_BC = Buf()
                    sz = psb("s_sz", [128, NTILE, 256]); b_sz = Buf()
                    xs_tm = psb("s_xstm", [128, 256]); b_xstm = Buf()
                    xdt = psb("s_xdt", [128, 4, 64], BF16); xdt2 = psb("s_xdt2", [128, 4, 64], BF16); b_xdt = Buf()
                    Btm = psb("s_Btm", [128, 128], BF16); b_Btm = Buf()
                    La = psb("s_La", [128, 4, 128]); b_La = Buf()
                    E4 = psb("s_E4", [128, 4, 128]); b_E4 = Buf()
                    MT = psb("s_MT", [128, 4, 128], BF16); b_MT = Buf()
                    t1_ = psb("s_t1", [128, 256]); t2_ = psb("s_t2", [128, 256]); b_t1 = Buf(); b_t2 = Buf()
                    t1 = t1_[:, :].rearrange("p (r c) -> p r c", c=64); t2 = t2_[:, :].rearrange("p (r c) -> p r c", c=64)
                    ssm = psb("s_ssm", [128, 1]); b_ssm = Buf()
                    Hb = psb("s_Hb", [128, 256], BF16); b_Hb = Buf()
                    for gi in range(4 if SST > 0 else 0):
                        wa, bwa = load_w([win_d[l, :, O_SX + gi * 256:O_SX + (gi + 1) * 256], win_d[l, :, O_SB + gi * 128:O_SB + (gi + 1) * 128],
                                          win_d[l, :, O_SC + gi * 128:O_SC + (gi + 1) * 128]], 8, 512)
                        chns = [2 * gi, 2 * gi + 1, 8 + gi, 12 + gi]
                        for qi in range(4):
                            op("dve", lambda e: e.tensor_copy(out=xs4[:, qi, 0:3], in_=halo_s[:, l, chns[qi], :]), [b_halo], [b_xs4])
                            for (s0, sn) in segs:
                                bi = bank()
                                proj_FM(bi, wa, bwa, qi * 128, 128, xnT, b_xnT, s0, sn)
                                op("act", lambda e: e.activation(out=xs4[:, qi, 3 + s0:3 + s0 + sn], in_=banks[bi][:, 0:sn], func=AF.Copy), [bbuf[bi]], [b_xs4])
                        for qi in range(4):
                            chn = chns[qi]
                            cw = lambda tap: pcol(l, R_SCW + tap * 16 + chn)
                            op("dve", lambda e: e.tensor_scalar(out=cs4[:, qi, 0:T], in0=xs4[:, qi, 3:3 + T], scalar1=cw(3), scalar2=pcol(l, R_SCB + chn), op0=ALU.mult, op1=ALU.add), [b_xs4, b_par], [b_cs4])
                            for tap in range(3):
                                op("dve", lambda e: e.scalar_tensor_tensor(out=cs4[:, qi, 0:T], in0=xs4[:, qi, tap:tap + T], scalar=cw(tap), in1=cs4[:, qi, 0:T], op0=ALU.mult, op1=ALU.add), [b_xs4, b_par, b_cs4], [b_cs4])
                            op("dve", lambda e: e.tensor_copy(out=halo_s[:, l, chn, :], in_=xs4[:, qi, T:T + 3]), [b_xs4], [b_halo])
                            op("act", lambda e: e.activation(out=cs4[:, qi, 0:T], in_=cs4[:, qi, 0:T], func=AF.Silu), [b_cs4], [b_cs4])
                        op("dve", lambda e: e.tensor_copy(out=BTb[:, 0:T], in_=cs4[:, 2, 0:T]), [b_cs4], [b_BC])
                        op("dve", lambda e: e.tensor_copy(out=CTb[:, 0:T], in_=cs4[:, 3, 0:T]), [b_cs4], [b_BC])
                        wz, bwz = load_w([win_d[l, :, O_SZ + gi * 256:O_SZ + (gi + 1) * 256]], 8, 256)
                        for i, (off, n) in enumerate(tiles):
                            bi = bank()
                            for kc in range(8):
                                mm(bi, banks[bi][:n, 0:256], xnT[:, kc, off:off + n], wz[:, kc, 0:256], kc == 0, kc == 7, [b_xnT, bwz])
                            op("act", lambda e: e.activation(out=sz[:n, i, :], in_=banks[bi][:n, 0:256], func=AF.Silu), [bbuf[bi]], [b_sz])
                        op("act", lambda e: e.activation(out=Hb[:, :], in_=Hs[:, l, gi, :], func=AF.Copy), [b_Hs[l][gi]], [b_Hb])
                        for i, (off, n) in enumerate(tiles if SST > 1 else []):
                            hd = slice(4 * gi, 4 * gi + 4)
                            bc4 = lambda ap_: ap_.unsqueeze(2).to_broadcast([n, 4, 64])
                            bi = bank()
                            tr(bi, banks[bi][:n, 0:128], cs4[:, 0, off:off + n], 128, [b_cs4])
                            tr(bi, banks[bi][:n, 128:256], cs4[:, 1, off:off + n], 128, [b_cs4])
                            px = banks[bi][:n, 0:256].rearrange("p (r c) -> p r c", c=64)
                            op("act", lambda e: e.activation(out=xs_tm[:n, :], in_=banks[bi][:n, 0:256], func=AF.Copy), [bbuf[bi]], [b_xstm])
                            op("dve", lambda e: e.tensor_tensor(out=xdt[:n], in0=px, in1=bc4(dtp[:n, i, hd]), op=ALU.mult), [bbuf[bi], b_ss], [b_xdt])
                            op("dve", lambda e: e.tensor_tensor(out=xdt2[:n], in0=px, in1=bc4(dte[:n, i, hd]), op=ALU.mult), [bbuf[bi], b_ss], [b_xdt])
                            if SST <= 2:
                                continue
                            bi = bank()
                            tr(bi, banks[bi][:n, 0:128], cs4[:, 2, off:off + n], 128, [b_cs4])
                            op("act", lambda e: e.activation(out=Btm[:n, :], in_=banks[bi][:n, 0:128], func=AF.Copy), [bbuf[bi]], [b_Btm])
                            if SST <= 3:
                                continue
                            b1 = bank()
                            mm(b1, banks[b1][:n, 0:n], BTb[:, off:off + n], CTb[:, off:off + n], True, True, [b_BC])
                            op("dve", lambda e: e.tensor_tensor(out=La[:n, :, 0:n], in0=adt[:n, i, hd].unsqueeze(2).to_broadcast([n, 4, n]), in1=Ui[:n, :n].unsqueeze(1).to_broadcast([n, 4, n]), op=ALU.mult), [b_ss, b_const], [b_La])
                            b2 = bank()
                            p2 = banks[b2][:n, 0:4 * n].rearrange("p (r c) -> p r c", c=n)
                            for r_ in range(4):
                                mm(b2, p2[:, r_, :], Ls[:n, :n], La[:n, r_, 0:n], True, True, [b_La, b_const])
                            op("act", lambda e: e.activation(out=E4[:n, :, 0:n], in_=p2, func=AF.Exp), [bbuf[b2]], [b_E4])
                            op("dve", lambda e: e.tensor_tensor(out=E4[:n, :, 0:n], in0=E4[:n, :, 0:n], in1=Ui[:n, :n].unsqueeze(1).to_broadcast([n, 4, n]), op=ALU.mult), [b_E4, b_const], [b_E4])
                            op("dve", lambda e: e.tensor_tensor(out=MT[:n, :, 0:n], in0=E4[:n, :, 0:n], in1=banks[b1][:n, 0:n].unsqueeze(1).to_broadcast([n, 4, n]), op=ALU.mult), [b_E4, bbuf[b1]], [b_MT])
                            if SST <= 4:
                                continue
                            b3 = bank()
                            for r_ in range(4):
                                mm(b3, banks[b3][:n, r_ * 64:(r_ + 1) * 64], MT[:n, r_, 0:n], xdt[:n, r_, :], True, True, [b_MT, b_xdt])
                            b4 = bank()
                            mm(b4, banks[b4][:n, 0:256], CTb[:, off:off + n], Hb[:, :], True, True, [b_BC, b_Hb])
                            b5 = bank()
                            mm(b5, banks[b5][:, 0:256], Btm[:n, :], xdt2[:n].rearrange("p r c -> p (r c)"), True, True, [b_Btm, b_xdt])
                            if SST <= 5:
                                continue
                            v4 = lambda b_: banks[b_][:n, 0:256].rearrange("p (r c) -> p r c", c=64)
                            op("dve", lambda e: e.tensor_tensor(out=t1[:n], in0=v4(b4), in1=bc4(eacum[:n, i, hd]), op=ALU.mult), [bbuf[b4], b_ss], [b_t1])
                            op("dve", lambda e: e.tensor_tensor(out=t1[:n], in0=t1[:n], in1=v4(b3), op=ALU.add), [b_t1, bbuf[b3]], [b_t1])
                            op("dve", lambda e: e.tensor_tensor(out=t2[:n], in0=xs_tm[:n, :].rearrange("p (r c) -> p r c", c=64), in1=bc4(ptm[:n, l, C_SD + 4 * gi:C_SD + 4 * gi + 4]), op=ALU.mult), [b_xstm, b_par], [b_t2])
                            op("dve", lambda e: e.tensor_tensor(out=t1[:n], in0=t1[:n], in1=t2[:n], op=ALU.add), [b_t1, b_t2], [b_t1])
                            op("dve", lambda e: e.tensor_tensor(out=t1[:n], in0=t1[:n], in1=sz[:n, i, :].rearrange("p (r c) -> p r c", c=64), op=ALU.mult), [b_t1, b_sz], [b_t1])
                            if SST <= 6:
                                continue
                            op("act", lambda e: e.activation(out=t2_[:n, :], in_=t1_[:n, :], func=AF.Square, accum_out=ssm[:n, 0:1]), [b_t1], [b_t2, b_ssm])
                            rsqrt_inplace(ssm[:n, 0:1], b_ssm, 1.0 / 256, RMS_EPS)
                            op("dve", lambda e: e.tensor_scalar(out=t1_[:n, :], in0=t1_[:n, :], scalar1=ssm[:n, 0:1], scalar2=None, op0=ALU.mult), [b_t1, b_ssm], [b_t1])
                            if SST <= 7:
                                continue
                            b6 = bank()
                            t1f = t1_[:n, :]
                            tr(b6, banks[b6][:, 0:n], t1f[:, 0:128], n, [b_t1])
                            tr(b6, banks[b6][:, 128:128 + n], t1f[:, 128:256], n, [b_t1])
                            for c_ in range(2):
                                op("dve", lambda e: e.tensor_scalar(out=y_s[:, 2 * gi + c_, off:off + n], in0=banks[b6][:, c_ * 128:c_ * 128 + n], scalar1=pcol(l, R_SNW + 2 * gi + c_), scalar2=None, op0=ALU.mult), [bbuf[b6], b_par], [b_ys])
                            if SST <= 8:
                                continue
                            hv = Hs[:, l, gi, :].rearrange("p (r c) -> p r c", c=64)
                            op("dve", lambda e: e.tensor_tensor(out=hv, in0=hv, in1=echk[:, i, hd].unsqueeze(2).to_broadcast([128, 4, 64]), op=ALU.mult), [b_Hs[l][gi], b_ss], [b_Hs[l][gi]])
                            op("dve", lambda e: e.tensor_tensor(out=Hs[:, l, gi, :], in0=Hs[:, l, gi, :], in1=banks[b5][:, 0:256], op=ALU.add), [b_Hs[l][gi], bbuf[b5]], [b_Hs[l][gi]])
                            op("act", lambda e: e.activation(out=Hb[:, :], in_=Hs[:, l, gi, :], func=AF.Copy), [b_Hs[l][gi]], [b_Hb])
                    pass
                if True:
                  if "C" in phases:
                    psb = lambda n_, sh, dt=F32: sb(n_, sh, dt, ph)
                    seqbase = NMETA if s == 0 else 0
                    qTb2 = psb("c_qTb", [64, 4, TMAX], BF16); b_qT2 = Buf()
                    eTs = [psb(f"c_eT{i_}", [128, 4, 128], BF16) for i_ in range(3)]; b_eT = [Buf() for _ in range(3)]
                    etmp = psb("c_etmp", [128, 4, 128]); b_etmp = Buf()
                    den = psb("c_den", [128, 4]); b_den = Buf()
                    otm_ = psb("c_otm", [128, 256]); b_otm = Buf()
                    otm = otm_[:, :].rearrange("p (r c) -> p r c", c=64)
                    wkv, bwkv = load_w([win_d[l, :, O_CK:O_CK + 256], win_d[l, :, O_CV:O_CV + 256]], 8, 512)
                    kcol = lambda t_: t_ if (s == 0 and t_ < NMETA) else NMETA + 128 + (t_ - seqbase)
                    for hk in range(4):
                        for (s0, sn) in segs:
                            bi = bank()
                            proj_FM(bi, wkv, bwkv, hk * 64, 64, xnT, b_xnT, s0, sn)
                            d0 = kcol(s0)
                            op("act", lambda e: e.activation(out=kTc[:, l, hk, d0:d0 + sn], in_=banks[bi][:64, 0:sn], func=AF.Copy), [bbuf[bi]], [b_kT[l]])
                    vslot = lambda i_: 0 if (s == 0 and i_ == 0) else 2 + i_ - (1 if s == 0 else 0)
                    for i, (off, n) in enumerate(tiles):
                        bi = bank()
                        for kc in range(8):
                            mm(bi, banks[bi][:n, 0:256], xnT[:, kc, off:off + n], wkv[:, kc, 256:512], kc == 0, kc == 7, [b_xnT, bwkv])
                        op("act", lambda e: e.activation(out=vA[:n, l, vslot(i), :, 0:64], in_=banks[bi][:n, 0:256].rearrange("p (r c) -> p r c", c=64), func=AF.Copy), [bbuf[bi]], [b_vA[l]])
                    for hk in range(4):
                        wq, bwq = load_w([win_d[l, :, O_CQ + hk * 256:O_CQ + (hk + 1) * 256]], 8, 256)
                        for r_ in range(4):
                            for (s0, sn) in segs:
                                bi = bank()
                                proj_FM(bi, wq, bwq, r_ * 64, 64, xnT, b_xnT, s0, sn)
                                op("act", lambda e: e.activation(out=qTb2[:, r_, s0:s0 + sn], in_=banks[bi][:64, 0:sn], func=AF.Copy), [bbuf[bi]], [b_qT2])
                        for i, (off, n) in enumerate(tiles):
                            is_meta = (s == 0 and i == 0)
                            if is_meta:
                                kbs = [(0, 0, NMETA, Ui)]
                            else:
                                k_ = i - (1 if s == 0 else 0)
                                kbs = [(NMETA + 128 + 128 * k_, 2 + k_, 128, Ui)]
                                if k_ > 0:
                                    kbs.append((NMETA + 128 + 128 * (k_ - 1), 2 + k_ - 1, 128, Ls))
                                elif s > 0:
                                    kbs.append((NMETA, 1, 128, Ls))
                                kbs.append((0, 0, NMETA, None))
                            for idx, (kc0, vs_, nk, msk) in enumerate(kbs):
                                bi = bank()
                                pq = banks[bi][:nk, 0:4 * n].rearrange("p (r c) -> p r c", c=n)
                                for r_ in range(4):
                                    mm(bi, pq[:, r_, :], kTc[:, l, hk, kc0:kc0 + nk], qTb2[:, r_, off:off + n], True, True, [b_kT[l], b_qT2])
                                if msk is None:
                                    op("act", lambda e: e.activation(out=eTs[idx][:nk, :, 0:n], in_=pq, func=AF.Exp, scale=0.125), [bbuf[bi]], [b_eT[idx]])
                                else:
                                    op("act", lambda e: e.activation(out=etmp[:nk, :, 0:n], in_=pq, func=AF.Exp, scale=0.125), [bbuf[bi]], [b_etmp])
                                    op("dve", lambda e: e.tensor_tensor(out=eTs[idx][:nk, :, 0:n], in0=etmp[:nk, :, 0:n], in1=msk[:nk, :n].unsqueeze(1).to_broadcast([nk, 4, n]), op=ALU.mult), [b_etmp, b_const], [b_eT[idx]])
                            bo = bank()
                            po = banks[bo][:n, 0:260].rearrange("p (r c) -> p r c", c=65)
                            for r_ in range(4):
                                for idx, (kc0, vs_, nk, msk) in enumerate(kbs):
                                    mm(bo, po[:, r_, :], eTs[idx][:nk, r_, 0:n], vA[:nk, l, vs_, hk, :], idx == 0, idx == len(kbs) - 1, [b_eT[idx], b_vA[l]])
                            op("dve", lambda e: e.tensor_tensor(out=den[:n, :], in0=po[:, :, 64], in1=esink[:n, l, 4 * hk:4 * hk + 4], op=ALU.add), [bbuf[bo], b_par], [b_den])
                            op("dve", lambda e: e.reciprocal(out=den[:n, :], in_=den[:n, :]), [b_den], [b_den])
                            op("dve", lambda e: e.tensor_tensor(out=otm[:n], in0=po[:, :, 0:64], in1=den[:n, :].unsqueeze(2).to_broadcast([n, 4, 64]), op=ALU.mult), [bbuf[bo], b_den], [b_otm])
                            b6 = bank()
                            of = otm_[:n, :]
                            tr(b6, banks[b6][:, 0:n], of[:, 0:128], n, [b_otm])
                            tr(b6, banks[b6][:, 128:128 + n], of[:, 128:256], n, [b_otm])
                            for c_ in range(2):
                                op("act", lambda e: e.activation(out=y_c[:, 2 * hk + c_, off:off + n], in_=banks[b6][:, c_ * 128:c_ * 128 + n], func=AF.Copy), [bbuf[b6]], [b_yc])
                    op("dve", lambda e: e.tensor_copy(out=kTc[:, l, :, NMETA:NMETA + 128], in_=kTc[:, l, :, NMETA + TS:NMETA + TS + 128]), [b_kT[l]], [b_kT[l]])
                    op("dve", lambda e: e.tensor_copy(out=vA[:, l, 1, :, 0:64], in_=vA[:, l, 1 + TPS, :, 0:64]), [b_vA[l]], [b_vA[l]])
                    pass
                if True:
                  if "M" in phases:
                    psb = lambda n_, sh, dt=F32: sb(n_, sh, dt, ph)
                    mrg = psb("m_mrg", [128, 8, TMAX], BF16); b_mrg = Buf()
                    sig = psb("m_sig", [128, 512]); b_sig = Buf()
                    tmpm = psb("m_tmp", [128, 512]); b_tmpm = Buf()
                    acc = psb("m_acc", [128, 512]); b_acc = Buf()
                    ysrc = [(y_g, b_yg, wpg_d), (y_s, b_ys, wps_d), (y_c, b_yc, wpc_d)]
                    for fc in range(8):
                        wg, bwg = load_w([win_d[l, :, O_GATE + br * 1024 + fc * 128:O_GATE + br * 1024 + (fc + 1) * 128] for br in range(3)], 8, 384)
                        wp, bwp = load_w([ysrc[br][2][l, :, fc * 128:(fc + 1) * 128] for br in range(3)], 8, 384)
                        for (s0, sn) in segs:
                            for br in range(3):
                                b1 = bank()
                                proj_FM(b1, wg, bwg, br * 128, 128, xnT, b_xnT, s0, sn)
                                op("act", lambda e: e.activation(out=sig[:, 0:sn], in_=banks[b1][:, 0:sn], func=AF.Sigmoid), [bbuf[b1]], [b_sig])
                                b2 = bank()
                                proj_FM(b2, wp, bwp, br * 128, 128, ysrc[br][0], ysrc[br][1], s0, sn)
                                if br == 0:
                                    op("dve", lambda e: e.tensor_tensor(out=acc[:, 0:sn], in0=sig[:, 0:sn], in1=banks[b2][:, 0:sn], op=ALU.mult), [b_sig, bbuf[b2]], [b_acc])
                                else:
                                    op("dve", lambda e: e.tensor_tensor(out=tmpm[:, 0:sn], in0=sig[:, 0:sn], in1=banks[b2][:, 0:sn], op=ALU.mult), [b_sig, bbuf[b2]], [b_tmpm])
                                    if br == 1:
                                        op("dve", lambda e: e.tensor_tensor(out=acc[:, 0:sn], in0=acc[:, 0:sn], in1=tmpm[:, 0:sn], op=ALU.add), [b_acc, b_tmpm], [b_acc])
                                    else:
                                        op("dve", lambda e: e.tensor_tensor(out=mrg[:, fc, s0:s0 + sn], in0=acc[:, 0:sn], in1=tmpm[:, 0:sn], op=ALU.add), [b_acc, b_tmpm], [b_mrg])
                    if dbg == "mrg" and l == 0 and s == 0:
                        dbg_tap(mrg, b_mrg)
                    for half in range(2):
                        wo, bwo = load_w([wout_d[l, :, half * 512:(half + 1) * 512]], 8, 512)
                        for i, (off, n) in enumerate(tiles):
                            bi = bank()
                            for kc in range(8):
                                mm(bi, banks[bi][:n, 0:512], mrg[:, kc, off:off + n], wo[:, kc, :], kc == 0, kc == 7, [b_mrg, bwo])
                            op("dve", lambda e: e.tensor_tensor(out=h[:n, i, half * 512:(half + 1) * 512], in0=h[:n, i, half * 512:(half + 1) * 512], in1=banks[bi][:n, 0:512], op=ALU.add), [b_h[i], bbuf[bi]], [b_h[i]])
                    pass
                if dbg and l == 0 and s == 0 and dbg in ("y_g", "y_s", "y_c"):
                    dbg_tap({"y_g": y_g, "y_s": y_s, "y_c": y_c}[dbg], {"y_g": b_yg, "y_s": b_ys, "y_c": b_yc}[dbg])
                norm_to_FM(tiles, l, R_N2)
                if True:
                  if "F" in phases:
                    psb = lambda n_, sh, dt=F32: sb(n_, sh, dt, ph)
                    actT = psb("f_act", [128, 4, TMAX], BF16); b_actT = Buf()
                    rl = psb("f_rl", [128, 512]); b_rl = Buf()
                    for dg in range(8):
                        wu, bwu = load_w([wup_d[l, :, dg * 512:(dg + 1) * 512]], 8, 512)
                        wd, bwd = load_w([wdn_d[l, dg * 512:(dg + 1) * 512, :]], 4, 1024)
                        for c_ in range(4):
                            for (s0, sn) in segs:
                                bi = bank()
                                proj_FM(bi, wu, bwu, c_ * 128, 128, xnT, b_xnT, s0, sn)
                                op("act", lambda e: e.activation(out=rl[:, 0:sn], in_=banks[bi][:, 0:sn], func=AF.Relu), [bbuf[bi]], [b_rl])
                                op("dve", lambda e: e.tensor_tensor(out=actT[:, c_, s0:s0 + sn], in0=rl[:, 0:sn], in1=rl[:, 0:sn], op=ALU.mult), [b_rl], [b_actT])
                        for i, (off, n) in enumerate(tiles):
                            for half in range(2):
                                bi = bank()
                                for c_ in range(4):
                                    mm(bi, banks[bi][:n, 0:512], actT[:, c_, off:off + n], wd[:, c_, half * 512:(half + 1) * 512], c_ == 0, c_ == 3, [b_actT, bwd])
                                op("dve", lambda e: e.tensor_tensor(out=h[:n, i, half * 512:(half + 1) * 512], in0=h[:n, i, half * 512:(half + 1) * 512], in1=banks[bi][:n, 0:512], op=ALU.add), [b_h[i], bbuf[bi]], [b_h[i]])
                    S.barrier()
                    ph.close()

            with ExitStack() as ph:
                ot = sb("f_ot", [128, 2, D], F32, ph)
                b_ot = [Buf(), Buf()]
                k_ = 0
                for i, (off, n) in enumerate(tiles):
                    if s == 0 and i == 0:
                        continue
                    op("act", lambda e: e.activation(out=nscr[:n, :], in_=h[:n, i, :], func=AF.Square, accum_out=nsm[:n, i:i + 1]), [b_h[i]], [b_nscr, b_nsm])
                    rsqrt_inplace(nsm[:n, i:i + 1], b_nsm, 1.0 / D, RMS_EPS)
                    op("dve", lambda e: e.scalar_tensor_tensor(out=ot[:n, k_ % 2, :], in0=h[:n, i, :], scalar=nsm[:n, i:i + 1], in1=fnw[:n, :], op0=ALU.mult, op1=ALU.mult),
                       [b_h[i], b_nsm, b_par], [b_ot[k_ % 2]])
                    r0 = seq0 + off - (NMETA if s == 0 else 0)
                    S.dma([(out_d[r0:r0 + n, :], ot[:n, k_ % 2, :])], reads=[b_ot[k_ % 2]], q="act")
                    k_ += 1
                S.barrier()
        S.final_wait("sp")
        S.final_wait("act")
        print("instructions", S.n_ins, "waits", S.n_wait)
    return nc


def pack_params(inp):
    pfm = np.zeros((2, 256, 128), np.float32)
    ptm = np.zeros((2, PTM_W), np.float32)
    for l in range(2):
        pfm[l, R_N1:R_N1 + 8] = np.asarray(inp["norm1_w"][l]).reshape(8, 128)
        pfm[l, R_N2:R_N2 + 8] = np.asarray(inp["norm2_w"][l]).reshape(8, 128)
        pfm[l, R_GCW:R_GCW + 96] = np.asarray(inp["gdn_conv_w"][l]).reshape(96, 128)
        pfm[l, R_SCW:R_SCW + 64] = np.asarray(inp["ssd_conv_w"][l]).reshape(64, 128)
        pfm[l, R_SCB:R_SCB + 16] = np.asarray(inp["ssd_conv_b"][l]).reshape(16, 128)
        pfm[l, R_GNW] = np.asarray(inp["gdn_norm_w"][l])
        pfm[l, R_SNW:R_SNW + 8] = np.asarray(inp["ssd_norm_w"][l]).reshape(8, 128)
        ptm[l, C_GAL:C_GAL + 8] = np.asarray(inp["gdn_a_log"][l])
        ptm[l, C_GDB:C_GDB + 8] = np.asarray(inp["gdn_dt_bias"][l])
        ptm[l, C_SDB:C_SDB + 16] = np.asarray(inp["ssd_dt_bias"][l])
        ptm[l, C_SAL:C_SAL + 16] = np.asarray(inp["ssd_a_log"][l])
        ptm[l, C_SD:C_SD + 16] = np.asarray(inp["ssd_d"][l])
        ptm[l, C_SNK:C_SNK + 16] = np.asarray(inp["swa_sinks"][l])
    return pfm, ptm


_NC_CACHE = {}


def kernel(**inputs):
    inp = {k: np.asarray(v) for k, v in inputs.items()}
    n = 8
    if "nc" not in _NC_CACHE:
        _NC_CACHE["nc"] = build(NST=8, NL=2, TPS=4)
    nc = _NC_CACHE["nc"]
    pfm, ptm = pack_params(inp)
    f32 = lambda a: np.ascontiguousarray(a, dtype=np.float32)
    shared = dict(meta=f32(inp["meta_tokens"]), w_in=f32(inp["w_in"]), w_pg=f32(inp["w_proj_gdn"]), w_ps=f32(inp["w_proj_ssd"]),
                  w_pc=f32(inp["w_proj_swa"]), w_out=f32(inp["w_out"]), w_up=f32(inp["w_up"]), w_dn=f32(inp["w_down"]),
                  pfm=pfm, ptm=ptm, fnw=f32(inp["final_norm_w"]).reshape(1, -1))
    in_maps = [dict(shared, x=f32(inp["x"][i])) for i in range(n)]
    res = run_bass_kernel_spmd(nc, in_maps, core_ids=list(range(n)))
    return np.stack([np.asarray(r["out"], dtype=np.float32) for r in res.results], axis=0)
```

```python
import numpy as np
from contextlib import ExitStack
import concourse.bass as bass
import concourse.mybir as mybir
from concourse.bass_utils import run_bass_kernel_spmd

F32 = mybir.dt.float32
BF16 = mybir.dt.bfloat16
AF = mybir.ActivationFunctionType
ALU = mybir.AluOpType
AX = mybir.AxisListType

D = 1024
SEQ = 4096
NMETA = 16
DFF = 4096
IN_W = 11808
O_GQ, O_GK, O_GV, O_GG, O_GB, O_GA = 0, 1024, 2048, 3072, 4096, 4104
O_SZ, O_SX, O_SB, O_SC, O_SDT = 4112, 5136, 6160, 6672, 7184
O_CQ, O_CK, O_CV, O_GATE = 7200, 8224, 8480, 8736
RMS_EPS = 1e-6
L2_EPS = 1e-6
R_N1, R_N2, R_GCW, R_SCW, R_SCB, R_GNW, R_SNW = 0, 8, 16, 112, 176, 192, 193
C_GAL, C_GDB, C_SDB, C_SAL, C_SD, C_SNK, PTM_W = 0, 8, 16, 32, 48, 64, 80


class Buf:
    __slots__ = ("name", "lw", "rd")

    def __init__(self, name=""):
        self.name = name
        self.lw = None
        self.rd = {}


class Sched:
    ENG = ("pe", "act", "dve", "pool", "sp")
    EPOCH = 30000

    def __init__(self, nc, stack, n_dma_sems=24):
        self.nc = nc
        self.stack = stack
        self.eng = {"pe": nc.tensor, "act": nc.scalar, "dve": nc.vector,
                    "pool": nc.gpsimd, "sp": nc.sync}
        self.semh = {}
        self.cnt = {}
        self.epoch = {}
        for e in self.ENG:
            self.epoch[e] = 0
            self.cnt[e] = 0
            self.semh[(e, 0)] = stack.enter_context(nc.semaphore(f"s_{e}_0"))
        self.waited = {e: {} for e in self.ENG}
        self.ndma = n_dma_sems
        self.dma_tot = [0] * n_dma_sems
        for j in range(n_dma_sems):
            self.semh[("d", j)] = stack.enter_context(nc.semaphore(f"s_dma_{j}"))
        self.dma_next = 0
        self.n_ins = 0
        self.n_wait = 0

    def _wait(self, e, tok):
        key, val = tok
        if self.waited[e].get(key, 0) >= val:
            return
        if key[0] == e and e == "pe":
            return
        self.eng[e].wait_ge(self.semh[key], val)
        self.waited[e][key] = val
        self.n_wait += 1

    def _deps(self, reads, writes):
        deps = {}

        def add(k, v):
            if deps.get(k, 0) < v:
                deps[k] = v
        for b in reads:
            if b.lw is not None:
                add(*b.lw)
        for b in writes:
            if b.lw is not None:
                add(*b.lw)
            for k, v in b.rd.items():
                add(k, v)
        return deps

    def _mark(self, tok, reads, writes):
        k, v = tok
        for b in reads:
            if b.rd.get(k, 0) < v:
                b.rd[k] = v
        for b in writes:
            b.lw = tok
            b.rd = {}

    def op(self, e, fn, reads=(), writes=()):
        deps = self._deps(reads, writes)
        for k, v in deps.items():
            self._wait(e, (k, v))
        if self.cnt[e] >= self.EPOCH:
            self.epoch[e] += 1
            self.cnt[e] = 0
            self.semh[(e, self.epoch[e])] = self.stack.enter_context(
                self.nc.semaphore(f"s_{e}_{self.epoch[e]}"))
        ins = fn(self.eng[e])
        self.cnt[e] += 1
        key = (e, self.epoch[e])
        ins.then_inc(self.semh[key], 1)
        tok = (key, self.cnt[e])
        self._mark(tok, reads, writes)
        self.n_ins += 1
        return tok

    def dma(self, pairs, reads=(), writes=(), q="sp"):
        j = self.dma_next
        self.dma_next = (self.dma_next + 1) % self.ndma
        key = ("d", j)
        if self.dma_tot[j] > 0:
            self._wait(q, (key, self.dma_tot[j]))
        deps = self._deps(reads, writes)
        for k, v in deps.items():
            self._wait(q, (k, v))
        for (o, i) in pairs:
            self.eng[q].dma_start(out=o, in_=i).then_inc(self.semh[key], 16)
            self.dma_tot[j] += 16
            self.n_ins += 1
        tok = (key, self.dma_tot[j])
        self._mark(tok, reads, writes)
        return tok

    def all_tokens(self):
        toks = []
        for e in self.ENG:
            if self.cnt[e] > 0:
                toks.append(((e, self.epoch[e]), self.cnt[e]))
            elif self.epoch[e] > 0:
                toks.append(((e, self.epoch[e] - 1), self.EPOCH))
        toks += [(("d", j), self.dma_tot[j]) for j in range(self.ndma) if self.dma_tot[j] > 0]
        return toks

    def barrier(self):
        toks = self.all_tokens()
        for e in self.ENG:
            if e in ("sp", "pool"):
                continue
            for t in toks:
                self._wait(e, t)

    def final_wait(self, e="sp"):
        for t in self.all_tokens():
            self._wait(e, t)


def build(NST=8, NL=2, TPS=4, dbg=False, phases="GSCMF"):
    TS = TPS * 128
    TMAX = TS + NMETA
    NTILE = TPS + 1
    nc = bass.Bass("TRN2", target_bir_lowering=False)
    dram = lambda n, s, k="ExternalInput": nc.dram_tensor(n, s, F32, kind=k).ap()
    x_d = dram("x", [SEQ, D])
    meta_d = dram("meta", [NMETA, D])
    win_d = dram("w_in", [2, D, IN_W])
    wpg_d = dram("w_pg", [2, D, D])
    wps_d = dram("w_ps", [2, D, D])
    wpc_d = dram("w_pc", [2, D, D])
    wout_d = dram("w_out", [2, D, D])
    wup_d = dram("w_up", [2, D, DFF])
    wdn_d = dram("w_dn", [2, DFF, D])
    pfm_d = dram("pfm", [2, 256, 128])
    ptm_d = dram("ptm", [2, PTM_W])
    fnw_d = dram("fnw", [1, D])
    out_d = dram("out", [NST * TS, D], "ExternalOutput")
    dbg_d = dram("dbg", [128, 8 * 528], "ExternalOutput") if dbg else None

    with ExitStack() as st:
        S = Sched(nc, st)
        sbytes = [0]

        def sb(name, shape, dt=F32, stack=None):
            sbytes[0] += 1
            t = (stack or st).enter_context(nc.sbuf_tensor(f"sb{sbytes[0]}_{name}", shape, dt))
            return t

        def op(e, fn, r=(), w=()):
            w = list(w) + [b for b in r if b.name.startswith("bank") and b not in w]
            return S.op(e, fn, reads=r, writes=w)

        banks = [st.enter_context(nc.psum_tensor(f"bank{i}", [128, 512], F32)) for i in range(8)]
        bbuf = [Buf(f"bank{i}") for i in range(8)]
        reserved = [False] * 8
        bank_rr = [0]

        def bank(reserve=False):
            for _ in range(8):
                i = bank_rr[0]
                bank_rr[0] = (i + 1) % 8
                if not reserved[i]:
                    if reserve:
                        reserved[i] = True
                    return i
            raise RuntimeError("no psum bank")

        def mm(bi, out_ap, lhsT, rhs, start, stop, r):
            op("pe", lambda e: e.matmul(out_ap, lhsT=lhsT, rhs=rhs, start=start, stop=stop), r, [bbuf[bi]])

        def tr(bi, out_ap, in_ap, kparts, r):
            op("pe", lambda e: e.transpose(out=out_ap, in_=in_ap, identity=ident[:kparts, :kparts]),
               list(r) + [b_const], [bbuf[bi]])

        ident = sb("ident", [128, 128])
        Ui = sb("Ui", [128, 128])
        Ls = sb("Ls", [128, 128])
        nUs = sb("nUs", [128, 128])
        ones = sb("ones", [128, 128])
        b_const = Buf("const")
        for t_, val in ((ident, 1.0), (Ui, 1.0), (Ls, 1.0), (nUs, -1.0), (ones, 1.0)):
            op("pool", lambda e, t_=t_, val=val: e.memset(t_[:], val), [], [b_const])
        sel = lambda t_, pat, cm, cmp: op("pool", lambda e: e.affine_select(
            out=t_[:], in_=t_[:], pattern=[[pat, 128]], compare_op=cmp, fill=0.0, base=0, channel_multiplier=cm),
            [b_const], [b_const])
        sel(ident, -1, 1, ALU.is_equal)
        sel(Ui, 1, -1, ALU.is_ge)
        sel(Ls, -1, 1, ALU.is_gt)
        sel(nUs, 1, -1, ALU.is_gt)

        pfmT = sb("pfmT", [128, 2, 256])
        ptm = sb("ptm", [128, 2, PTM_W])
        negA_g = sb("negA_g", [128, 2, 8])
        A_s = sb("A_s", [128, 2, 16])
        esink = sb("esink", [128, 2, 16])
        b_par = Buf("par")
        ptmp = sb("ptmp", [128, 2, 128])
        b_ptmp = Buf("ptmp")
        for l in range(2):
            S.dma([(ptmp[:, 0, :], pfm_d[l, 0:128, :]), (ptmp[:, 1, :], pfm_d[l, 128:256, :])],
                  writes=[b_ptmp], q="act")
            bi = bank()
            for hlf in range(2):
                tr(bi, banks[bi][:, hlf * 128:(hlf + 1) * 128], ptmp[:, hlf, :], 128, [b_ptmp])
            op("dve", lambda e: e.tensor_copy(out=pfmT[:, l, :], in_=banks[bi][:, 0:256]), [bbuf[bi]], [b_par])
            S.dma([(ptm[:, l, :], ptm_d[l:l + 1, :].partition_broadcast(128))], writes=[b_par], q="act")
        for l in range(2):
            op("act", lambda e: e.activation(out=negA_g[:, l, :], in_=ptm[:, l, C_GAL:C_GAL + 8], func=AF.Exp), [b_par], [b_par])
            op("act", lambda e: e.activation(out=A_s[:, l, :], in_=ptm[:, l, C_SAL:C_SAL + 16], func=AF.Exp), [b_par], [b_par])
            op("act", lambda e: e.activation(out=esink[:, l, :], in_=ptm[:, l, C_SNK:C_SNK + 16], func=AF.Exp), [b_par], [b_par])
            op("dve", lambda e: e.tensor_scalar(out=negA_g[:, l, :], in0=negA_g[:, l, :], scalar1=-1.0, scalar2=None, op0=ALU.mult), [b_par], [b_par])
            op("dve", lambda e: e.tensor_scalar(out=A_s[:, l, :], in0=A_s[:, l, :], scalar1=-1.0, scalar2=None, op0=ALU.mult), [b_par], [b_par])
        pcol = lambda l, row: pfmT[:, l, row:row + 1]

        h = sb("h", [128, NTILE, D])
        b_h = [Buf(f"h{i}") for i in range(NTILE)]
        xnT = sb("xnT", [128, 8, TMAX], BF16)
        b_xnT = Buf("xnT")
        y_g = sb("y_g", [128, 8, TMAX], BF16)
        y_s = sb("y_s", [128, 8, TMAX], BF16)
        y_c = sb("y_c", [128, 8, TMAX], BF16)
        b_yg, b_ys, b_yc = Buf("yg"), Buf("ys"), Buf("yc")
        Sg = sb("Sg", [128, 2, 8, 128])
        b_Sg = [[Buf() for _ in range(8)] for _ in range(2)]
        Hs = sb("Hs", [128, 2, 4, 256])
        b_Hs = [[Buf() for _ in range(4)] for _ in range(2)]
        halo_g = sb("halo_g", [128, 2, 24, 3])
        halo_s = sb("halo_s", [128, 2, 16, 3])
        b_halo = Buf("halo")
        KW = NMETA + 128 + TS
        kTc = sb("kTc", [64, 2, 4, KW], BF16)
        b_kT = [Buf() for _ in range(2)]
        vA = sb("vA", [128, 2, 2 + TPS, 4, 65], BF16)
        b_vA = [Buf() for _ in range(2)]
        for t_ in (Sg, Hs, halo_g, halo_s):
            op("pool", lambda e, t_=t_: e.memset(t_[:], 0.0), [], [b_halo])
        op("pool", lambda e: e.memset(kTc[:], 0.0), [], [b_kT[0], b_kT[1]])
        op("pool", lambda e: e.memset(vA[:], 1.0), [], [b_vA[0], b_vA[1]])
        for l in range(2):
            for hh in range(8):
                b_Sg[l][hh].lw = b_halo.lw
            for g_ in range(4):
                b_Hs[l][g_].lw = b_halo.lw

        NSTG, NWB = 2, 3
        stg = [sb(f"stg{i}", [128, 2048]) for i in range(NSTG)]
        b_stg = [Buf() for _ in range(NSTG)]
        wbf = [sb(f"wbf{i}", [128, 4096], BF16) for i in range(NWB)]
        b_wbf = [Buf() for _ in range(NWB)]
        wrr = [0, 0]

        def issue_w(parts, kc, cols):
            wi = wrr[1]
            wrr[1] = (wi + 1) % NWB
            wv = wbf[wi][:, 0:kc * cols].rearrange("p (k c) -> p k c", k=kc)
            nsplit = 2 if kc * cols > 2048 else 1
            assert kc % nsplit == 0 and kc * cols // nsplit <= 2048
            kh = kc // nsplit
            for hf in range(nsplit):
                si = wrr[0]
                wrr[0] = (si + 1) % NSTG
                sv = stg[si][:, 0:kh * cols].rearrange("p (k c) -> p k c", k=kh)
                pairs = []
                c0 = 0
                for d_ap in parts:
                    c = d_ap.shape[1]
                    pairs.append((sv[:, :, c0:c0 + c], d_ap[hf * kh * 128:(hf + 1) * kh * 128, :].rearrange("(k p) c -> p k c", p=128)))
                    c0 += c
                assert c0 == cols
                S.dma(pairs, writes=[b_stg[si]], q="sp")
                if hf == 0:
                    op("pool", lambda e: e.tensor_copy(out=wv[:, hf * kh:(hf + 1) * kh, :], in_=sv), [b_stg[si]], [b_wbf[wi]])
                else:
                    op("act", lambda e: e.activation(out=wv[:, hf * kh:(hf + 1) * kh, :], in_=sv, func=AF.Copy), [b_stg[si]], [b_wbf[wi]])
            return wv, b_wbf[wi]

        def layer_specs(l):
            sp = []
            if "G" in phases:
                sp.append(([win_d[l, :, O_GB:O_GB + 16]], 8, 16))
                for hh in range(8):
                    sp.append(([win_d[l, :, O_GQ + hh * 128:O_GQ + (hh + 1) * 128], win_d[l, :, O_GK + hh * 128:O_GK + (hh + 1) * 128],
                                win_d[l, :, O_GV + hh * 128:O_GV + (hh + 1) * 128], win_d[l, :, O_GG + hh * 128:O_GG + (hh + 1) * 128]], 8, 512))
            if "S" in phases:
                sp.append(([win_d[l, :, O_SDT:O_SDT + 16]], 8, 16))
                for gi in range(4):
                    sp.append(([win_d[l, :, O_SX + gi * 256:O_SX + (gi + 1) * 256], win_d[l, :, O_SB + gi * 128:O_SB + (gi + 1) * 128],
                                win_d[l, :, O_SC + gi * 128:O_SC + (gi + 1) * 128]], 8, 512))
                    sp.append(([win_d[l, :, O_SZ + gi * 256:O_SZ + (gi + 1) * 256]], 8, 256))
            if "C" in phases:
                sp.append(([win_d[l, :, O_CK:O_CK + 256], win_d[l, :, O_CV:O_CV + 256]], 8, 512))
                for hk in range(4):
                    sp.append(([win_d[l, :, O_CQ + hk * 256:O_CQ + (hk + 1) * 256]], 8, 256))
            if "M" in phases:
                wps_ = (wpg_d, wps_d, wpc_d)
                for fc in range(8):
                    sp.append(([win_d[l, :, O_GATE + br * 1024 + fc * 128:O_GATE + br * 1024 + (fc + 1) * 128] for br in range(3)], 8, 384))
                    sp.append(([wps_[br][l, :, fc * 128:(fc + 1) * 128] for br in range(3)], 8, 384))
                for half in range(2):
                    sp.append(([wout_d[l, :, half * 512:(half + 1) * 512]], 8, 512))
            if "F" in phases:
                for dg in range(8):
                    sp.append(([wup_d[l, :, dg * 512:(dg + 1) * 512]], 8, 512))
                    sp.append(([wdn_d[l, dg * 512:(dg + 1) * 512, :]], 4, 1024))
            return sp

        all_specs = [sp_ for _s in range(NST) for l_ in range(NL) for sp_ in layer_specs(l_)]
        wqs = {"i": 0, "pend": None}

        def load_w(parts, kc, cols):
            i = wqs["i"]
            if wqs["pend"] is None:
                wqs["pend"] = issue_w(*all_specs[i])
            spec = all_specs[i]
            assert spec[1] == kc and spec[2] == cols and len(spec[0]) == len(parts), (i, spec[1:], kc, cols)
            cur = wqs["pend"]
            wqs["i"] = i + 1
            wqs["pend"] = issue_w(*all_specs[i + 1]) if i + 1 < len(all_specs) else None
            return cur

        def softplus_inplace(x_ap, t_ap, bx, bt):
            op("act", lambda e: e.activation(out=t_ap, in_=x_ap, func=AF.Abs), [bx], [bt])
            op("act", lambda e: e.activation(out=t_ap, in_=t_ap, func=AF.Exp, scale=-1.0), [bt], [bt])
            op("act", lambda e: e.activation(out=t_ap, in_=t_ap, func=AF.Ln, bias=1.0, scale=1.0), [bt], [bt])
            op("dve", lambda e: e.scalar_tensor_tensor(out=x_ap, in0=x_ap, scalar=0.0, in1=t_ap, op0=ALU.max, op1=ALU.add), [bx, bt], [bx])

        def rsqrt_inplace(x_ap, bx, scale, eps):
            op("act", lambda e: e.activation(out=x_ap, in_=x_ap, func=AF.Sqrt, bias=eps, scale=scale), [bx], [bx])
            op("dve", lambda e: e.reciprocal(out=x_ap, in_=x_ap), [bx], [bx])

        nscr = sb("nscr", [128, D])
        b_nscr = Buf("nscr")
        nsm = sb("nsm", [128, NTILE])
        b_nsm = Buf("nsm")

        def norm_to_FM(tiles, l, row):
            for i, (off, n) in enumerate(tiles):
                op("act", lambda e: e.activation(out=nscr[:n, :], in_=h[:n, i, :], func=AF.Square, accum_out=nsm[:n, i:i + 1]),
                   [b_h[i]], [b_nscr, b_nsm])
                rsqrt_inplace(nsm[:n, i:i + 1], b_nsm, 1.0 / D, RMS_EPS)
                op("dve", lambda e: e.tensor_scalar(out=nscr[:n, :], in0=h[:n, i, :], scalar1=nsm[:n, i:i + 1], scalar2=None, op0=ALU.mult),
                   [b_h[i], b_nsm], [b_nscr])
                for half in range(2):
                    bi = bank()
                    pv = banks[bi][:, :].rearrange("p (c t) -> p c t", c=4)
                    for c in range(4):
                        cc = half * 4 + c
                        tr(bi, pv[:, c, 0:n], nscr[:n, cc * 128:(cc + 1) * 128], n, [b_nscr])
                    op("dve", lambda e: e.tensor_tensor(
                        out=xnT[:, half * 4:half * 4 + 4, off:off + n], in0=pv[:, :, 0:n],
                        in1=pfmT[:, l, row + half * 4:row + half * 4 + 4].unsqueeze(2).to_broadcast([128, 4, n]), op=ALU.mult),
                        [bbuf[bi], b_par], [b_xnT])

        def proj_FM(bi, wv, bw, c0, ncol, src, bsrc, s0, sn, kcs=8):
            for kc in range(kcs):
                mm(bi, banks[bi][:ncol, 0:sn], wv[:, kc, c0:c0 + ncol], src[:, kc, s0:s0 + sn], kc == 0, kc == kcs - 1, [bw, bsrc])

        def dbg_tap(src, bsrc):
            with ExitStack() as ph2:
                dbg_copy = sb("dbgc", [128, 8, TMAX], F32, ph2)
                b_dbg = Buf()
                op("dve", lambda e: e.tensor_copy(out=dbg_copy[:, :, :], in_=src[:, :, :]), [bsrc], [b_dbg])
                S.dma([(dbg_d.rearrange("p (c t) -> p c t", c=8), dbg_copy[:, :, :])], reads=[b_dbg], q="act")
                S.barrier()

        for s in range(NST):
            if s == 0:
                tiles = [(0, NMETA)] + [(NMETA + 128 * i, 128) for i in range(TPS)]
                segs = [(0, NMETA), (NMETA, TS)]
                T = TMAX
            else:
                tiles = [(128 * i, 128) for i in range(TPS)]
                segs = [(0, TS)]
                T = TS
            seq0 = s * TS
            if s == 0:
                cgroups = [([0], NMETA)] + [([NMETA + 64 * j for j in range(4 * g_, 4 * g_ + 4)], 64) for g_ in range(2 * TPS // 4)]
            else:
                cgroups = [([64 * j for j in range(4 * g_, 4 * g_ + 4)], 64) for g_ in range(2 * TPS // 4)]
            nchunk = sum(len(g[0]) for g in cgroups)
            pairs = []
            wl = []
            for i, (off, n) in enumerate(tiles):
                if s == 0 and i == 0:
                    pairs.append((h[:n, i, :], meta_d[:, :]))
                else:
                    r0 = seq0 + off - (NMETA if s == 0 else 0)
                    pairs.append((h[:n, i, :], x_d[r0:r0 + n, :]))
                wl.append(b_h[i])
            S.dma(pairs, writes=wl, q="act")

            for l in range(NL):
                norm_to_FM(tiles, l, R_N1)
                with ExitStack() as ph:
                  if "G" in phases:
                    psb = lambda n_, sh, dt=F32: sb(n_, sh, dt, ph)
                    NCH = 2 * TPS + 1
                    ba = psb("g_ba", [64, NCH, 16]); b_ba = Buf()
                    tsm = psb("g_tsm", [64, NCH, 8]); b_tsm = Buf()
                    beta = psb("g_beta", [64, NCH, 8]); gsm = psb("g_gsm", [64, NCH, 8])
                    bk = psb("g_bk", [64, NCH, 8]); etail = psb("g_etail", [64, NCH, 8])
                    eglast = psb("g_eglast", [128, NCH, 8]); b_sm = Buf()
                    wv, bw = load_w([win_d[l, :, O_GB:O_GB + 16]], 8, 16)
                    ci = 0
                    cinfo = []
                    for offs, cs in cgroups:
                        bi = bank()
                        for j, off in enumerate(offs):
                            for kc in range(8):
                                mm(bi, banks[bi][:cs, j * 16:(j + 1) * 16], xnT[:, kc, off:off + cs], wv[:, kc, 0:16], kc == 0, kc == 7, [b_xnT, bw])
                            cinfo.append((ci + j, off, cs))
                        nj = len(offs)
                        op("dve", lambda e: e.tensor_copy(out=ba[:cs, ci:ci + nj, :], in_=banks[bi][:cs, 0:nj * 16].rearrange("p (j c) -> p j c", c=16)), [bbuf[bi]], [b_ba])
                        ci += nj
                    assert ci == nchunk
                    NC_ = nchunk
                    op("act", lambda e: e.activation(out=beta[:, 0:NC_, :], in_=ba[:, 0:NC_, 0:8], func=AF.Sigmoid), [b_ba], [b_sm])
                    op("dve", lambda e: e.tensor_tensor(out=gsm[:, 0:NC_, :], in0=ba[:, 0:NC_, 8:16], in1=ptm[:64, l, C_GDB:C_GDB + 8].unsqueeze(1).to_broadcast([64, NC_, 8]), op=ALU.add), [b_ba, b_par], [b_sm])
                    softplus_inplace(gsm[:, 0:NC_, :].rearrange("p j c -> p (j c)"), tsm[:, 0:NC_, :].rearrange("p j c -> p (j c)"), b_sm, b_tsm)
                    op("dve", lambda e: e.tensor_tensor(out=gsm[:, 0:NC_, :], in0=gsm[:, 0:NC_, :], in1=negA_g[:64, l, :].unsqueeze(1).to_broadcast([64, NC_, 8]), op=ALU.mult), [b_sm, b_par], [b_sm])
                    ci = 0
                    for offs, cs in cgroups:
                        nj = len(offs)
                        rhs = gsm[:cs, ci:ci + nj, :].rearrange("p j c -> p (j c)")
                        bi = bank()
                        mm(bi, banks[bi][:cs, 0:nj * 8], Ui[:cs, :cs], rhs, True, True, [b_sm, b_const])
                        mm(bi, banks[bi][:, 128:128 + nj * 8], ones[:cs, :], rhs, True, True, [b_sm, b_const])
                        gam_v = banks[bi][:cs, 0:nj * 8].rearrange("p (j c) -> p j c", c=8)
                        gl_v = banks[bi][:, 128:128 + nj * 8].rearrange("p (j c) -> p j c", c=8)
                        op("act", lambda e: e.activation(out=bk[:cs, ci:ci + nj, :], in_=gam_v, func=AF.Exp), [bbuf[bi]], [b_sm])
                        op("dve", lambda e: e.tensor_tensor(out=bk[:cs, ci:ci + nj, :], in0=bk[:cs, ci:ci + nj, :], in1=beta[:cs, ci:ci + nj, :], op=ALU.mult), [b_sm], [b_sm])
                        op("act", lambda e: e.activation(out=eglast[:, ci:ci + nj, :], in_=gl_v, func=AF.Exp), [bbuf[bi]], [b_sm])
                        op("act", lambda e: e.activation(out=tsm[:cs, ci:ci + nj, :], in_=gl_v[:cs], func=AF.Copy), [bbuf[bi]], [b_tsm])
                        op("dve", lambda e: e.tensor_tensor(out=etail[:cs, ci:ci + nj, :], in0=tsm[:cs, ci:ci + nj, :], in1=gam_v, op=ALU.subtract), [b_tsm, bbuf[bi]], [b_sm])
                        op("act", lambda e: e.activation(out=etail[:cs, ci:ci + nj, :], in_=etail[:cs, ci:ci + nj, :], func=AF.Exp), [b_sm], [b_sm])
                        ci += nj
                    GST = 99
                    xq = psb("g_xq", [128, 3, TMAX + 3]); b_xq = Buf()
                    cq = psb("g_cq", [128, 3, TMAX]); b_cq = Buf()
                    sgt = psb("g_sgt", [128, TMAX]); b_sgt = Buf()
                    sq = psb("g_sq", [128, TMAX]); b_sq = Buf()
                    rin = psb("g_rin", [128, TMAX]); b_rin = Buf()
                    egb = psb("g_egb", [128, TMAX]); b_egb = Buf()
                    kTb = psb("g_kTb", [128, TMAX], BF16); qTb = psb("g_qTb", [128, TMAX], BF16); qdb = psb("g_qdb", [128, TMAX], BF16); b_qk = Buf()
                    m64 = [psb(f"g_m{i}", [64, NCH, 64]) for i in range(9)]
                    b_m = [Buf() for _ in range(9)]
                    E_, DT_, MB_, Bm, BT_, P_, PT_, M_, M2_ = m64
                    bE, bDT, bMB, bBm, bBT, bP, bPT, bM, bM2 = b_m
                    Rk = psb("g_Rk", [64, NCH, 128], BF16); Rv = psb("g_Rv", [64, NCH, 128], BF16); ktl = psb("g_ktl", [64, NCH, 128], BF16)
                    TTb = psb("g_TTb", [64, NCH, 64], BF16); aTb = psb("g_aTb", [64, NCH, 64], BF16)
                    nWT = psb("g_nWT", [128, NCH, 64], BF16)
                    oT = psb("g_oT", [128, TMAX]); b_oT = Buf()
                    vnb = psb("g_vnb", [64, 128], BF16); b_vnb = Buf()
                    Sb = psb("g_Sb", [128, 128], BF16); b_Sb = Buf()
                    for hh in range(8 if GST > 0 else 0):
                        wv, bw = load_w([win_d[l, :, O_GQ + hh * 128:O_GQ + (hh + 1) * 128], win_d[l, :, O_GK + hh * 128:O_GK + (hh + 1) * 128],
                                         win_d[l, :, O_GV + hh * 128:O_GV + (hh + 1) * 128], win_d[l, :, O_GG + hh * 128:O_GG + (hh + 1) * 128]], 8, 512)
                        for qi in range(3):
                            op("dve", lambda e: e.tensor_copy(out=xq[:, qi, 0:3], in_=halo_g[:, l, qi * 8 + hh, :]), [b_halo], [b_xq])
                        for qi in range(4):
                            for (s0, sn) in segs:
                                bi = bank()
                                proj_FM(bi, wv, bw, qi * 128, 128, xnT, b_xnT, s0, sn)
                                if qi < 3:
                                    op("act", lambda e: e.activation(out=xq[:, qi, 3 + s0:3 + s0 + sn], in_=banks[bi][:, 0:sn], func=AF.Copy), [bbuf[bi]], [b_xq])
                                else:
                                    op("act", lambda e: e.activation(out=sgt[:, s0:s0 + sn], in_=banks[bi][:, 0:sn], func=AF.Silu), [bbuf[bi]], [b_sgt])
                        for qi in range(3):
                            chn = qi * 8 + hh
                            cw = lambda tap: pcol(l, R_GCW + tap * 24 + chn)
                            op("dve", lambda e: e.tensor_scalar(out=cq[:, qi, 0:T], in0=xq[:, qi, 3:3 + T], scalar1=cw(3), scalar2=None, op0=ALU.mult), [b_xq, b_par], [b_cq])
                            for tap in range(3):
                                op("dve", lambda e: e.scalar_tensor_tensor(out=cq[:, qi, 0:T], in0=xq[:, qi, tap:tap + T], scalar=cw(tap), in1=cq[:, qi, 0:T], op0=ALU.mult, op1=ALU.add), [b_xq, b_par, b_cq], [b_cq])
                            op("dve", lambda e: e.tensor_copy(out=halo_g[:, l, chn, :], in_=xq[:, qi, T:T + 3]), [b_xq], [b_halo])
                            op("act", lambda e: e.activation(out=cq[:, qi, 0:T], in_=cq[:, qi, 0:T], func=AF.Silu), [b_cq], [b_cq])
                        for qi in range(2):
                            op("dve", lambda e: e.tensor_tensor(out=sq[:, 0:T], in0=cq[:, qi, 0:T], in1=cq[:, qi, 0:T], op=ALU.mult), [b_cq], [b_sq])
                            for (s0, sn) in segs:
                                bi = bank()
                                mm(bi, banks[bi][:, 0:sn], ones[:, :], sq[:, s0:s0 + sn], True, True, [b_sq, b_const])
                                op("act", lambda e: e.activation(out=rin[:, s0:s0 + sn], in_=banks[bi][:, 0:sn], func=AF.Sqrt, bias=L2_EPS, scale=1.0), [bbuf[bi]], [b_rin])
                            op("dve", lambda e: e.reciprocal(out=rin[:, 0:T], in_=rin[:, 0:T]), [b_rin], [b_rin])
                            if qi == 0:
                                op("dve", lambda e: e.scalar_tensor_tensor(out=cq[:, 0, 0:T], in0=cq[:, 0, 0:T], scalar=128.0 ** -0.5, in1=rin[:, 0:T], op0=ALU.mult, op1=ALU.mult), [b_cq, b_rin], [b_cq])
                            else:
                                op("dve", lambda e: e.tensor_tensor(out=cq[:, 1, 0:T], in0=cq[:, 1, 0:T], in1=rin[:, 0:T], op=ALU.mult), [b_cq, b_rin], [b_cq])
                        op("act", lambda e: e.activation(out=kTb[:, 0:T], in_=cq[:, 1, 0:T], func=AF.Copy), [b_cq], [b_qk])
                        op("act", lambda e: e.activation(out=qTb[:, 0:T], in_=cq[:, 0, 0:T], func=AF.Copy), [b_cq], [b_qk])
                        def pre_gen(offs, cs, ci, G):
                            nj = len(offs)
                            gs0 = offs[0]
                            gl_ = nj * cs
                            v3 = lambda t_: t_[:cs, ci:ci + nj, 0:cs]
                            Uib = Ui[:cs, :cs].unsqueeze(1).to_broadcast([cs, nj, cs])
                            p3 = lambda b_: banks[b_][:cs, 0:nj * cs].rearrange("p (j c) -> p j c", c=cs)
                            op("dve", lambda e: e.tensor_tensor(out=v3(DT_), in0=gsm[:cs, ci:ci + nj, hh].unsqueeze(2).to_broadcast([cs, nj, cs]), in1=Uib, op=ALU.mult), [b_sm, b_const], [G["DT"]])
                            b1 = bank()
                            for j in range(nj):
                                mm(b1, p3(b1)[:, j, :], Ls[:cs, :cs], DT_[:cs, ci + j, 0:cs], True, True, [G["DT"], b_const])
                            op("act", lambda e: e.activation(out=v3(E_), in_=p3(b1), func=AF.Exp), [bbuf[b1]], [G["E"]])
                            yield
                            op("dve", lambda e: e.tensor_tensor(out=v3(MB_), in0=beta[:cs, ci:ci + nj, hh].unsqueeze(2).to_broadcast([cs, nj, cs]), in1=ident[:cs, :cs].unsqueeze(1).to_broadcast([cs, nj, cs]), op=ALU.mult), [b_sm, b_const], [G["MB"]])
                            b2 = bank()
                            for j in range(nj):
                                mm(b2, p3(b2)[:, j, :], ones[:cs, :cs], MB_[:cs, ci + j, 0:cs], True, True, [G["MB"], b_const])
                            op("dve", lambda e: e.tensor_tensor(out=v3(MB_), in0=v3(E_), in1=p3(b2), op=ALU.mult), [G["E"], bbuf[b2]], [G["MB"]])
                            op("dve", lambda e: e.tensor_tensor(out=v3(MB_), in0=v3(MB_), in1=nUs[:cs, :cs].unsqueeze(1).to_broadcast([cs, nj, cs]), op=ALU.mult), [G["MB"], b_const], [G["MB"]])
                            op("dve", lambda e: e.tensor_tensor(out=v3(DT_), in0=v3(E_), in1=Uib, op=ALU.mult), [G["E"], b_const], [G["DT"]])
                            yield
                            b1 = bank()
                            for j in range(nj):
                                o_ = offs[j]
                                mm(b1, p3(b1)[:, j, :], kTb[:, o_:o_ + cs], kTb[:, o_:o_ + cs], True, True, [b_qk])
                            op("dve", lambda e: e.tensor_tensor(out=v3(Bm), in0=v3(MB_), in1=p3(b1), op=ALU.mult), [G["MB"], bbuf[b1]], [G["Bm"]])
                            yield
                            b1 = bank()
                            for j in range(nj):
                                tr(b1, p3(b1)[:, j, :], Bm[:cs, ci + j, 0:cs], cs, [G["Bm"]])
                            op("act", lambda e: e.activation(out=v3(BT_), in_=p3(b1), func=AF.Copy), [bbuf[b1]], [G["BT"]])
                            op("dve", lambda e: e.tensor_tensor(out=v3(M_), in0=v3(Bm), in1=ident[:cs, :cs].unsqueeze(1).to_broadcast([cs, nj, cs]), op=ALU.add), [G["Bm"], b_const], [G["M"]])
                            yield
                            b3 = bank()
                            for j in range(nj):
                                mm(b3, banks[b3][:, j * cs:(j + 1) * cs], gsm[:cs, ci + j, hh:hh + 1].to_broadcast([cs, 128]), Ui[:cs, :cs], True, True, [b_sm, b_const])
                            op("act", lambda e: e.activation(out=egb[:, gs0:gs0 + gl_], in_=banks[b3][:, 0:gl_], func=AF.Exp), [bbuf[b3]], [G["egb"]])
                            op("dve", lambda e: e.tensor_tensor(out=qdb[:, gs0:gs0 + gl_], in0=cq[:, 0, gs0:gs0 + gl_], in1=egb[:, gs0:gs0 + gl_], op=ALU.mult), [b_cq, G["egb"]], [G["qdb"]])
                            nlev = 5 if cs == 64 else 3
                            Pc, PTc, bPc, bPTc = Bm, BT_, G["Bm"], G["BT"]
                            Pn, PTn, bPn, bPTn = P_, PT_, G["P"], G["PT"]
                            Mc, Mn, bMc, bMn = M_, M2_, G["M"], G["M2"]
                            def side_tr():
                                for j0 in range(0, nj, 4):
                                    jn = min(4, nj - j0)
                                    bi = bank()
                                    pk = banks[bi][:cs, 0:jn * 128].rearrange("p (j c) -> p j c", c=128)
                                    for j in range(jn):
                                        tr(bi, pk[:, j, :], cq[:, 1, offs[j0 + j]:offs[j0 + j] + cs], 128, [b_cq])
                                    bcs = lambda t_: t_[:cs, ci + j0:ci + j0 + jn, hh].unsqueeze(2).to_broadcast([cs, jn, 128])
                                    op("dve", lambda e: e.tensor_tensor(out=Rk[:cs, ci + j0:ci + j0 + jn, :], in0=pk, in1=bcs(bk), op=ALU.mult), [bbuf[bi], b_sm], [G["R"]])
                                    op("dve", lambda e: e.tensor_tensor(out=ktl[:cs, ci + j0:ci + j0 + jn, :], in0=pk, in1=bcs(etail), op=ALU.mult), [bbuf[bi], b_sm], [G["R"]])
                                    yield
                                    bi = bank()
                                    pk2 = banks[bi][:cs, 0:jn * 128].rearrange("p (j c) -> p j c", c=128)
                                    for j in range(jn):
                                        tr(bi, pk2[:, j, :], cq[:, 2, offs[j0 + j]:offs[j0 + j] + cs], 128, [b_cq])
                                    op("dve", lambda e: e.tensor_tensor(out=Rv[:cs, ci + j0:ci + j0 + jn, :], in0=pk2, in1=bcs(beta), op=ALU.mult), [bbuf[bi], b_sm], [G["R"]])
                                    yield
                                b1_ = bank()
                                for j in range(nj):
                                    o_ = offs[j]
                                    mm(b1_, p3(b1_)[:, j, :], kTb[:, o_:o_ + cs], qTb[:, o_:o_ + cs], True, True, [b_qk])
                                op("dve", lambda e: e.tensor_tensor(out=v3(aTb), in0=v3(DT_), in1=p3(b1_), op=ALU.mult), [G["DT"], bbuf[b1_]], [G["aT"]])
                                yield
                            side = side_tr()
                            for lev in range(nlev):
                                last = lev == nlev - 1
                                b2 = bank()
                                for j in range(nj):
                                    mm(b2, p3(b2)[:, j, :], Pc[:cs, ci + j, 0:cs], PTc[:cs, ci + j, 0:cs], True, True, [bPc, bPTc])
                                if not last:
                                    b1 = bank()
                                    for j in range(nj):
                                        mm(b1, p3(b1)[:, j, :], PTc[:cs, ci + j, 0:cs], Pc[:cs, ci + j, 0:cs], True, True, [bPc, bPTc])
                                op("act", lambda e: e.activation(out=v3(PTn), in_=p3(b2), func=AF.Copy), [bbuf[b2]], [bPTn])
                                if not last:
                                    op("dve", lambda e: e.tensor_copy(out=v3(Pn), in_=p3(b1)), [bbuf[b1]], [bPn])
                                yield
                                next(side, None)
                                b3 = bank()
                                for j in range(nj):
                                    mm(b3, p3(b3)[:, j, :], PTn[:cs, ci + j, 0:cs], Mc[:cs, ci + j, 0:cs], True, True, [bPTn, bMc])
                                op("dve", lambda e: e.tensor_tensor(out=v3(Mn), in0=v3(Mc), in1=p3(b3), op=ALU.add), [bMc, bbuf[b3]], [bMn])
                                Pc, Pn, bPc, bPn = Pn, Pc, bPn, bPc
                                PTc, PTn, bPTc, bPTn = PTn, PTc, bPTn, bPTc
                                Mc, Mn, bMc, bMn = Mn, Mc, bMn, bMc
                                yield
                            for _ in side:
                                yield
                            op("act", lambda e: e.activation(out=v3(TTb), in_=v3(Mc), func=AF.Copy), [bMc], [G["TT"]])
                            yield
                            b1 = bank()
                            for j in range(nj):
                                mm(b1, banks[b1][:, j * cs:(j + 1) * cs], Rk[:cs, ci + j, :], TTb[:cs, ci + j, 0:cs], True, True, [G["R"], G["TT"]])
                            op("act", lambda e: e.activation(out=nWT[:, ci:ci + nj, 0:cs], in_=banks[b1][:, 0:nj * cs].rearrange("p (j c) -> p j c", c=cs), func=AF.Copy, scale=-1.0), [bbuf[b1]], [G["nWT"]])
                            yield

                        def chain(offs, cs, ci, G):
                            nj = len(offs)
                            gs0 = offs[0]
                            gl_ = nj * cs
                            bo = bank(reserve=True)
                            for j in range(nj):
                                o_ = offs[j]
                                op("act", lambda e: e.activation(out=Sb[:, :], in_=Sg[:, l, hh, :], func=AF.Copy), [b_Sg[l][hh]], [b_Sb])
                                b1 = bank()
                                mm(b1, banks[b1][:cs, 0:128], TTb[:cs, ci + j, 0:cs], Rv[:cs, ci + j, :], True, False, [G["TT"], G["R"]])
                                mm(b1, banks[b1][:cs, 0:128], nWT[:, ci + j, 0:cs], Sb[:, :], False, True, [G["nWT"], b_Sb])
                                op("act", lambda e: e.activation(out=vnb[:cs, :], in_=banks[b1][:cs, 0:128], func=AF.Copy), [bbuf[b1]], [b_vnb])
                                mm(bo, banks[bo][:, j * cs:(j + 1) * cs], Sb[:, :], qdb[:, o_:o_ + cs], True, False, [b_Sb, G["qdb"]])
                                mm(bo, banks[bo][:, j * cs:(j + 1) * cs], vnb[:cs, :], aTb[:cs, ci + j, 0:cs], False, True, [b_vnb, G["aT"]])
                                b2 = bank()
                                mm(b2, banks[b2][:, 0:128], ktl[:cs, ci + j, :], vnb[:cs, :], True, True, [G["R"], b_vnb])
                                op("dve", lambda e: e.scalar_tensor_tensor(out=Sg[:, l, hh, :], in0=Sg[:, l, hh, :], scalar=eglast[:, ci + j, hh:hh + 1], in1=banks[b2][:, 0:128], op0=ALU.mult, op1=ALU.add),
                                   [b_Sg[l][hh], b_sm, bbuf[b2]], [b_Sg[l][hh]])
                            op("dve", lambda e: e.tensor_copy(out=oT[:, gs0:gs0 + gl_], in_=banks[bo][:, 0:gl_]), [bbuf[bo]], [G["oT"]])
                            reserved[bo] = False

                        grp = []
                        ci_ = 0
                        for offs, cs in cgroups:
                            Gd = {k_: Buf() for k_ in ("DT", "E", "MB", "Bm", "BT", "P", "PT", "M", "M2", "R", "TT", "aT", "nWT", "egb", "qdb", "oT")}
                            grp.append((offs, cs, ci_, Gd))
                            ci_ += len(offs)
                        gens = [pre_gen(*g_) for g_ in grp]
                        while gens:
                            for g_ in list(gens):
                                try:
                                    next(g_)
                                except StopIteration:
                                    gens.remove(g_)
                        for g_ in grp:
                            chain(*g_)
                        b_oTs = [g_[3]["oT"] for g_ in grp]
                        op("dve", lambda e: e.tensor_tensor(out=sq[:, 0:T], in0=oT[:, 0:T], in1=oT[:, 0:T], op=ALU.mult), b_oTs, [b_sq])
                        for (s0, sn) in segs:
                            bi = bank()
                            mm(bi, banks[bi][:, 0:sn], ones[:, :], sq[:, s0:s0 + sn], True, True, [b_sq, b_const])
                            op("act", lambda e: e.activation(out=rin[:, s0:s0 + sn], in_=banks[bi][:, 0:sn], func=AF.Sqrt, bias=RMS_EPS, scale=1.0 / 128), [bbuf[bi]], [b_rin])
                        op("dve", lambda e: e.reciprocal(out=rin[:, 0:T], in_=rin[:, 0:T]), [b_rin], [b_rin])
                        op("dve", lambda e: e.tensor_tensor(out=oT[:, 0:T], in0=oT[:, 0:T], in1=rin[:, 0:T], op=ALU.mult), b_oTs + [b_rin], b_oTs)
                        op("dve", lambda e: e.scalar_tensor_tensor(out=y_g[:, hh, 0:T], in0=oT[:, 0:T], scalar=pcol(l, R_GNW), in1=sgt[:, 0:T], op0=ALU.mult, op1=ALU.mult), b_oTs + [b_par, b_sgt], [b_yg])
                    S.barrier()
                ph = ExitStack()
                if True:
                  if "S" in phases:
                    psb = lambda n_, sh, dt=F32: sb(n_, sh, dt, ph)
                    NT_ = len(tiles)
                    dtp = psb("s_dtp", [128, NTILE, 16]); adt = psb("s_adt", [128, NTILE, 16]); tsm2 = psb("s_tsm", [128, NTILE, 16])
                    eacum = psb("s_eacum", [128, NTILE, 16]); edec = psb("s_edec", [128, NTILE, 16]); echk = psb("s_echk", [128, NTILE, 16])
                    dte = psb("s_dte", [128, NTILE, 16])
                    b_ss = Buf(); b_st = Buf()
                    wv, bw = load_w([win_d[l, :, O_SDT:O_SDT + 16]], 8, 16)
                    bi = bank()
                    for i, (off, n) in enumerate(tiles):
                        for kc in range(8):
                            mm(bi, banks[bi][:n, i * 16:(i + 1) * 16], xnT[:, kc, off:off + n], wv[:, kc, 0:16], kc == 0, kc == 7, [b_xnT, bw])
                    op("dve", lambda e: e.tensor_tensor(out=dtp[:, 0:NT_, :], in0=banks[bi][:, 0:NT_ * 16].rearrange("p (j c) -> p j c", c=16),
                                                        in1=ptm[:, l, C_SDB:C_SDB + 16].unsqueeze(1).to_broadcast([128, NT_, 16]), op=ALU.add), [bbuf[bi], b_par], [b_ss])
                    softplus_inplace(dtp[:, 0:NT_, :].rearrange("p j c -> p (j c)"), tsm2[:, 0:NT_, :].rearrange("p j c -> p (j c)"), b_ss, b_st)
                    op("dve", lambda e: e.tensor_tensor(out=adt[:, 0:NT_, :], in0=dtp[:, 0:NT_, :], in1=A_s[:, l, :].unsqueeze(1).to_broadcast([128, NT_, 16]), op=ALU.mult), [b_ss, b_par], [b_ss])
                    for i, (off, n) in enumerate(tiles):
                        bi = bank()
                        mm(bi, banks[bi][:n, 0:16], Ui[:n, :n], adt[:n, i, :], True, True, [b_ss, b_const])
                        mm(bi, banks[bi][:, 16:32], ones[:n, :], adt[:n, i, :], True, True, [b_ss, b_const])
                        op("act", lambda e: e.activation(out=eacum[:n, i, :], in_=banks[bi][:n, 0:16], func=AF.Exp), [bbuf[bi]], [b_ss])
                        op("act", lambda e: e.activation(out=echk[:, i, :], in_=banks[bi][:, 16:32], func=AF.Exp), [bbuf[bi]], [b_ss])
                        op("act", lambda e: e.activation(out=tsm2[:n, i, :], in_=banks[bi][:n, 16:32], func=AF.Copy), [bbuf[bi]], [b_st])
                        op("dve", lambda e: e.tensor_tensor(out=edec[:n, i, :], in0=tsm2[:n, i, :], in1=banks[bi][:n, 0:16], op=ALU.subtract), [b_st, bbuf[bi]], [b_ss])
                        op("act", lambda e: e.activation(out=edec[:n, i, :], in_=edec[:n, i, :], func=AF.Exp), [b_ss], [b_ss])
                        op("dve", lambda e: e.tensor_tensor(out=dte[:n, i, :], in0=dtp[:n, i, :], in1=edec[:n, i, :], op=ALU.mult), [b_ss], [b_ss])
                    SST = 99
                    xs4 = psb("s_xs4", [128, 4, TMAX + 3]); b_xs4 = Buf()
                    cs4 = psb("s_cs4", [128, 4, TMAX]); b_cs4 = Buf()
                    BTb = psb("s_BTb", [128, TMAX], BF16); CTb = psb("s_CTb", [128, TMAX], BF16); b_BC = Buf()
                    sz = psb("s_sz", [128, NTILE, 256]); b_sz = Buf()
                    ssets = []
                    for k_ in range(2):
                        ssets.append({"t": (psb(f"s_xstm{k_}", [128, 256]), psb(f"s_xdt{k_}", [128, 4, 64], BF16), psb(f"s_xdt2{k_}", [128, 4, 64], BF16),
                                            psb(f"s_Btm{k_}", [128, 128], BF16), psb(f"s_La{k_}", [128, 4, 128]), psb(f"s_MT{k_}", [128, 4, 128], BF16),
                                            psb(f"s_t1{k_}", [128, 256]), psb(f"s_t2{k_}", [128, 256]), psb(f"s_ssm{k_}", [128, 1])),
                                      "b": tuple(Buf() for _ in range(8))})
                    Hb = psb("s_Hb", [128, 256], BF16); b_Hb = Buf()
                    for gi in range(4 if SST > 0 else 0):
                        wa, bwa = load_w([win_d[l, :, O_SX + gi * 256:O_SX + (gi + 1) * 256], win_d[l, :, O_SB + gi * 128:O_SB + (gi + 1) * 128],
                                          win_d[l, :, O_SC + gi * 128:O_SC + (gi + 1) * 128]], 8, 512)
                        chns = [2 * gi, 2 * gi + 1, 8 + gi, 12 + gi]
                        for qi in range(4):
                            op("dve", lambda e: e.tensor_copy(out=xs4[:, qi, 0:3], in_=halo_s[:, l, chns[qi], :]), [b_halo], [b_xs4])
                            for (s0, sn) in segs:
                                bi = bank()
                                proj_FM(bi, wa, bwa, qi * 128, 128, xnT, b_xnT, s0, sn)
                                op("act", lambda e: e.activation(out=xs4[:, qi, 3 + s0:3 + s0 + sn], in_=banks[bi][:, 0:sn], func=AF.Copy), [bbuf[bi]], [b_xs4])
                        for qi in range(4):
                            chn = chns[qi]
                            cw = lambda tap: pcol(l, R_SCW + tap * 16 + chn)
                            op("dve", lambda e: e.tensor_scalar(out=cs4[:, qi, 0:T], in0=xs4[:, qi, 3:3 + T], scalar1=cw(3), scalar2=pcol(l, R_SCB + chn), op0=ALU.mult, op1=ALU.add), [b_xs4, b_par], [b_cs4])
                            for tap in range(3):
                                op("dve", lambda e: e.scalar_tensor_tensor(out=cs4[:, qi, 0:T], in0=xs4[:, qi, tap:tap + T], scalar=cw(tap), in1=cs4[:, qi, 0:T], op0=ALU.mult, op1=ALU.add), [b_xs4, b_par, b_cs4], [b_cs4])
                            op("dve", lambda e: e.tensor_copy(out=halo_s[:, l, chn, :], in_=xs4[:, qi, T:T + 3]), [b_xs4], [b_halo])
                            op("act", lambda e: e.activation(out=cs4[:, qi, 0:T], in_=cs4[:, qi, 0:T], func=AF.Silu), [b_cs4], [b_cs4])
                        op("dve", lambda e: e.tensor_copy(out=BTb[:, 0:T], in_=cs4[:, 2, 0:T]), [b_cs4], [b_BC])
                        op("dve", lambda e: e.tensor_copy(out=CTb[:, 0:T], in_=cs4[:, 3, 0:T]), [b_cs4], [b_BC])
                        wz, bwz = load_w([win_d[l, :, O_SZ + gi * 256:O_SZ + (gi + 1) * 256]], 8, 256)
                        for i, (off, n) in enumerate(tiles):
                            bi = bank()
                            for kc in range(8):
                                mm(bi, banks[bi][:n, 0:256], xnT[:, kc, off:off + n], wz[:, kc, 0:256], kc == 0, kc == 7, [b_xnT, bwz])
                            op("act", lambda e: e.activation(out=sz[:n, i, :], in_=banks[bi][:n, 0:256], func=AF.Silu), [bbuf[bi]], [b_sz])
                        op("act", lambda e: e.activation(out=Hb[:, :], in_=Hs[:, l, gi, :], func=AF.Copy), [b_Hs[l][gi]], [b_Hb])
                        def ssd_tile(i, off, n, K):
                            xs_tm, xdt, xdt2, Btm, La, MT, t1_, t2_, ssm = K["t"]
                            E4 = La
                            b_xstm, b_xdt, b_Btm, b_La, b_MT, b_t1, b_t2, b_ssm = K["b"]
                            b_E4 = b_La
                            t1 = t1_[:, :].rearrange("p (r c) -> p r c", c=64); t2 = t2_[:, :].rearrange("p (r c) -> p r c", c=64)
                            hd = slice(4 * gi, 4 * gi + 4)
                            bc4 = lambda ap_: ap_.unsqueeze(2).to_broadcast([n, 4, 64])
                            bi = bank()
                            tr(bi, banks[bi][:n, 0:128], cs4[:, 0, off:off + n], 128, [b_cs4])
                            tr(bi, banks[bi][:n, 128:256], cs4[:, 1, off:off + n], 128, [b_cs4])
                            px = banks[bi][:n, 0:256].rearrange("p (r c) -> p r c", c=64)
                            op("act", lambda e: e.activation(out=xs_tm[:n, :], in_=banks[bi][:n, 0:256], func=AF.Copy), [bbuf[bi]], [b_xstm])
                            op("dve", lambda e: e.tensor_tensor(out=xdt[:n], in0=px, in1=bc4(dtp[:n, i, hd]), op=ALU.mult), [bbuf[bi], b_ss], [b_xdt])
                            op("dve", lambda e: e.tensor_tensor(out=xdt2[:n], in0=px, in1=bc4(dte[:n, i, hd]), op=ALU.mult), [bbuf[bi], b_ss], [b_xdt])
                            bi = bank()
                            tr(bi, banks[bi][:n, 0:128], cs4[:, 2, off:off + n], 128, [b_cs4])
                            op("act", lambda e: e.activation(out=Btm[:n, :], in_=banks[bi][:n, 0:128], func=AF.Copy), [bbuf[bi]], [b_Btm])
                            yield
                            b1 = bank()
                            mm(b1, banks[b1][:n, 0:n], BTb[:, off:off + n], CTb[:, off:off + n], True, True, [b_BC])
                            op("dve", lambda e: e.tensor_tensor(out=La[:n, :, 0:n], in0=adt[:n, i, hd].unsqueeze(2).to_broadcast([n, 4, n]), in1=Ui[:n, :n].unsqueeze(1).to_broadcast([n, 4, n]), op=ALU.mult), [b_ss, b_const], [b_La])
                            b2 = bank()
                            p2 = banks[b2][:n, 0:4 * n].rearrange("p (r c) -> p r c", c=n)
                            for r_ in range(4):
                                mm(b2, p2[:, r_, :], Ls[:n, :n], La[:n, r_, 0:n], True, True, [b_La, b_const])
                            op("act", lambda e: e.activation(out=E4[:n, :, 0:n], in_=p2, func=AF.Exp), [bbuf[b2]], [b_E4])
                            yield
                            op("dve", lambda e: e.tensor_tensor(out=E4[:n, :, 0:n], in0=E4[:n, :, 0:n], in1=Ui[:n, :n].unsqueeze(1).to_broadcast([n, 4, n]), op=ALU.mult), [b_E4, b_const], [b_E4])
                            op("dve", lambda e: e.tensor_tensor(out=MT[:n, :, 0:n], in0=E4[:n, :, 0:n], in1=banks[b1][:n, 0:n].unsqueeze(1).to_broadcast([n, 4, n]), op=ALU.mult), [b_E4, bbuf[b1]], [b_MT])
                            b3 = bank()
                            for r_ in range(4):
                                mm(b3, banks[b3][:n, r_ * 64:(r_ + 1) * 64], MT[:n, r_, 0:n], xdt[:n, r_, :], True, True, [b_MT, b_xdt])
                            yield
                            b4 = bank()
                            mm(b4, banks[b4][:n, 0:256], CTb[:, off:off + n], Hb[:, :], True, True, [b_BC, b_Hb])
                            b5 = bank()
                            mm(b5, banks[b5][:, 0:256], Btm[:n, :], xdt2[:n].rearrange("p r c -> p (r c)"), True, True, [b_Btm, b_xdt])
                            hv = Hs[:, l, gi, :].rearrange("p (r c) -> p r c", c=64)
                            op("dve", lambda e: e.tensor_tensor(out=hv, in0=hv, in1=echk[:, i, hd].unsqueeze(2).to_broadcast([128, 4, 64]), op=ALU.mult), [b_Hs[l][gi], b_ss], [b_Hs[l][gi]])
                            op("dve", lambda e: e.tensor_tensor(out=Hs[:, l, gi, :], in0=Hs[:, l, gi, :], in1=banks[b5][:, 0:256], op=ALU.add), [b_Hs[l][gi], bbuf[b5]], [b_Hs[l][gi]])
                            op("act", lambda e: e.activation(out=Hb[:, :], in_=Hs[:, l, gi, :], func=AF.Copy), [b_Hs[l][gi]], [b_Hb])
                            v4 = lambda b_: banks[b_][:n, 0:256].rearrange("p (r c) -> p r c", c=64)
                            op("dve", lambda e: e.tensor_tensor(out=t1[:n], in0=v4(b4), in1=bc4(eacum[:n, i, hd]), op=ALU.mult), [bbuf[b4], b_ss], [b_t1])
                            yield
                            op("dve", lambda e: e.tensor_tensor(out=t1[:n], in0=t1[:n], in1=v4(b3), op=ALU.add), [b_t1, bbuf[b3]], [b_t1])
                            op("dve", lambda e: e.tensor_tensor(out=t2[:n], in0=xs_tm[:n, :].rearrange("p (r c) -> p r c", c=64), in1=bc4(ptm[:n, l, C_SD + 4 * gi:C_SD + 4 * gi + 4]), op=ALU.mult), [b_xstm, b_par], [b_t2])
                            op("dve", lambda e: e.tensor_tensor(out=t1[:n], in0=t1[:n], in1=t2[:n], op=ALU.add), [b_t1, b_t2], [b_t1])
                            op("dve", lambda e: e.tensor_tensor(out=t1[:n], in0=t1[:n], in1=sz[:n, i, :].rearrange("p (r c) -> p r c", c=64), op=ALU.mult), [b_t1, b_sz], [b_t1])
                            op("act", lambda e: e.activation(out=t2_[:n, :], in_=t1_[:n, :], func=AF.Square, accum_out=ssm[:n, 0:1]), [b_t1], [b_t2, b_ssm])
                            yield
                            rsqrt_inplace(ssm[:n, 0:1], b_ssm, 1.0 / 256, RMS_EPS)
                            op("dve", lambda e: e.tensor_scalar(out=t1_[:n, :], in0=t1_[:n, :], scalar1=ssm[:n, 0:1], scalar2=None, op0=ALU.mult), [b_t1, b_ssm], [b_t1])
                            b6 = bank()
                            tr(b6, banks[b6][:, 0:n], t1_[:n, 0:128], n, [b_t1])
                            tr(b6, banks[b6][:, 128:128 + n], t1_[:n, 128:256], n, [b_t1])
                            for c_ in range(2):
                                op("dve", lambda e: e.tensor_scalar(out=y_s[:, 2 * gi + c_, off:off + n], in0=banks[b6][:, c_ * 128:c_ * 128 + n], scalar1=pcol(l, R_SNW + 2 * gi + c_), scalar2=None, op0=ALU.mult), [bbuf[b6], b_par], [b_ys])
                            yield

                        tl_ = list(enumerate(tiles))
                        for p0 in range(0, len(tl_), 2):
                            gens = [ssd_tile(i_, o_, n_, ssets[k_]) for k_, (i_, (o_, n_)) in enumerate(tl_[p0:p0 + 2])]
                            while gens:
                                for g_ in list(gens):
                                    try:
                                        next(g_)
                                    except StopIteration:
                                        gens.remove(g_)
                    pass
                if True:
                  if "C" in phases:
                    psb = lambda n_, sh, dt=F32: sb(n_, sh, dt, ph)
                    seqbase = NMETA if s == 0 else 0
                    qTb2 = psb("c_qTb", [64, 4, TMAX], BF16); b_qT2 = Buf()
                    eTs = [psb(f"c_eT{i_}", [128, 4, 128], BF16) for i_ in range(3)]; b_eT = [Buf() for _ in range(3)]
                    etmp = psb("c_etmp", [128, 4, 128]); b_etmp = Buf()
                    den = psb("c_den", [128, 4]); b_den = Buf()
                    otm_ = psb("c_otm", [128, 256]); b_otm = Buf()
                    otm = otm_[:, :].rearrange("p (r c) -> p r c", c=64)
                    wkv, bwkv = load_w([win_d[l, :, O_CK:O_CK + 256], win_d[l, :, O_CV:O_CV + 256]], 8, 512)
                    kcol = lambda t_: t_ if (s == 0 and t_ < NMETA) else NMETA + 128 + (t_ - seqbase)
                    for hk in range(4):
                        for (s0, sn) in segs:
                            bi = bank()
                            proj_FM(bi, wkv, bwkv, hk * 64, 64, xnT, b_xnT, s0, sn)
                            d0 = kcol(s0)
                            op("act", lambda e: e.activation(out=kTc[:, l, hk, d0:d0 + sn], in_=banks[bi][:64, 0:sn], func=AF.Copy), [bbuf[bi]], [b_kT[l]])
                    vslot = lambda i_: 0 if (s == 0 and i_ == 0) else 2 + i_ - (1 if s == 0 else 0)
                    for i, (off, n) in enumerate(tiles):
                        bi = bank()
                        for kc in range(8):
                            mm(bi, banks[bi][:n, 0:256], xnT[:, kc, off:off + n], wkv[:, kc, 256:512], kc == 0, kc == 7, [b_xnT, bwkv])
                        op("act", lambda e: e.activation(out=vA[:n, l, vslot(i), :, 0:64], in_=banks[bi][:n, 0:256].rearrange("p (r c) -> p r c", c=64), func=AF.Copy), [bbuf[bi]], [b_vA[l]])
                    for hk in range(4):
                        wq, bwq = load_w([win_d[l, :, O_CQ + hk * 256:O_CQ + (hk + 1) * 256]], 8, 256)
                        for r_ in range(4):
                            for (s0, sn) in segs:
                                bi = bank()
                                proj_FM(bi, wq, bwq, r_ * 64, 64, xnT, b_xnT, s0, sn)
                                op("act", lambda e: e.activation(out=qTb2[:, r_, s0:s0 + sn], in_=banks[bi][:64, 0:sn], func=AF.Copy), [bbuf[bi]], [b_qT2])
                        for i, (off, n) in enumerate(tiles):
                            is_meta = (s == 0 and i == 0)
                            if is_meta:
                                kbs = [(0, 0, NMETA, Ui)]
                            else:
                                k_ = i - (1 if s == 0 else 0)
                                kbs = [(NMETA + 128 + 128 * k_, 2 + k_, 128, Ui)]
                                if k_ > 0:
                                    kbs.append((NMETA + 128 + 128 * (k_ - 1), 2 + k_ - 1, 128, Ls))
                                elif s > 0:
                                    kbs.append((NMETA, 1, 128, Ls))
                                kbs.append((0, 0, NMETA, None))
                            for idx, (kc0, vs_, nk, msk) in enumerate(kbs):
                                bi = bank()
                                pq = banks[bi][:nk, 0:4 * n].rearrange("p (r c) -> p r c", c=n)
                                for r_ in range(4):
                                    mm(bi, pq[:, r_, :], kTc[:, l, hk, kc0:kc0 + nk], qTb2[:, r_, off:off + n], True, True, [b_kT[l], b_qT2])
                                if msk is None:
                                    op("act", lambda e: e.activation(out=eTs[idx][:nk, :, 0:n], in_=pq, func=AF.Exp, scale=0.125), [bbuf[bi]], [b_eT[idx]])
                                else:
                                    op("act", lambda e: e.activation(out=etmp[:nk, :, 0:n], in_=pq, func=AF.Exp, scale=0.125), [bbuf[bi]], [b_etmp])
                                    op("dve", lambda e: e.tensor_tensor(out=eTs[idx][:nk, :, 0:n], in0=etmp[:nk, :, 0:n], in1=msk[:nk, :n].unsqueeze(1).to_broadcast([nk, 4, n]), op=ALU.mult), [b_etmp, b_const], [b_eT[idx]])
                            bo = bank()
                            po = banks[bo][:n, 0:260].rearrange("p (r c) -> p r c", c=65)
                            for r_ in range(4):
                                for idx, (kc0, vs_, nk, msk) in enumerate(kbs):
                                    mm(bo, po[:, r_, :], eTs[idx][:nk, r_, 0:n], vA[:nk, l, vs_, hk, :], idx == 0, idx == len(kbs) - 1, [b_eT[idx], b_vA[l]])
                            op("dve", lambda e: e.tensor_tensor(out=den[:n, :], in0=po[:, :, 64], in1=esink[:n, l, 4 * hk:4 * hk + 4], op=ALU.add), [bbuf[bo], b_par], [b_den])
                            op("dve", lambda e: e.reciprocal(out=den[:n, :], in_=den[:n, :]), [b_den], [b_den])
                            op("dve", lambda e: e.tensor_tensor(out=otm[:n], in0=po[:, :, 0:64], in1=den[:n, :].unsqueeze(2).to_broadcast([n, 4, 64]), op=ALU.mult), [bbuf[bo], b_den], [b_otm])
                            b6 = bank()
                            of = otm_[:n, :]
                            tr(b6, banks[b6][:, 0:n], of[:, 0:128], n, [b_otm])
                            tr(b6, banks[b6][:, 128:128 + n], of[:, 128:256], n, [b_otm])
                            for c_ in range(2):
                                op("act", lambda e: e.activation(out=y_c[:, 2 * hk + c_, off:off + n], in_=banks[b6][:, c_ * 128:c_ * 128 + n], func=AF.Copy), [bbuf[b6]], [b_yc])
                    op("dve", lambda e: e.tensor_copy(out=kTc[:, l, :, NMETA:NMETA + 128], in_=kTc[:, l, :, NMETA + TS:NMETA + TS + 128]), [b_kT[l]], [b_kT[l]])
                    op("dve", lambda e: e.tensor_copy(out=vA[:, l, 1, :, 0:64], in_=vA[:, l, 1 + TPS, :, 0:64]), [b_vA[l]], [b_vA[l]])
                    pass
                if True:
                  if "M" in phases:
                    psb = lambda n_, sh, dt=F32: sb(n_, sh, dt, ph)
                    mrg = psb("m_mrg", [128, 8, TMAX], BF16); b_mrg = Buf()
                    sig = psb("m_sig", [128, 512]); b_sig = Buf()
                    tmpm = psb("m_tmp", [128, 512]); b_tmpm = Buf()
                    acc = psb("m_acc", [128, 512]); b_acc = Buf()
                    ysrc = [(y_g, b_yg, wpg_d), (y_s, b_ys, wps_d), (y_c, b_yc, wpc_d)]
                    for fc in range(8):
                        wg, bwg = load_w([win_d[l, :, O_GATE + br * 1024 + fc * 128:O_GATE + br * 1024 + (fc + 1) * 128] for br in range(3)], 8, 384)
                        wp, bwp = load_w([ysrc[br][2][l, :, fc * 128:(fc + 1) * 128] for br in range(3)], 8, 384)
                        for (s0, sn) in segs:
                            for br in range(3):
                                b1 = bank()
                                proj_FM(b1, wg, bwg, br * 128, 128, xnT, b_xnT, s0, sn)
                                op("act", lambda e: e.activation(out=sig[:, 0:sn], in_=banks[b1][:, 0:sn], func=AF.Sigmoid), [bbuf[b1]], [b_sig])
                                b2 = bank()
                                proj_FM(b2, wp, bwp, br * 128, 128, ysrc[br][0], ysrc[br][1], s0, sn)
                                if br == 0:
                                    op("dve", lambda e: e.tensor_tensor(out=acc[:, 0:sn], in0=sig[:, 0:sn], in1=banks[b2][:, 0:sn], op=ALU.mult), [b_sig, bbuf[b2]], [b_acc])
                                else:
                                    op("dve", lambda e: e.tensor_tensor(out=tmpm[:, 0:sn], in0=sig[:, 0:sn], in1=banks[b2][:, 0:sn], op=ALU.mult), [b_sig, bbuf[b2]], [b_tmpm])
                                    if br == 1:
                                        op("dve", lambda e: e.tensor_tensor(out=acc[:, 0:sn], in0=acc[:, 0:sn], in1=tmpm[:, 0:sn], op=ALU.add), [b_acc, b_tmpm], [b_acc])
                                    else:
                                        op("dve", lambda e: e.tensor_tensor(out=mrg[:, fc, s0:s0 + sn], in0=acc[:, 0:sn], in1=tmpm[:, 0:sn], op=ALU.add), [b_acc, b_tmpm], [b_mrg])
                    if dbg == "mrg" and l == 0 and s == 0:
                        dbg_tap(mrg, b_mrg)
                    for half in range(2):
                        wo, bwo = load_w([wout_d[l, :, half * 512:(half + 1) * 512]], 8, 512)
                        for i, (off, n) in enumerate(tiles):
                            bi = bank()
                            for kc in range(8):
                                mm(bi, banks[bi][:n, 0:512], mrg[:, kc, off:off + n], wo[:, kc, :], kc == 0, kc == 7, [b_mrg, bwo])
                            op("dve", lambda e: e.tensor_tensor(out=h[:n, i, half * 512:(half + 1) * 512], in0=h[:n, i, half * 512:(half + 1) * 512], in1=banks[bi][:n, 0:512], op=ALU.add), [b_h[i], bbuf[bi]], [b_h[i]])
                    pass
                if dbg and l == 0 and s == 0 and dbg in ("y_g", "y_s", "y_c"):
                    dbg_tap({"y_g": y_g, "y_s": y_s, "y_c": y_c}[dbg], {"y_g": b_yg, "y_s": b_ys, "y_c": b_yc}[dbg])
                norm_to_FM(tiles, l, R_N2)
                if True:
                  if "F" in phases:
                    psb = lambda n_, sh, dt=F32: sb(n_, sh, dt, ph)
                    actT = psb("f_act", [128, 4, TMAX], BF16); b_actT = Buf()
                    rl = sig; b_rl = b_sig
                    for dg in range(8):
                        wu, bwu = load_w([wup_d[l, :, dg * 512:(dg + 1) * 512]], 8, 512)
                        wd, bwd = load_w([wdn_d[l, dg * 512:(dg + 1) * 512, :]], 4, 1024)
                        for c_ in range(4):
                            for (s0, sn) in segs:
                                bi = bank()
                                proj_FM(bi, wu, bwu, c_ * 128, 128, xnT, b_xnT, s0, sn)
                                op("act", lambda e: e.activation(out=rl[:, 0:sn], in_=banks[bi][:, 0:sn], func=AF.Relu), [bbuf[bi]], [b_rl])
                                op("dve", lambda e: e.tensor_tensor(out=actT[:, c_, s0:s0 + sn], in0=rl[:, 0:sn], in1=rl[:, 0:sn], op=ALU.mult), [b_rl], [b_actT])
                        for i, (off, n) in enumerate(tiles):
                            for half in range(2):
                                bi = bank()
                                for c_ in range(4):
                                    mm(bi, banks[bi][:n, 0:512], actT[:, c_, off:off + n], wd[:, c_, half * 512:(half + 1) * 512], c_ == 0, c_ == 3, [b_actT, bwd])
                                op("dve", lambda e: e.tensor_tensor(out=h[:n, i, half * 512:(half + 1) * 512], in0=h[:n, i, half * 512:(half + 1) * 512], in1=banks[bi][:n, 0:512], op=ALU.add), [b_h[i], bbuf[bi]], [b_h[i]])
                    S.barrier()
                    ph.close()

            with ExitStack() as ph:
                ot = sb("f_ot", [128, 2, D], F32, ph)
                fnw = sb("fnw", [128, D], F32, ph)
                b_fnw = Buf()
                S.dma([(fnw[:], fnw_d[0:1, :].partition_broadcast(128))], writes=[b_fnw], q="act")
                b_ot = [Buf(), Buf()]
                k_ = 0
                for i, (off, n) in enumerate(tiles):
                    if s == 0 and i == 0:
                        continue
                    op("act", lambda e: e.activation(out=nscr[:n, :], in_=h[:n, i, :], func=AF.Square, accum_out=nsm[:n, i:i + 1]), [b_h[i]], [b_nscr, b_nsm])
                    rsqrt_inplace(nsm[:n, i:i + 1], b_nsm, 1.0 / D, RMS_EPS)
                    op("dve", lambda e: e.scalar_tensor_tensor(out=ot[:n, k_ % 2, :], in0=h[:n, i, :], scalar=nsm[:n, i:i + 1], in1=fnw[:n, :], op0=ALU.mult, op1=ALU.mult),
                       [b_h[i], b_nsm, b_fnw], [b_ot[k_ % 2]])
                    r0 = seq0 + off - (NMETA if s == 0 else 0)
                    S.dma([(out_d[r0:r0 + n, :], ot[:n, k_ % 2, :])], reads=[b_ot[k_ % 2]], q="act")
                    k_ += 1
                S.barrier()
        S.final_wait("sp")
        S.final_wait("act")
        print("instructions", S.n_ins, "waits", S.n_wait)
    return nc


def pack_params(inp):
    pfm = np.zeros((2, 256, 128), np.float32)
    ptm = np.zeros((2, PTM_W), np.float32)
    for l in range(2):
        pfm[l, R_N1:R_N1 + 8] = np.asarray(inp["norm1_w"][l]).reshape(8, 128)
        pfm[l, R_N2:R_N2 + 8] = np.asarray(inp["norm2_w"][l]).reshape(8, 128)
        pfm[l, R_GCW:R_GCW + 96] = np.asarray(inp["gdn_conv_w"][l]).reshape(96, 128)
        pfm[l, R_SCW:R_SCW + 64] = np.asarray(inp["ssd_conv_w"][l]).reshape(64, 128)
        pfm[l, R_SCB:R_SCB + 16] = np.asarray(inp["ssd_conv_b"][l]).reshape(16, 128)
        pfm[l, R_GNW] = np.asarray(inp["gdn_norm_w"][l])
        pfm[l, R_SNW:R_SNW + 8] = np.asarray(inp["ssd_norm_w"][l]).reshape(8, 128)
        ptm[l, C_GAL:C_GAL + 8] = np.asarray(inp["gdn_a_log"][l])
        ptm[l, C_GDB:C_GDB + 8] = np.asarray(inp["gdn_dt_bias"][l])
        ptm[l, C_SDB:C_SDB + 16] = np.asarray(inp["ssd_dt_bias"][l])
        ptm[l, C_SAL:C_SAL + 16] = np.asarray(inp["ssd_a_log"][l])
        ptm[l, C_SD:C_SD + 16] = np.asarray(inp["ssd_d"][l])
        ptm[l, C_SNK:C_SNK + 16] = np.asarray(inp["swa_sinks"][l])
    return pfm, ptm


_NC_CACHE = {}


def kernel(**inputs):
    inp = {k: np.asarray(v) for k, v in inputs.items()}
    n = 8
    if "nc" not in _NC_CACHE:
        _NC_CACHE["nc"] = build(NST=8, NL=2, TPS=4)
    nc = _NC_CACHE["nc"]
    pfm, ptm = pack_params(inp)
    f32 = lambda a: np.ascontiguousarray(a, dtype=np.float32)
    shared = dict(meta=f32(inp["meta_tokens"]), w_in=f32(inp["w_in"]), w_pg=f32(inp["w_proj_gdn"]), w_ps=f32(inp["w_proj_ssd"]),
                  w_pc=f32(inp["w_proj_swa"]), w_out=f32(inp["w_out"]), w_up=f32(inp["w_up"]), w_dn=f32(inp["w_down"]),
                  pfm=pfm, ptm=ptm, fnw=f32(inp["final_norm_w"]).reshape(1, -1))
    in_maps = [dict(shared, x=f32(inp["x"][i])) for i in range(n)]
    res = run_bass_kernel_spmd(nc, in_maps, core_ids=list(range(n)))
    return np.stack([np.asarray(r["out"], dtype=np.float32) for r in res.results], axis=0)
```

```python
import numpy as np
from contextlib import ExitStack
import concourse.bass as bass
import concourse.mybir as mybir
from concourse.bass_utils import run_bass_kernel_spmd

F32 = mybir.dt.float32
BF16 = mybir.dt.bfloat16
AF = mybir.ActivationFunctionType
ALU = mybir.AluOpType
AX = mybir.AxisListType

D = 1024
SEQ = 4096
NMETA = 16
DFF = 4096
IN_W = 11808
O_GQ, O_GK, O_GV, O_GG, O_GB, O_GA = 0, 1024, 2048, 3072, 4096, 4104
O_SZ, O_SX, O_SB, O_SC, O_SDT = 4112, 5136, 6160, 6672, 7184
O_CQ, O_CK, O_CV, O_GATE = 7200, 8224, 8480, 8736
RMS_EPS = 1e-6
L2_EPS = 1e-6
R_N1, R_N2, R_GCW, R_SCW, R_SCB, R_GNW, R_SNW = 0, 8, 16, 112, 176, 192, 193
C_GAL, C_GDB, C_SDB, C_SAL, C_SD, C_SNK, PTM_W = 0, 8, 16, 32, 48, 64, 80


class Buf:
    __slots__ = ("name", "lw", "rd")

    def __init__(self, name=""):
        self.name = name
        self.lw = None
        self.rd = {}


class Sched:
    ENG = ("pe", "act", "dve", "pool", "sp")
    EPOCH = 30000

    def __init__(self, nc, stack, n_dma_sems=24):
        self.nc = nc
        self.stack = stack
        self.eng = {"pe": nc.tensor, "act": nc.scalar, "dve": nc.vector,
                    "pool": nc.gpsimd, "sp": nc.sync}
        self.semh = {}
        self.cnt = {}
        self.epoch = {}
        for e in self.ENG:
            self.epoch[e] = 0
            self.cnt[e] = 0
            self.semh[(e, 0)] = stack.enter_context(nc.semaphore(f"s_{e}_0"))
        self.waited = {e: {} for e in self.ENG}
        self.ndma = n_dma_sems
        self.dma_tot = [0] * n_dma_sems
        for j in range(n_dma_sems):
            self.semh[("d", j)] = stack.enter_context(nc.semaphore(f"s_dma_{j}"))
        self.dma_next = 0
        self.n_ins = 0
        self.n_wait = 0

    def _wait(self, e, tok):
        key, val = tok
        if self.waited[e].get(key, 0) >= val:
            return
        if key[0] == e and e == "pe":
            return
        self.eng[e].wait_ge(self.semh[key], val)
        self.waited[e][key] = val
        self.n_wait += 1

    def _deps(self, reads, writes):
        deps = {}

        def add(k, v):
            if deps.get(k, 0) < v:
                deps[k] = v
        for b in reads:
            if b.lw is not None:
                add(*b.lw)
        for b in writes:
            if b.lw is not None:
                add(*b.lw)
            for k, v in b.rd.items():
                add(k, v)
        return deps

    def _mark(self, tok, reads, writes):
        k, v = tok
        for b in reads:
            if b.rd.get(k, 0) < v:
                b.rd[k] = v
        for b in writes:
            b.lw = tok
            b.rd = {}

    def op(self, e, fn, reads=(), writes=()):
        deps = self._deps(reads, writes)
        for k, v in deps.items():
            self._wait(e, (k, v))
        if self.cnt[e] >= self.EPOCH:
            self.epoch[e] += 1
            self.cnt[e] = 0
            self.semh[(e, self.epoch[e])] = self.stack.enter_context(
                self.nc.semaphore(f"s_{e}_{self.epoch[e]}"))
        ins = fn(self.eng[e])
        self.cnt[e] += 1
        key = (e, self.epoch[e])
        ins.then_inc(self.semh[key], 1)
        tok = (key, self.cnt[e])
        self._mark(tok, reads, writes)
        self.n_ins += 1
        return tok

    def dma(self, pairs, reads=(), writes=(), q="sp"):
        j = self.dma_next
        self.dma_next = (self.dma_next + 1) % self.ndma
        key = ("d", j)
        if self.dma_tot[j] > 0:
            self._wait(q, (key, self.dma_tot[j]))
        deps = self._deps(reads, writes)
        for k, v in deps.items():
            self._wait(q, (k, v))
        for (o, i) in pairs:
            self.eng[q].dma_start(out=o, in_=i).then_inc(self.semh[key], 16)
            self.dma_tot[j] += 16
            self.n_ins += 1
        tok = (key, self.dma_tot[j])
        self._mark(tok, reads, writes)
        return tok

    def all_tokens(self):
        toks = []
        for e in self.ENG:
            if self.cnt[e] > 0:
                toks.append(((e, self.epoch[e]), self.cnt[e]))
            elif self.epoch[e] > 0:
                toks.append(((e, self.epoch[e] - 1), self.EPOCH))
        toks += [(("d", j), self.dma_tot[j]) for j in range(self.ndma) if self.dma_tot[j] > 0]
        return toks

    def barrier(self):
        toks = self.all_tokens()
        for e in self.ENG:
            if e in ("sp", "pool"):
                continue
            for t in toks:
                self._wait(e, t)

    def final_wait(self, e="sp"):
        for t in self.all_tokens():
            self._wait(e, t)


def build(NST=8, NL=2, TPS=4, dbg=False, phases="GSCMF"):
    TS = TPS * 128
    TMAX = TS + NMETA
    NTILE = TPS + 1
    nc = bass.Bass("TRN2", target_bir_lowering=False)
    dram = lambda n, s, k="ExternalInput": nc.dram_tensor(n, s, F32, kind=k).ap()
    x_d = dram("x", [SEQ, D])
    meta_d = dram("meta", [NMETA, D])
    win_d = dram("w_in", [2, D, IN_W])
    wpg_d = dram("w_pg", [2, D, D])
    wps_d = dram("w_ps", [2, D, D])
    wpc_d = dram("w_pc", [2, D, D])
    wout_d = dram("w_out", [2, D, D])
    wup_d = dram("w_up", [2, D, DFF])
    wdn_d = dram("w_dn", [2, DFF, D])
    pfm_d = dram("pfm", [2, 256, 128])
    ptm_d = dram("ptm", [2, PTM_W])
    fnw_d = dram("fnw", [1, D])
    out_d = dram("out", [NST * TS, D], "ExternalOutput")
    dbg_d = dram("dbg", [128, 8 * 528], "ExternalOutput") if dbg else None

    with ExitStack() as st:
        S = Sched(nc, st)
        sbytes = [0]

        def sb(name, shape, dt=F32, stack=None):
            sbytes[0] += 1
            t = (stack or st).enter_context(nc.sbuf_tensor(f"sb{sbytes[0]}_{name}", shape, dt))
            return t

        def op(e, fn, r=(), w=()):
            w = list(w) + [b for b in r if b.name.startswith("bank") and b not in w]
            return S.op(e, fn, reads=r, writes=w)

        banks = [st.enter_context(nc.psum_tensor(f"bank{i}", [128, 512], F32)) for i in range(8)]
        bbuf = [Buf(f"bank{i}") for i in range(8)]
        reserved = [False] * 8
        bank_rr = [0]

        def bank(reserve=False):
            for _ in range(8):
                i = bank_rr[0]
                bank_rr[0] = (i + 1) % 8
                if not reserved[i]:
                    if reserve:
                        reserved[i] = True
                    return i
            raise RuntimeError("no psum bank")

        def mm(bi, out_ap, lhsT, rhs, start, stop, r):
            op("pe", lambda e: e.matmul(out_ap, lhsT=lhsT, rhs=rhs, start=start, stop=stop), r, [bbuf[bi]])

        def tr(bi, out_ap, in_ap, kparts, r):
            op("pe", lambda e: e.transpose(out=out_ap, in_=in_ap, identity=ident[:kparts, :kparts]),
               list(r) + [b_const], [bbuf[bi]])

        ident = sb("ident", [128, 128])
        Ui = sb("Ui", [128, 128])
        Ls = sb("Ls", [128, 128])
        nUs = sb("nUs", [128, 128])
        ones = sb("ones", [128, 128])
        b_const = Buf("const")
        for t_, val in ((ident, 1.0), (Ui, 1.0), (Ls, 1.0), (nUs, -1.0), (ones, 1.0)):
            op("pool", lambda e, t_=t_, val=val: e.memset(t_[:], val), [], [b_const])
        sel = lambda t_, pat, cm, cmp: op("pool", lambda e: e.affine_select(
            out=t_[:], in_=t_[:], pattern=[[pat, 128]], compare_op=cmp, fill=0.0, base=0, channel_multiplier=cm),
            [b_const], [b_const])
        sel(ident, -1, 1, ALU.is_equal)
        sel(Ui, 1, -1, ALU.is_ge)
        sel(Ls, -1, 1, ALU.is_gt)
        sel(nUs, 1, -1, ALU.is_gt)

        pfmT = sb("pfmT", [128, 2, 256])
        ptm = sb("ptm", [128, 2, PTM_W])
        negA_g = sb("negA_g", [128, 2, 8])
        A_s = sb("A_s", [128, 2, 16])
        esink = sb("esink", [128, 2, 16])
        b_par = Buf("par")
        ptmp = sb("ptmp", [128, 2, 128])
        b_ptmp = Buf("ptmp")
        for l in range(2):
            S.dma([(ptmp[:, 0, :], pfm_d[l, 0:128, :]), (ptmp[:, 1, :], pfm_d[l, 128:256, :])],
                  writes=[b_ptmp], q="act")
            bi = bank()
            for hlf in range(2):
                tr(bi, banks[bi][:, hlf * 128:(hlf + 1) * 128], ptmp[:, hlf, :], 128, [b_ptmp])
            op("dve", lambda e: e.tensor_copy(out=pfmT[:, l, :], in_=banks[bi][:, 0:256]), [bbuf[bi]], [b_par])
            S.dma([(ptm[:, l, :], ptm_d[l:l + 1, :].partition_broadcast(128))], writes=[b_par], q="act")
        for l in range(2):
            op("act", lambda e: e.activation(out=negA_g[:, l, :], in_=ptm[:, l, C_GAL:C_GAL + 8], func=AF.Exp), [b_par], [b_par])
            op("act", lambda e: e.activation(out=A_s[:, l, :], in_=ptm[:, l, C_SAL:C_SAL + 16], func=AF.Exp), [b_par], [b_par])
            op("act", lambda e: e.activation(out=esink[:, l, :], in_=ptm[:, l, C_SNK:C_SNK + 16], func=AF.Exp), [b_par], [b_par])
            op("dve", lambda e: e.tensor_scalar(out=negA_g[:, l, :], in0=negA_g[:, l, :], scalar1=-1.0, scalar2=None, op0=ALU.mult), [b_par], [b_par])
            op("dve", lambda e: e.tensor_scalar(out=A_s[:, l, :], in0=A_s[:, l, :], scalar1=-1.0, scalar2=None, op0=ALU.mult), [b_par], [b_par])
        pcol = lambda l, row: pfmT[:, l, row:row + 1]

        h = sb("h", [128, NTILE, D])
        b_h = [Buf(f"h{i}") for i in range(NTILE)]
        xnT = sb("xnT", [128, 8, TMAX], BF16)
        b_xnT = Buf("xnT")
        y_g = sb("y_g", [128, 8, TMAX], BF16)
        y_s = sb("y_s", [128, 8, TMAX], BF16)
        y_c = sb("y_c", [128, 8, TMAX], BF16)
        b_yg, b_ys, b_yc = Buf("yg"), Buf("ys"), Buf("yc")
        Sg = sb("Sg", [128, 2, 8, 128])
        b_Sg = [[Buf() for _ in range(8)] for _ in range(2)]
        Hs = sb("Hs", [128, 2, 4, 256])
        b_Hs = [[Buf() for _ in range(4)] for _ in range(2)]
        halo_g = sb("halo_g", [128, 2, 24, 3])
        halo_s = sb("halo_s", [128, 2, 16, 3])
        b_halo = Buf("halo")
        KW = NMETA + 128 + TS
        kTc = sb("kTc", [64, 2, 4, KW], BF16)
        b_kT = [Buf() for _ in range(2)]
        vA = sb("vA", [128, 2, 2 + TPS, 4, 65], BF16)
        b_vA = [Buf() for _ in range(2)]
        for t_ in (Sg, Hs, halo_g, halo_s):
            op("pool", lambda e, t_=t_: e.memset(t_[:], 0.0), [], [b_halo])
        op("pool", lambda e: e.memset(kTc[:], 0.0), [], [b_kT[0], b_kT[1]])
        op("pool", lambda e: e.memset(vA[:], 1.0), [], [b_vA[0], b_vA[1]])
        for l in range(2):
            for hh in range(8):
                b_Sg[l][hh].lw = b_halo.lw
            for g_ in range(4):
                b_Hs[l][g_].lw = b_halo.lw

        NSTG, NWB = 2, 3
        stg = [sb(f"stg{i}", [128, 2048]) for i in range(NSTG)]
        b_stg = [Buf() for _ in range(NSTG)]
        wbf = [sb(f"wbf{i}", [128, 4096], BF16) for i in range(NWB)]
        b_wbf = [Buf() for _ in range(NWB)]
        wrr = [0, 0]

        def issue_w(parts, kc, cols):
            wi = wrr[1]
            wrr[1] = (wi + 1) % NWB
            wv = wbf[wi][:, 0:kc * cols].rearrange("p (k c) -> p k c", k=kc)
            nsplit = 2 if kc * cols > 2048 else 1
            assert kc % nsplit == 0 and kc * cols // nsplit <= 2048
            kh = kc // nsplit
            for hf in range(nsplit):
                si = wrr[0]
                wrr[0] = (si + 1) % NSTG
                sv = stg[si][:, 0:kh * cols].rearrange("p (k c) -> p k c", k=kh)
                pairs = []
                c0 = 0
                for d_ap in parts:
                    c = d_ap.shape[1]
                    pairs.append((sv[:, :, c0:c0 + c], d_ap[hf * kh * 128:(hf + 1) * kh * 128, :].rearrange("(k p) c -> p k c", p=128)))
                    c0 += c
                assert c0 == cols
                S.dma(pairs, writes=[b_stg[si]], q="sp")
                if hf == 0:
                    op("pool", lambda e: e.tensor_copy(out=wv[:, hf * kh:(hf + 1) * kh, :], in_=sv), [b_stg[si]], [b_wbf[wi]])
                else:
                    op("act", lambda e: e.activation(out=wv[:, hf * kh:(hf + 1) * kh, :], in_=sv, func=AF.Copy), [b_stg[si]], [b_wbf[wi]])
            return wv, b_wbf[wi]

        def layer_specs(l):
            sp = []
            if "G" in phases:
                sp.append(([win_d[l, :, O_GB:O_GB + 16]], 8, 16))
                for hh in range(8):
                    sp.append(([win_d[l, :, O_GQ + hh * 128:O_GQ + (hh + 1) * 128], win_d[l, :, O_GK + hh * 128:O_GK + (hh + 1) * 128],
                                win_d[l, :, O_GV + hh * 128:O_GV + (hh + 1) * 128], win_d[l, :, O_GG + hh * 128:O_GG + (hh + 1) * 128]], 8, 512))
            if "S" in phases:
                sp.append(([win_d[l, :, O_SDT:O_SDT + 16]], 8, 16))
                for gi in range(4):
                    sp.append(([win_d[l, :, O_SX + gi * 256:O_SX + (gi + 1) * 256], win_d[l, :, O_SB + gi * 128:O_SB + (gi + 1) * 128],
                                win_d[l, :, O_SC + gi * 128:O_SC + (gi + 1) * 128]], 8, 512))
                    sp.append(([win_d[l, :, O_SZ + gi * 256:O_SZ + (gi + 1) * 256]], 8, 256))
            if "C" in phases:
                sp.append(([win_d[l, :, O_CK:O_CK + 256], win_d[l, :, O_CV:O_CV + 256]], 8, 512))
                for hk in range(4):
                    sp.append(([win_d[l, :, O_CQ + hk * 256:O_CQ + (hk + 1) * 256]], 8, 256))
            if "M" in phases:
                wps_ = (wpg_d, wps_d, wpc_d)
                for fc in range(8):
                    sp.append(([win_d[l, :, O_GATE + br * 1024 + fc * 128:O_GATE + br * 1024 + (fc + 1) * 128] for br in range(3)], 8, 384))
                    sp.append(([wps_[br][l, :, fc * 128:(fc + 1) * 128] for br in range(3)], 8, 384))
                for half in range(2):
                    sp.append(([wout_d[l, :, half * 512:(half + 1) * 512]], 8, 512))
            if "F" in phases:
                for dg in range(8):
                    sp.append(([wup_d[l, :, dg * 512:(dg + 1) * 512]], 8, 512))
                    sp.append(([wdn_d[l, dg * 512:(dg + 1) * 512, :]], 4, 1024))
            return sp

        all_specs = [sp_ for _s in range(NST) for l_ in range(NL) for sp_ in layer_specs(l_)]
        wqs = {"i": 0, "pend": None}

        def load_w(parts, kc, cols):
            i = wqs["i"]
            if wqs["pend"] is None:
                wqs["pend"] = issue_w(*all_specs[i])
            spec = all_specs[i]
            assert spec[1] == kc and spec[2] == cols and len(spec[0]) == len(parts), (i, spec[1:], kc, cols)
            cur = wqs["pend"]
            wqs["i"] = i + 1
            wqs["pend"] = issue_w(*all_specs[i + 1]) if i + 1 < len(all_specs) else None
            return cur

        def softplus_inplace(x_ap, t_ap, bx, bt):
            op("act", lambda e: e.activation(out=t_ap, in_=x_ap, func=AF.Abs), [bx], [bt])
            op("act", lambda e: e.activation(out=t_ap, in_=t_ap, func=AF.Exp, scale=-1.0), [bt], [bt])
            op("act", lambda e: e.activation(out=t_ap, in_=t_ap, func=AF.Ln, bias=1.0, scale=1.0), [bt], [bt])
            op("dve", lambda e: e.scalar_tensor_tensor(out=x_ap, in0=x_ap, scalar=0.0, in1=t_ap, op0=ALU.max, op1=ALU.add), [bx, bt], [bx])

        def rsqrt_inplace(x_ap, bx, scale, eps):
            op("act", lambda e: e.activation(out=x_ap, in_=x_ap, func=AF.Ln, bias=eps, scale=scale), [bx], [bx])
            op("act", lambda e: e.activation(out=x_ap, in_=x_ap, func=AF.Exp, scale=-0.5), [bx], [bx])

        nscr = sb("nscr", [128, D])
        b_nscr = Buf("nscr")
        nsm = sb("nsm", [128, NTILE])
        b_nsm = Buf("nsm")

        def norm_to_FM(tiles, l, row):
            for i, (off, n) in enumerate(tiles):
                op("act", lambda e: e.activation(out=nscr[:n, :], in_=h[:n, i, :], func=AF.Square, accum_out=nsm[:n, i:i + 1]),
                   [b_h[i]], [b_nscr, b_nsm])
                rsqrt_inplace(nsm[:n, i:i + 1], b_nsm, 1.0 / D, RMS_EPS)
                op("dve", lambda e: e.tensor_scalar(out=nscr[:n, :], in0=h[:n, i, :], scalar1=nsm[:n, i:i + 1], scalar2=None, op0=ALU.mult),
                   [b_h[i], b_nsm], [b_nscr])
                for half in range(2):
                    bi = bank()
                    pv = banks[bi][:, :].rearrange("p (c t) -> p c t", c=4)
                    for c in range(4):
                        cc = half * 4 + c
                        tr(bi, pv[:, c, 0:n], nscr[:n, cc * 128:(cc + 1) * 128], n, [b_nscr])
                    op("dve", lambda e: e.tensor_tensor(
                        out=xnT[:, half * 4:half * 4 + 4, off:off + n], in0=pv[:, :, 0:n],
                        in1=pfmT[:, l, row + half * 4:row + half * 4 + 4].unsqueeze(2).to_broadcast([128, 4, n]), op=ALU.mult),
                        [bbuf[bi], b_par], [b_xnT])

        def proj_FM(bi, wv, bw, c0, ncol, src, bsrc, s0, sn, kcs=8):
            for kc in range(kcs):
                mm(bi, banks[bi][:ncol, 0:sn], wv[:, kc, c0:c0 + ncol], src[:, kc, s0:s0 + sn], kc == 0, kc == kcs - 1, [bw, bsrc])

        def dbg_tap(src, bsrc):
            with ExitStack() as ph2:
                dbg_copy = sb("dbgc", [128, 8, TMAX], F32, ph2)
                b_dbg = Buf()
                op("dve", lambda e: e.tensor_copy(out=dbg_copy[:, :, :], in_=src[:, :, :]), [bsrc], [b_dbg])
                S.dma([(dbg_d.rearrange("p (c t) -> p c t", c=8), dbg_copy[:, :, :])], reads=[b_dbg], q="act")
                S.barrier()

        for s in range(NST):
            if s == 0:
                tiles = [(0, NMETA)] + [(NMETA + 128 * i, 128) for i in range(TPS)]
                segs = [(0, NMETA), (NMETA, TS)]
                T = TMAX
            else:
                tiles = [(128 * i, 128) for i in range(TPS)]
                segs = [(0, TS)]
                T = TS
            seq0 = s * TS
            if s == 0:
                cgroups = [([0], NMETA)] + [([NMETA + 64 * j for j in range(4 * g_, 4 * g_ + 4)], 64) for g_ in range(2 * TPS // 4)]
            else:
                cgroups = [([64 * j for j in range(4 * g_, 4 * g_ + 4)], 64) for g_ in range(2 * TPS // 4)]
            nchunk = sum(len(g[0]) for g in cgroups)
            pairs = []
            wl = []
            for i, (off, n) in enumerate(tiles):
                if s == 0 and i == 0:
                    pairs.append((h[:n, i, :], meta_d[:, :]))
                else:
                    r0 = seq0 + off - (NMETA if s == 0 else 0)
                    pairs.append((h[:n, i, :], x_d[r0:r0 + n, :]))
                wl.append(b_h[i])
            S.dma(pairs, writes=wl, q="act")

            for l in range(NL):
                norm_to_FM(tiles, l, R_N1)
                with ExitStack() as ph:
                  if "G" in phases:
                    psb = lambda n_, sh, dt=F32: sb(n_, sh, dt, ph)
                    NCH = 2 * TPS + 1
                    ba = psb("g_ba", [64, NCH, 16]); b_ba = Buf()
                    tsm = psb("g_tsm", [64, NCH, 8]); b_tsm = Buf()
                    beta = psb("g_beta", [64, NCH, 8]); gsm = psb("g_gsm", [64, NCH, 8])
                    bk = psb("g_bk", [64, NCH, 8]); etail = psb("g_etail", [64, NCH, 8])
                    eglast = psb("g_eglast", [128, NCH, 8]); b_sm = Buf()
                    wv, bw = load_w([win_d[l, :, O_GB:O_GB + 16]], 8, 16)
                    ci = 0
                    cinfo = []
                    for offs, cs in cgroups:
                        bi = bank()
                        for j, off in enumerate(offs):
                            for kc in range(8):
                                mm(bi, banks[bi][:cs, j * 16:(j + 1) * 16], xnT[:, kc, off:off + cs], wv[:, kc, 0:16], kc == 0, kc == 7, [b_xnT, bw])
                            cinfo.append((ci + j, off, cs))
                        nj = len(offs)
                        op("dve", lambda e: e.tensor_copy(out=ba[:cs, ci:ci + nj, :], in_=banks[bi][:cs, 0:nj * 16].rearrange("p (j c) -> p j c", c=16)), [bbuf[bi]], [b_ba])
                        ci += nj
                    assert ci == nchunk
                    NC_ = nchunk
                    op("act", lambda e: e.activation(out=beta[:, 0:NC_, :], in_=ba[:, 0:NC_, 0:8], func=AF.Sigmoid), [b_ba], [b_sm])
                    op("dve", lambda e: e.tensor_tensor(out=gsm[:, 0:NC_, :], in0=ba[:, 0:NC_, 8:16], in1=ptm[:64, l, C_GDB:C_GDB + 8].unsqueeze(1).to_broadcast([64, NC_, 8]), op=ALU.add), [b_ba, b_par], [b_sm])
                    softplus_inplace(gsm[:, 0:NC_, :].rearrange("p j c -> p (j c)"), tsm[:, 0:NC_, :].rearrange("p j c -> p (j c)"), b_sm, b_tsm)
                    op("dve", lambda e: e.tensor_tensor(out=gsm[:, 0:NC_, :], in0=gsm[:, 0:NC_, :], in1=negA_g[:64, l, :].unsqueeze(1).to_broadcast([64, NC_, 8]), op=ALU.mult), [b_sm, b_par], [b_sm])
                    ci = 0
                    for offs, cs in cgroups:
                        nj = len(offs)
                        rhs = gsm[:cs, ci:ci + nj, :].rearrange("p j c -> p (j c)")
                        bi = bank()
                        mm(bi, banks[bi][:cs, 0:nj * 8], Ui[:cs, :cs], rhs, True, True, [b_sm, b_const])
                        mm(bi, banks[bi][:, 128:128 + nj * 8], ones[:cs, :], rhs, True, True, [b_sm, b_const])
                        gam_v = banks[bi][:cs, 0:nj * 8].rearrange("p (j c) -> p j c", c=8)
                        gl_v = banks[bi][:, 128:128 + nj * 8].rearrange("p (j c) -> p j c", c=8)
                        op("act", lambda e: e.activation(out=bk[:cs, ci:ci + nj, :], in_=gam_v, func=AF.Exp), [bbuf[bi]], [b_sm])
                        op("dve", lambda e: e.tensor_tensor(out=bk[:cs, ci:ci + nj, :], in0=bk[:cs, ci:ci + nj, :], in1=beta[:cs, ci:ci + nj, :], op=ALU.mult), [b_sm], [b_sm])
                        op("act", lambda e: e.activation(out=eglast[:, ci:ci + nj, :], in_=gl_v, func=AF.Exp), [bbuf[bi]], [b_sm])
                        op("act", lambda e: e.activation(out=tsm[:cs, ci:ci + nj, :], in_=gl_v[:cs], func=AF.Copy), [bbuf[bi]], [b_tsm])
                        op("dve", lambda e: e.tensor_tensor(out=etail[:cs, ci:ci + nj, :], in0=tsm[:cs, ci:ci + nj, :], in1=gam_v, op=ALU.subtract), [b_tsm, bbuf[bi]], [b_sm])
                        op("act", lambda e: e.activation(out=etail[:cs, ci:ci + nj, :], in_=etail[:cs, ci:ci + nj, :], func=AF.Exp), [b_sm], [b_sm])
                        ci += nj
                    GST = 99
                    xq = psb("g_xq", [128, 3, TMAX + 3]); b_xq = Buf()
                    cq = psb("g_cq", [128, 3, TMAX]); b_cq = Buf()
                    sgt = psb("g_sgt", [128, TMAX]); b_sgt = Buf()
                    sq = psb("g_sq", [128, TMAX]); b_sq = Buf()
                    rin = psb("g_rin", [128, TMAX]); b_rin = Buf()
                    egb = psb("g_egb", [128, TMAX]); b_egb = Buf()
                    kTb = psb("g_kTb", [128, TMAX], BF16); qTb = psb("g_qTb", [128, TMAX], BF16); qdb = psb("g_qdb", [128, TMAX], BF16); b_qk = Buf()
                    m64 = [psb(f"g_m{i}", [64, NCH, 64]) for i in range(9)]
                    b_m = [Buf() for _ in range(9)]
                    E_, DT_, MB_, Bm, BT_, P_, PT_, M_, M2_ = m64
                    bE, bDT, bMB, bBm, bBT, bP, bPT, bM, bM2 = b_m
                    Rk = psb("g_Rk", [64, NCH, 128], BF16); Rv = psb("g_Rv", [64, NCH, 128], BF16); ktl = psb("g_ktl", [64, NCH, 128], BF16)
                    TTb = psb("g_TTb", [64, NCH, 64], BF16); aTb = psb("g_aTb", [64, NCH, 64], BF16)
                    nWT = psb("g_nWT", [128, NCH, 64], BF16)
                    oT = psb("g_oT", [128, TMAX]); b_oT = Buf()
                    vnb = psb("g_vnb", [64, 128], BF16); b_vnb = Buf()
                    Sb = psb("g_Sb", [128, 128], BF16); b_Sb = Buf()
                    for hh in range(8 if GST > 0 else 0):
                        wv, bw = load_w([win_d[l, :, O_GQ + hh * 128:O_GQ + (hh + 1) * 128], win_d[l, :, O_GK + hh * 128:O_GK + (hh + 1) * 128],
                                         win_d[l, :, O_GV + hh * 128:O_GV + (hh + 1) * 128], win_d[l, :, O_GG + hh * 128:O_GG + (hh + 1) * 128]], 8, 512)
                        for qi in range(3):
                            op("dve", lambda e: e.tensor_copy(out=xq[:, qi, 0:3], in_=halo_g[:, l, qi * 8 + hh, :]), [b_halo], [b_xq])
                        for qi in range(4):
                            for (s0, sn) in segs:
                                bi = bank()
                                proj_FM(bi, wv, bw, qi * 128, 128, xnT, b_xnT, s0, sn)
                                if qi < 3:
                                    op("act", lambda e: e.activation(out=xq[:, qi, 3 + s0:3 + s0 + sn], in_=banks[bi][:, 0:sn], func=AF.Copy), [bbuf[bi]], [b_xq])
                                else:
                                    op("act", lambda e: e.activation(out=sgt[:, s0:s0 + sn], in_=banks[bi][:, 0:sn], func=AF.Silu), [bbuf[bi]], [b_sgt])
                        for qi in range(3):
                            chn = qi * 8 + hh
                            cw = lambda tap: pcol(l, R_GCW + tap * 24 + chn)
                            op("dve", lambda e: e.tensor_scalar(out=cq[:, qi, 0:T], in0=xq[:, qi, 3:3 + T], scalar1=cw(3), scalar2=None, op0=ALU.mult), [b_xq, b_par], [b_cq])
                            for tap in range(3):
                                op("dve", lambda e: e.scalar_tensor_tensor(out=cq[:, qi, 0:T], in0=xq[:, qi, tap:tap + T], scalar=cw(tap), in1=cq[:, qi, 0:T], op0=ALU.mult, op1=ALU.add), [b_xq, b_par, b_cq], [b_cq])
                            op("dve", lambda e: e.tensor_copy(out=halo_g[:, l, chn, :], in_=xq[:, qi, T:T + 3]), [b_xq], [b_halo])
                            op("act", lambda e: e.activation(out=cq[:, qi, 0:T], in_=cq[:, qi, 0:T], func=AF.Silu), [b_cq], [b_cq])
                        for qi in range(2):
                            op("dve", lambda e: e.tensor_tensor(out=sq[:, 0:T], in0=cq[:, qi, 0:T], in1=cq[:, qi, 0:T], op=ALU.mult), [b_cq], [b_sq])
                            for (s0, sn) in segs:
                                bi = bank()
                                mm(bi, banks[bi][:, 0:sn], ones[:, :], sq[:, s0:s0 + sn], True, True, [b_sq, b_const])
                                op("act", lambda e: e.activation(out=rin[:, s0:s0 + sn], in_=banks[bi][:, 0:sn], func=AF.Ln, bias=L2_EPS, scale=1.0), [bbuf[bi]], [b_rin])
                            op("act", lambda e: e.activation(out=rin[:, 0:T], in_=rin[:, 0:T], func=AF.Exp, scale=-0.5), [b_rin], [b_rin])
                            if qi == 0:
                                op("dve", lambda e: e.scalar_tensor_tensor(out=cq[:, 0, 0:T], in0=cq[:, 0, 0:T], scalar=128.0 ** -0.5, in1=rin[:, 0:T], op0=ALU.mult, op1=ALU.mult), [b_cq, b_rin], [b_cq])
                            else:
                                op("dve", lambda e: e.tensor_tensor(out=cq[:, 1, 0:T], in0=cq[:, 1, 0:T], in1=rin[:, 0:T], op=ALU.mult), [b_cq, b_rin], [b_cq])
                        op("act", lambda e: e.activation(out=kTb[:, 0:T], in_=cq[:, 1, 0:T], func=AF.Copy), [b_cq], [b_qk])
                        op("act", lambda e: e.activation(out=qTb[:, 0:T], in_=cq[:, 0, 0:T], func=AF.Copy), [b_cq], [b_qk])
                        def pre_gen(offs, cs, ci, G):
                            nj = len(offs)
                            gs0 = offs[0]
                            gl_ = nj * cs
                            v3 = lambda t_: t_[:cs, ci:ci + nj, 0:cs]
                            Uib = Ui[:cs, :cs].unsqueeze(1).to_broadcast([cs, nj, cs])
                            p3 = lambda b_: banks[b_][:cs, 0:nj * cs].rearrange("p (j c) -> p j c", c=cs)
                            op("dve", lambda e: e.tensor_tensor(out=v3(DT_), in0=gsm[:cs, ci:ci + nj, hh].unsqueeze(2).to_broadcast([cs, nj, cs]), in1=Uib, op=ALU.mult), [b_sm, b_const], [G["DT"]])
                            b1 = bank()
                            for j in range(nj):
                                mm(b1, p3(b1)[:, j, :], Ls[:cs, :cs], DT_[:cs, ci + j, 0:cs], True, True, [G["DT"], b_const])
                            op("act", lambda e: e.activation(out=v3(E_), in_=p3(b1), func=AF.Exp), [bbuf[b1]], [G["E"]])
                            yield
                            op("dve", lambda e: e.tensor_tensor(out=v3(MB_), in0=beta[:cs, ci:ci + nj, hh].unsqueeze(2).to_broadcast([cs, nj, cs]), in1=ident[:cs, :cs].unsqueeze(1).to_broadcast([cs, nj, cs]), op=ALU.mult), [b_sm, b_const], [G["MB"]])
                            b2 = bank()
                            for j in range(nj):
                                mm(b2, p3(b2)[:, j, :], ones[:cs, :cs], MB_[:cs, ci + j, 0:cs], True, True, [G["MB"], b_const])
                            op("dve", lambda e: e.tensor_tensor(out=v3(MB_), in0=v3(E_), in1=p3(b2), op=ALU.mult), [G["E"], bbuf[b2]], [G["MB"]])
                            op("dve", lambda e: e.tensor_tensor(out=v3(MB_), in0=v3(MB_), in1=nUs[:cs, :cs].unsqueeze(1).to_broadcast([cs, nj, cs]), op=ALU.mult), [G["MB"], b_const], [G["MB"]])
                            op("dve", lambda e: e.tensor_tensor(out=v3(DT_), in0=v3(E_), in1=Uib, op=ALU.mult), [G["E"], b_const], [G["DT"]])
                            yield
                            b1 = bank()
                            for j in range(nj):
                                o_ = offs[j]
                                mm(b1, p3(b1)[:, j, :], kTb[:, o_:o_ + cs], kTb[:, o_:o_ + cs], True, True, [b_qk])
                            op("dve", lambda e: e.tensor_tensor(out=v3(Bm), in0=v3(MB_), in1=p3(b1), op=ALU.mult), [G["MB"], bbuf[b1]], [G["Bm"]])
                            yield
                            b1 = bank()
                            for j in range(nj):
                                tr(b1, p3(b1)[:, j, :], Bm[:cs, ci + j, 0:cs], cs, [G["Bm"]])
                            op("act", lambda e: e.activation(out=v3(BT_), in_=p3(b1), func=AF.Copy), [bbuf[b1]], [G["BT"]])
                            op("dve", lambda e: e.tensor_tensor(out=v3(M_), in0=v3(Bm), in1=ident[:cs, :cs].unsqueeze(1).to_broadcast([cs, nj, cs]), op=ALU.add), [G["Bm"], b_const], [G["M"]])
                            yield
                            b3 = bank()
                            for j in range(nj):
                                mm(b3, banks[b3][:, j * cs:(j + 1) * cs], gsm[:cs, ci + j, hh:hh + 1].to_broadcast([cs, 128]), Ui[:cs, :cs], True, True, [b_sm, b_const])
                            op("act", lambda e: e.activation(out=egb[:, gs0:gs0 + gl_], in_=banks[b3][:, 0:gl_], func=AF.Exp), [bbuf[b3]], [G["egb"]])
                            op("dve", lambda e: e.tensor_tensor(out=qdb[:, gs0:gs0 + gl_], in0=cq[:, 0, gs0:gs0 + gl_], in1=egb[:, gs0:gs0 + gl_], op=ALU.mult), [b_cq, G["egb"]], [G["qdb"]])
                            nlev = 5 if cs == 64 else 3
                            Pc, PTc, bPc, bPTc = Bm, BT_, G["Bm"], G["BT"]
                            Pn, PTn, bPn, bPTn = P_, PT_, G["P"], G["PT"]
                            Mc, Mn, bMc, bMn = M_, M2_, G["M"], G["M2"]
                            def side_tr():
                                for j0 in range(0, nj, 4):
                                    jn = min(4, nj - j0)
                                    bi = bank()
                                    pk = banks[bi][:cs, 0:jn * 128].rearrange("p (j c) -> p j c", c=128)
                                    for j in range(jn):
                                        tr(bi, pk[:, j, :], cq[:, 1, offs[j0 + j]:offs[j0 + j] + cs], 128, [b_cq])
                                    bcs = lambda t_: t_[:cs, ci + j0:ci + j0 + jn, hh].unsqueeze(2).to_broadcast([cs, jn, 128])
                                    op("dve", lambda e: e.tensor_tensor(out=Rk[:cs, ci + j0:ci + j0 + jn, :], in0=pk, in1=bcs(bk), op=ALU.mult), [bbuf[bi], b_sm], [G["R"]])
                                    op("dve", lambda e: e.tensor_tensor(out=ktl[:cs, ci + j0:ci + j0 + jn, :], in0=pk, in1=bcs(etail), op=ALU.mult), [bbuf[bi], b_sm], [G["R"]])
                                    yield
                                    bi = bank()
                                    pk2 = banks[bi][:cs, 0:jn * 128].rearrange("p (j c) -> p j c", c=128)
                                    for j in range(jn):
                                        tr(bi, pk2[:, j, :], cq[:, 2, offs[j0 + j]:offs[j0 + j] + cs], 128, [b_cq])
                                    op("dve", lambda e: e.tensor_tensor(out=Rv[:cs, ci + j0:ci + j0 + jn, :], in0=pk2, in1=bcs(beta), op=ALU.mult), [bbuf[bi], b_sm], [G["R"]])
                                    yield
                                b1_ = bank()
                                for j in range(nj):
                                    o_ = offs[j]
                                    mm(b1_, p3(b1_)[:, j, :], kTb[:, o_:o_ + cs], qTb[:, o_:o_ + cs], True, True, [b_qk])
                                op("dve", lambda e: e.tensor_tensor(out=v3(aTb), in0=v3(DT_), in1=p3(b1_), op=ALU.mult), [G["DT"], bbuf[b1_]], [G["aT"]])
                                yield
                            side = side_tr()
                            for lev in range(nlev):
                                last = lev == nlev - 1
                                b2 = bank()
                                for j in range(nj):
                                    mm(b2, p3(b2)[:, j, :], Pc[:cs, ci + j, 0:cs], PTc[:cs, ci + j, 0:cs], True, True, [bPc, bPTc])
                                if not last:
                                    b1 = bank()
                                    for j in range(nj):
                                        mm(b1, p3(b1)[:, j, :], PTc[:cs, ci + j, 0:cs], Pc[:cs, ci + j, 0:cs], True, True, [bPc, bPTc])
                                op("act", lambda e: e.activation(out=v3(PTn), in_=p3(b2), func=AF.Copy), [bbuf[b2]], [bPTn])
                                if not last:
                                    op("dve", lambda e: e.tensor_copy(out=v3(Pn), in_=p3(b1)), [bbuf[b1]], [bPn])
                                yield
                                next(side, None)
                                b3 = bank()
                                for j in range(nj):
                                    mm(b3, p3(b3)[:, j, :], PTn[:cs, ci + j, 0:cs], Mc[:cs, ci + j, 0:cs], True, True, [bPTn, bMc])
                                op("dve", lambda e: e.tensor_tensor(out=v3(Mn), in0=v3(Mc), in1=p3(b3), op=ALU.add), [bMc, bbuf[b3]], [bMn])
                                Pc, Pn, bPc, bPn = Pn, Pc, bPn, bPc
                                PTc, PTn, bPTc, bPTn = PTn, PTc, bPTn, bPTc
                                Mc, Mn, bMc, bMn = Mn, Mc, bMn, bMc
                                yield
                            for _ in side:
                                yield
                            op("act", lambda e: e.activation(out=v3(TTb), in_=v3(Mc), func=AF.Copy), [bMc], [G["TT"]])
                            yield
                            b1 = bank()
                            for j in range(nj):
                                mm(b1, banks[b1][:, j * cs:(j + 1) * cs], Rk[:cs, ci + j, :], TTb[:cs, ci + j, 0:cs], True, True, [G["R"], G["TT"]])
                            op("act", lambda e: e.activation(out=nWT[:, ci:ci + nj, 0:cs], in_=banks[b1][:, 0:nj * cs].rearrange("p (j c) -> p j c", c=cs), func=AF.Copy, scale=-1.0), [bbuf[b1]], [G["nWT"]])
                            yield

                        def chain(offs, cs, ci, G):
                            nj = len(offs)
                            gs0 = offs[0]
                            gl_ = nj * cs
                            bo = bank(reserve=True)
                            for j in range(nj):
                                o_ = offs[j]
                                op("act", lambda e: e.activation(out=Sb[:, :], in_=Sg[:, l, hh, :], func=AF.Copy), [b_Sg[l][hh]], [b_Sb])
                                b1 = bank()
                                mm(b1, banks[b1][:cs, 0:128], TTb[:cs, ci + j, 0:cs], Rv[:cs, ci + j, :], True, False, [G["TT"], G["R"]])
                                mm(b1, banks[b1][:cs, 0:128], nWT[:, ci + j, 0:cs], Sb[:, :], False, True, [G["nWT"], b_Sb])
                                op("act", lambda e: e.activation(out=vnb[:cs, :], in_=banks[b1][:cs, 0:128], func=AF.Copy), [bbuf[b1]], [b_vnb])
                                mm(bo, banks[bo][:, j * cs:(j + 1) * cs], Sb[:, :], qdb[:, o_:o_ + cs], True, False, [b_Sb, G["qdb"]])
                                mm(bo, banks[bo][:, j * cs:(j + 1) * cs], vnb[:cs, :], aTb[:cs, ci + j, 0:cs], False, True, [b_vnb, G["aT"]])
                                b2 = bank()
                                mm(b2, banks[b2][:, 0:128], ktl[:cs, ci + j, :], vnb[:cs, :], True, True, [G["R"], b_vnb])
                                op("dve", lambda e: e.scalar_tensor_tensor(out=Sg[:, l, hh, :], in0=Sg[:, l, hh, :], scalar=eglast[:, ci + j, hh:hh + 1], in1=banks[b2][:, 0:128], op0=ALU.mult, op1=ALU.add),
                                   [b_Sg[l][hh], b_sm, bbuf[b2]], [b_Sg[l][hh]])
                            op("dve", lambda e: e.tensor_copy(out=oT[:, gs0:gs0 + gl_], in_=banks[bo][:, 0:gl_]), [bbuf[bo]], [G["oT"]])
                            reserved[bo] = False

                        grp = []
                        ci_ = 0
                        for offs, cs in cgroups:
                            Gd = {k_: Buf() for k_ in ("DT", "E", "MB", "Bm", "BT", "P", "PT", "M", "M2", "R", "TT", "aT", "nWT", "egb", "qdb", "oT")}
                            grp.append((offs, cs, ci_, Gd))
                            ci_ += len(offs)
                        gens = [pre_gen(*g_) for g_ in grp]
                        while gens:
                            for g_ in list(gens):
                                try:
                                    next(g_)
                                except StopIteration:
                                    gens.remove(g_)
                        for g_ in grp:
                            chain(*g_)
                        b_oTs = [g_[3]["oT"] for g_ in grp]
                        op("dve", lambda e: e.tensor_tensor(out=sq[:, 0:T], in0=oT[:, 0:T], in1=oT[:, 0:T], op=ALU.mult), b_oTs, [b_sq])
                        for (s0, sn) in segs:
                            bi = bank()
                            mm(bi, banks[bi][:, 0:sn], ones[:, :], sq[:, s0:s0 + sn], True, True, [b_sq, b_const])
                            op("act", lambda e: e.activation(out=rin[:, s0:s0 + sn], in_=banks[bi][:, 0:sn], func=AF.Ln, bias=RMS_EPS, scale=1.0 / 128), [bbuf[bi]], [b_rin])
                        op("act", lambda e: e.activation(out=rin[:, 0:T], in_=rin[:, 0:T], func=AF.Exp, scale=-0.5), [b_rin], [b_rin])
                        op("dve", lambda e: e.tensor_tensor(out=oT[:, 0:T], in0=oT[:, 0:T], in1=rin[:, 0:T], op=ALU.mult), b_oTs + [b_rin], b_oTs)
                        op("dve", lambda e: e.scalar_tensor_tensor(out=y_g[:, hh, 0:T], in0=oT[:, 0:T], scalar=pcol(l, R_GNW), in1=sgt[:, 0:T], op0=ALU.mult, op1=ALU.mult), b_oTs + [b_par, b_sgt], [b_yg])
                    S.barrier()
                ph = ExitStack()
                if True:
                  if "S" in phases:
                    psb = lambda n_, sh, dt=F32: sb(n_, sh, dt, ph)
                    NT_ = len(tiles)
                    dtp = psb("s_dtp", [128, NTILE, 16]); adt = psb("s_adt", [128, NTILE, 16]); tsm2 = psb("s_tsm", [128, NTILE, 16])
                    eacum = psb("s_eacum", [128, NTILE, 16]); edec = psb("s_edec", [128, NTILE, 16]); echk = psb("s_echk", [128, NTILE, 16])
                    dte = psb("s_dte", [128, NTILE, 16])
                    b_ss = Buf(); b_st = Buf()
                    wv, bw = load_w([win_d[l, :, O_SDT:O_SDT + 16]], 8, 16)
                    bi = bank()
                    for i, (off, n) in enumerate(tiles):
                        for kc in range(8):
                            mm(bi, banks[bi][:n, i * 16:(i + 1) * 16], xnT[:, kc, off:off + n], wv[:, kc, 0:16], kc == 0, kc == 7, [b_xnT, bw])
                    op("dve", lambda e: e.tensor_tensor(out=dtp[:, 0:NT_, :], in0=banks[bi][:, 0:NT_ * 16].rearrange("p (j c) -> p j c", c=16),
                                                        in1=ptm[:, l, C_SDB:C_SDB + 16].unsqueeze(1).to_broadcast([128, NT_, 16]), op=ALU.add), [bbuf[bi], b_par], [b_ss])
                    softplus_inplace(dtp[:, 0:NT_, :].rearrange("p j c -> p (j c)"), tsm2[:, 0:NT_, :].rearrange("p j c -> p (j c)"), b_ss, b_st)
                    op("dve", lambda e: e.tensor_tensor(out=adt[:, 0:NT_, :], in0=dtp[:, 0:NT_, :], in1=A_s[:, l, :].unsqueeze(1).to_broadcast([128, NT_, 16]), op=ALU.mult), [b_ss, b_par], [b_ss])
                    for i, (off, n) in enumerate(tiles):
                        bi = bank()
                        mm(bi, banks[bi][:n, 0:16], Ui[:n, :n], adt[:n, i, :], True, True, [b_ss, b_const])
                        mm(bi, banks[bi][:, 16:32], ones[:n, :], adt[:n, i, :], True, True, [b_ss, b_const])
                        op("act", lambda e: e.activation(out=eacum[:n, i, :], in_=banks[bi][:n, 0:16], func=AF.Exp), [bbuf[bi]], [b_ss])
                        op("act", lambda e: e.activation(out=echk[:, i, :], in_=banks[bi][:, 16:32], func=AF.Exp), [bbuf[bi]], [b_ss])
                        op("act", lambda e: e.activation(out=tsm2[:n, i, :], in_=banks[bi][:n, 16:32], func=AF.Copy), [bbuf[bi]], [b_st])
                        op("dve", lambda e: e.tensor_tensor(out=edec[:n, i, :], in0=tsm2[:n, i, :], in1=banks[bi][:n, 0:16], op=ALU.subtract), [b_st, bbuf[bi]], [b_ss])
                        op("act", lambda e: e.activation(out=edec[:n, i, :], in_=edec[:n, i, :], func=AF.Exp), [b_ss], [b_ss])
                        op("dve", lambda e: e.tensor_tensor(out=dte[:n, i, :], in0=dtp[:n, i, :], in1=edec[:n, i, :], op=ALU.mult), [b_ss], [b_ss])
                    SST = 99
                    xs4 = psb("s_xs4", [128, 4, TMAX + 3]); b_xs4 = Buf()
                    cs4 = psb("s_cs4", [128, 4, TMAX]); b_cs4 = Buf()
                    BTb = psb("s_BTb", [128, TMAX], BF16); CTb = psb("s_CTb", [128, TMAX], BF16); b_BC = Buf()
                    sz = psb("s_sz", [128, NTILE, 256]); b_sz = Buf()
                    ssets = []
                    for k_ in range(2):
                        ssets.append({"t": (psb(f"s_xstm{k_}", [128, 256]), psb(f"s_xdt{k_}", [128, 4, 64], BF16), psb(f"s_xdt2{k_}", [128, 4, 64], BF16),
                                            psb(f"s_Btm{k_}", [128, 128], BF16), psb(f"s_La{k_}", [128, 4, 128]), psb(f"s_MT{k_}", [128, 4, 128], BF16),
                                            psb(f"s_t1{k_}", [128, 256]), psb(f"s_t2{k_}", [128, 256]), psb(f"s_ssm{k_}", [128, 1])),
                                      "b": tuple(Buf() for _ in range(8))})
                    Hb = psb("s_Hb", [128, 256], BF16); b_Hb = Buf()
                    for gi in range(4 if SST > 0 else 0):
                        wa, bwa = load_w([win_d[l, :, O_SX + gi * 256:O_SX + (gi + 1) * 256], win_d[l, :, O_SB + gi * 128:O_SB + (gi + 1) * 128],
                                          win_d[l, :, O_SC + gi * 128:O_SC + (gi + 1) * 128]], 8, 512)
                        chns = [2 * gi, 2 * gi + 1, 8 + gi, 12 + gi]
                        for qi in range(4):
                            op("dve", lambda e: e.tensor_copy(out=xs4[:, qi, 0:3], in_=halo_s[:, l, chns[qi], :]), [b_halo], [b_xs4])
                            for (s0, sn) in segs:
                                bi = bank()
                                proj_FM(bi, wa, bwa, qi * 128, 128, xnT, b_xnT, s0, sn)
                                op("act", lambda e: e.activation(out=xs4[:, qi, 3 + s0:3 + s0 + sn], in_=banks[bi][:, 0:sn], func=AF.Copy), [bbuf[bi]], [b_xs4])
                        for qi in range(4):
                            chn = chns[qi]
                            cw = lambda tap: pcol(l, R_SCW + tap * 16 + chn)
                            op("dve", lambda e: e.tensor_scalar(out=cs4[:, qi, 0:T], in0=xs4[:, qi, 3:3 + T], scalar1=cw(3), scalar2=pcol(l, R_SCB + chn), op0=ALU.mult, op1=ALU.add), [b_xs4, b_par], [b_cs4])
                            for tap in range(3):
                                op("dve", lambda e: e.scalar_tensor_tensor(out=cs4[:, qi, 0:T], in0=xs4[:, qi, tap:tap + T], scalar=cw(tap), in1=cs4[:, qi, 0:T], op0=ALU.mult, op1=ALU.add), [b_xs4, b_par, b_cs4], [b_cs4])
                            op("dve", lambda e: e.tensor_copy(out=halo_s[:, l, chn, :], in_=xs4[:, qi, T:T + 3]), [b_xs4], [b_halo])
                            op("act", lambda e: e.activation(out=cs4[:, qi, 0:T], in_=cs4[:, qi, 0:T], func=AF.Silu), [b_cs4], [b_cs4])
                        op("dve", lambda e: e.tensor_copy(out=BTb[:, 0:T], in_=cs4[:, 2, 0:T]), [b_cs4], [b_BC])
                        op("dve", lambda e: e.tensor_copy(out=CTb[:, 0:T], in_=cs4[:, 3, 0:T]), [b_cs4], [b_BC])
                        wz, bwz = load_w([win_d[l, :, O_SZ + gi * 256:O_SZ + (gi + 1) * 256]], 8, 256)
                        for i, (off, n) in enumerate(tiles):
                            bi = bank()
                            for kc in range(8):
                                mm(bi, banks[bi][:n, 0:256], xnT[:, kc, off:off + n], wz[:, kc, 0:256], kc == 0, kc == 7, [b_xnT, bwz])
                            op("act", lambda e: e.activation(out=sz[:n, i, :], in_=banks[bi][:n, 0:256], func=AF.Silu), [bbuf[bi]], [b_sz])
                        op("act", lambda e: e.activation(out=Hb[:, :], in_=Hs[:, l, gi, :], func=AF.Copy), [b_Hs[l][gi]], [b_Hb])
                        def ssd_tile(i, off, n, K):
                            xs_tm, xdt, xdt2, Btm, La, MT, t1_, t2_, ssm = K["t"]
                            E4 = La
                            b_xstm, b_xdt, b_Btm, b_La, b_MT, b_t1, b_t2, b_ssm = K["b"]
                            b_E4 = b_La
                            t1 = t1_[:, :].rearrange("p (r c) -> p r c", c=64); t2 = t2_[:, :].rearrange("p (r c) -> p r c", c=64)
                            hd = slice(4 * gi, 4 * gi + 4)
                            bc4 = lambda ap_: ap_.unsqueeze(2).to_broadcast([n, 4, 64])
                            bi = bank()
                            tr(bi, banks[bi][:n, 0:128], cs4[:, 0, off:off + n], 128, [b_cs4])
                            tr(bi, banks[bi][:n, 128:256], cs4[:, 1, off:off + n], 128, [b_cs4])
                            px = banks[bi][:n, 0:256].rearrange("p (r c) -> p r c", c=64)
                            op("act", lambda e: e.activation(out=xs_tm[:n, :], in_=banks[bi][:n, 0:256], func=AF.Copy), [bbuf[bi]], [b_xstm])
                            op("dve", lambda e: e.tensor_tensor(out=xdt[:n], in0=px, in1=bc4(dtp[:n, i, hd]), op=ALU.mult), [bbuf[bi], b_ss], [b_xdt])
                            op("dve", lambda e: e.tensor_tensor(out=xdt2[:n], in0=px, in1=bc4(dte[:n, i, hd]), op=ALU.mult), [bbuf[bi], b_ss], [b_xdt])
                            bi = bank()
                            tr(bi, banks[bi][:n, 0:128], cs4[:, 2, off:off + n], 128, [b_cs4])
                            op("act", lambda e: e.activation(out=Btm[:n, :], in_=banks[bi][:n, 0:128], func=AF.Copy), [bbuf[bi]], [b_Btm])
                            yield
                            b1 = bank()
                            mm(b1, banks[b1][:n, 0:n], BTb[:, off:off + n], CTb[:, off:off + n], True, True, [b_BC])
                            op("dve", lambda e: e.tensor_tensor(out=La[:n, :, 0:n], in0=adt[:n, i, hd].unsqueeze(2).to_broadcast([n, 4, n]), in1=Ui[:n, :n].unsqueeze(1).to_broadcast([n, 4, n]), op=ALU.mult), [b_ss, b_const], [b_La])
                            b2 = bank()
                            p2 = banks[b2][:n, 0:4 * n].rearrange("p (r c) -> p r c", c=n)
                            for r_ in range(4):
                                mm(b2, p2[:, r_, :], Ls[:n, :n], La[:n, r_, 0:n], True, True, [b_La, b_const])
                            op("act", lambda e: e.activation(out=E4[:n, :, 0:n], in_=p2, func=AF.Exp), [bbuf[b2]], [b_E4])
                            yield
                            op("dve", lambda e: e.tensor_tensor(out=E4[:n, :, 0:n], in0=E4[:n, :, 0:n], in1=Ui[:n, :n].unsqueeze(1).to_broadcast([n, 4, n]), op=ALU.mult), [b_E4, b_const], [b_E4])
                            op("dve", lambda e: e.tensor_tensor(out=MT[:n, :, 0:n], in0=E4[:n, :, 0:n], in1=banks[b1][:n, 0:n].unsqueeze(1).to_broadcast([n, 4, n]), op=ALU.mult), [b_E4, bbuf[b1]], [b_MT])
                            b3 = bank()
                            for r_ in range(4):
                                mm(b3, banks[b3][:n, r_ * 64:(r_ + 1) * 64], MT[:n, r_, 0:n], xdt[:n, r_, :], True, True, [b_MT, b_xdt])
                            yield
                            b4 = bank()
                            mm(b4, banks[b4][:n, 0:256], CTb[:, off:off + n], Hb[:, :], True, True, [b_BC, b_Hb])
                            b5 = bank()
                            mm(b5, banks[b5][:, 0:256], Btm[:n, :], xdt2[:n].rearrange("p r c -> p (r c)"), True, True, [b_Btm, b_xdt])
                            hv = Hs[:, l, gi, :].rearrange("p (r c) -> p r c", c=64)
                            op("dve", lambda e: e.tensor_tensor(out=hv, in0=hv, in1=echk[:, i, hd].unsqueeze(2).to_broadcast([128, 4, 64]), op=ALU.mult), [b_Hs[l][gi], b_ss], [b_Hs[l][gi]])
                            op("dve", lambda e: e.tensor_tensor(out=Hs[:, l, gi, :], in0=Hs[:, l, gi, :], in1=banks[b5][:, 0:256], op=ALU.add), [b_Hs[l][gi], bbuf[b5]], [b_Hs[l][gi]])
                            op("act", lambda e: e.activation(out=Hb[:, :], in_=Hs[:, l, gi, :], func=AF.Copy), [b_Hs[l][gi]], [b_Hb])
                            v4 = lambda b_: banks[b_][:n, 0:256].rearrange("p (r c) -> p r c", c=64)
                            op("dve", lambda e: e.tensor_tensor(out=t1[:n], in0=v4(b4), in1=bc4(eacum[:n, i, hd]), op=ALU.mult), [bbuf[b4], b_ss], [b_t1])
                            yield
                            op("dve", lambda e: e.tensor_tensor(out=t1[:n], in0=t1[:n], in1=v4(b3), op=ALU.add), [b_t1, bbuf[b3]], [b_t1])
                            op("dve", lambda e: e.tensor_tensor(out=t2[:n], in0=xs_tm[:n, :].rearrange("p (r c) -> p r c", c=64), in1=bc4(ptm[:n, l, C_SD + 4 * gi:C_SD + 4 * gi + 4]), op=ALU.mult), [b_xstm, b_par], [b_t2])
                            op("dve", lambda e: e.tensor_tensor(out=t1[:n], in0=t1[:n], in1=t2[:n], op=ALU.add), [b_t1, b_t2], [b_t1])
                            op("dve", lambda e: e.tensor_tensor(out=t1[:n], in0=t1[:n], in1=sz[:n, i, :].rearrange("p (r c) -> p r c", c=64), op=ALU.mult), [b_t1, b_sz], [b_t1])
                            op("act", lambda e: e.activation(out=t2_[:n, :], in_=t1_[:n, :], func=AF.Square, accum_out=ssm[:n, 0:1]), [b_t1], [b_t2, b_ssm])
                            yield
                            rsqrt_inplace(ssm[:n, 0:1], b_ssm, 1.0 / 256, RMS_EPS)
                            op("dve", lambda e: e.tensor_scalar(out=t1_[:n, :], in0=t1_[:n, :], scalar1=ssm[:n, 0:1], scalar2=None, op0=ALU.mult), [b_t1, b_ssm], [b_t1])
                            b6 = bank()
                            tr(b6, banks[b6][:, 0:n], t1_[:n, 0:128], n, [b_t1])
                            tr(b6, banks[b6][:, 128:128 + n], t1_[:n, 128:256], n, [b_t1])
                            for c_ in range(2):
                                op("dve", lambda e: e.tensor_scalar(out=y_s[:, 2 * gi + c_, off:off + n], in0=banks[b6][:, c_ * 128:c_ * 128 + n], scalar1=pcol(l, R_SNW + 2 * gi + c_), scalar2=None, op0=ALU.mult), [bbuf[b6], b_par], [b_ys])
                            yield

                        tl_ = list(enumerate(tiles))
                        for p0 in range(0, len(tl_), 2):
                            gens = [ssd_tile(i_, o_, n_, ssets[k_]) for k_, (i_, (o_, n_)) in enumerate(tl_[p0:p0 + 2])]
                            while gens:
                                for g_ in list(gens):
                                    try:
                                        next(g_)
                                    except StopIteration:
                                        gens.remove(g_)
                    pass
                if True:
                  if "C" in phases:
                    psb = lambda n_, sh, dt=F32: sb(n_, sh, dt, ph)
                    seqbase = NMETA if s == 0 else 0
                    qTb2 = psb("c_qTb", [64, 4, TMAX], BF16); b_qT2 = Buf()
                    eTs = [psb(f"c_eT{i_}", [128, 4, 128], BF16) for i_ in range(3)]; b_eT = [Buf() for _ in range(3)]
                    etmp = psb("c_etmp", [128, 4, 128]); b_etmp = Buf()
                    den = psb("c_den", [128, 4]); b_den = Buf()
                    otm_ = psb("c_otm", [128, 256]); b_otm = Buf()
                    otm = otm_[:, :].rearrange("p (r c) -> p r c", c=64)
                    wkv, bwkv = load_w([win_d[l, :, O_CK:O_CK + 256], win_d[l, :, O_CV:O_CV + 256]], 8, 512)
                    kcol = lambda t_: t_ if (s == 0 and t_ < NMETA) else NMETA + 128 + (t_ - seqbase)
                    for hk in range(4):
                        for (s0, sn) in segs:
                            bi = bank()
                            proj_FM(bi, wkv, bwkv, hk * 64, 64, xnT, b_xnT, s0, sn)
                            d0 = kcol(s0)
                            op("act", lambda e: e.activation(out=kTc[:, l, hk, d0:d0 + sn], in_=banks[bi][:64, 0:sn], func=AF.Copy), [bbuf[bi]], [b_kT[l]])
                    vslot = lambda i_: 0 if (s == 0 and i_ == 0) else 2 + i_ - (1 if s == 0 else 0)
                    for i, (off, n) in enumerate(tiles):
                        bi = bank()
                        for kc in range(8):
                            mm(bi, banks[bi][:n, 0:256], xnT[:, kc, off:off + n], wkv[:, kc, 256:512], kc == 0, kc == 7, [b_xnT, bwkv])
                        op("act", lambda e: e.activation(out=vA[:n, l, vslot(i), :, 0:64], in_=banks[bi][:n, 0:256].rearrange("p (r c) -> p r c", c=64), func=AF.Copy), [bbuf[bi]], [b_vA[l]])
                    for hk in range(4):
                        wq, bwq = load_w([win_d[l, :, O_CQ + hk * 256:O_CQ + (hk + 1) * 256]], 8, 256)
                        for r_ in range(4):
                            for (s0, sn) in segs:
                                bi = bank()
                                proj_FM(bi, wq, bwq, r_ * 64, 64, xnT, b_xnT, s0, sn)
                                op("act", lambda e: e.activation(out=qTb2[:, r_, s0:s0 + sn], in_=banks[bi][:64, 0:sn], func=AF.Copy), [bbuf[bi]], [b_qT2])
                        for i, (off, n) in enumerate(tiles):
                            is_meta = (s == 0 and i == 0)
                            if is_meta:
                                kbs = [(0, 0, NMETA, Ui)]
                            else:
                                k_ = i - (1 if s == 0 else 0)
                                kbs = [(NMETA + 128 + 128 * k_, 2 + k_, 128, Ui)]
                                if k_ > 0:
                                    kbs.append((NMETA + 128 + 128 * (k_ - 1), 2 + k_ - 1, 128, Ls))
                                elif s > 0:
                                    kbs.append((NMETA, 1, 128, Ls))
                                kbs.append((0, 0, NMETA, None))
                            for idx, (kc0, vs_, nk, msk) in enumerate(kbs):
                                bi = bank()
                                pq = banks[bi][:nk, 0:4 * n].rearrange("p (r c) -> p r c", c=n)
                                for r_ in range(4):
                                    mm(bi, pq[:, r_, :], kTc[:, l, hk, kc0:kc0 + nk], qTb2[:, r_, off:off + n], True, True, [b_kT[l], b_qT2])
                                if msk is None:
                                    op("act", lambda e: e.activation(out=eTs[idx][:nk, :, 0:n], in_=pq, func=AF.Exp, scale=0.125), [bbuf[bi]], [b_eT[idx]])
                                else:
                                    op("act", lambda e: e.activation(out=etmp[:nk, :, 0:n], in_=pq, func=AF.Exp, scale=0.125), [bbuf[bi]], [b_etmp])
                                    op("dve", lambda e: e.tensor_tensor(out=eTs[idx][:nk, :, 0:n], in0=etmp[:nk, :, 0:n], in1=msk[:nk, :n].unsqueeze(1).to_broadcast([nk, 4, n]), op=ALU.mult), [b_etmp, b_const], [b_eT[idx]])
                            bo = bank()
                            po = banks[bo][:n, 0:260].rearrange("p (r c) -> p r c", c=65)
                            for r_ in range(4):
                                for idx, (kc0, vs_, nk, msk) in enumerate(kbs):
                                    mm(bo, po[:, r_, :], eTs[idx][:nk, r_, 0:n], vA[:nk, l, vs_, hk, :], idx == 0, idx == len(kbs) - 1, [b_eT[idx], b_vA[l]])
                            op("dve", lambda e: e.tensor_tensor(out=den[:n, :], in0=po[:, :, 64], in1=esink[:n, l, 4 * hk:4 * hk + 4], op=ALU.add), [bbuf[bo], b_par], [b_den])
                            op("dve", lambda e: e.reciprocal(out=den[:n, :], in_=den[:n, :]), [b_den], [b_den])
                            op("dve", lambda e: e.tensor_tensor(out=otm[:n], in0=po[:, :, 0:64], in1=den[:n, :].unsqueeze(2).to_broadcast([n, 4, 64]), op=ALU.mult), [bbuf[bo], b_den], [b_otm])
                            b6 = bank()
                            of = otm_[:n, :]
                            tr(b6, banks[b6][:, 0:n], of[:, 0:128], n, [b_otm])
                            tr(b6, banks[b6][:, 128:128 + n], of[:, 128:256], n, [b_otm])
                            for c_ in range(2):
                                op("act", lambda e: e.activation(out=y_c[:, 2 * hk + c_, off:off + n], in_=banks[b6][:, c_ * 128:c_ * 128 + n], func=AF.Copy), [bbuf[b6]], [b_yc])
                    op("dve", lambda e: e.tensor_copy(out=kTc[:, l, :, NMETA:NMETA + 128], in_=kTc[:, l, :, NMETA + TS:NMETA + TS + 128]), [b_kT[l]], [b_kT[l]])
                    op("dve", lambda e: e.tensor_copy(out=vA[:, l, 1, :, 0:64], in_=vA[:, l, 1 + TPS, :, 0:64]), [b_vA[l]], [b_vA[l]])
                    pass
                if True:
                  if "M" in phases:
                    psb = lambda n_, sh, dt=F32: sb(n_, sh, dt, ph)
                    mrg = psb("m_mrg", [128, 8, TMAX], BF16); b_mrg = Buf()
                    sig = psb("m_sig", [128, 512]); b_sig = Buf()
                    tmpm = psb("m_tmp", [128, 512]); b_tmpm = Buf()
                    acc = psb("m_acc", [128, 512]); b_acc = Buf()
                    ysrc = [(y_g, b_yg, wpg_d), (y_s, b_ys, wps_d), (y_c, b_yc, wpc_d)]
                    for fc in range(8):
                        wg, bwg = load_w([win_d[l, :, O_GATE + br * 1024 + fc * 128:O_GATE + br * 1024 + (fc + 1) * 128] for br in range(3)], 8, 384)
                        wp, bwp = load_w([ysrc[br][2][l, :, fc * 128:(fc + 1) * 128] for br in range(3)], 8, 384)
                        for (s0, sn) in segs:
                            for br in range(3):
                                b1 = bank()
                                proj_FM(b1, wg, bwg, br * 128, 128, xnT, b_xnT, s0, sn)
                                op("act", lambda e: e.activation(out=sig[:, 0:sn], in_=banks[b1][:, 0:sn], func=AF.Sigmoid), [bbuf[b1]], [b_sig])
                                b2 = bank()
                                proj_FM(b2, wp, bwp, br * 128, 128, ysrc[br][0], ysrc[br][1], s0, sn)
                                if br == 0:
                                    op("dve", lambda e: e.tensor_tensor(out=acc[:, 0:sn], in0=sig[:, 0:sn], in1=banks[b2][:, 0:sn], op=ALU.mult), [b_sig, bbuf[b2]], [b_acc])
                                else:
                                    op("dve", lambda e: e.tensor_tensor(out=tmpm[:, 0:sn], in0=sig[:, 0:sn], in1=banks[b2][:, 0:sn], op=ALU.mult), [b_sig, bbuf[b2]], [b_tmpm])
                                    if br == 1:
                                        op("dve", lambda e: e.tensor_tensor(out=acc[:, 0:sn], in0=acc[:, 0:sn], in1=tmpm[:, 0:sn], op=ALU.add), [b_acc, b_tmpm], [b_acc])
                                    else:
                                        op("dve", lambda e: e.tensor_tensor(out=mrg[:, fc, s0:s0 + sn], in0=acc[:, 0:sn], in1=tmpm[:, 0:sn], op=ALU.add), [b_acc, b_tmpm], [b_mrg])
                    if dbg == "mrg" and l == 0 and s == 0:
                        dbg_tap(mrg, b_mrg)
                    for half in range(2):
                        wo, bwo = load_w([wout_d[l, :, half * 512:(half + 1) * 512]], 8, 512)
                        for i, (off, n) in enumerate(tiles):
                            bi = bank()
                            for kc in range(8):
                                mm(bi, banks[bi][:n, 0:512], mrg[:, kc, off:off + n], wo[:, kc, :], kc == 0, kc == 7, [b_mrg, bwo])
                            op("dve", lambda e: e.tensor_tensor(out=h[:n, i, half * 512:(half + 1) * 512], in0=h[:n, i, half * 512:(half + 1) * 512], in1=banks[bi][:n, 0:512], op=ALU.add), [b_h[i], bbuf[bi]], [b_h[i]])
                    pass
                if dbg and l == 0 and s == 0 and dbg in ("y_g", "y_s", "y_c"):
                    dbg_tap({"y_g": y_g, "y_s": y_s, "y_c": y_c}[dbg], {"y_g": b_yg, "y_s": b_ys, "y_c": b_yc}[dbg])
                norm_to_FM(tiles, l, R_N2)
                if True:
                  if "F" in phases:
                    psb = lambda n_, sh, dt=F32: sb(n_, sh, dt, ph)
                    actT = psb("f_act", [128, 4, TMAX], BF16); b_actT = Buf()
                    rl = sig; b_rl = b_sig
                    for dg in range(8):
                        wu, bwu = load_w([wup_d[l, :, dg * 512:(dg + 1) * 512]], 8, 512)
                        wd, bwd = load_w([wdn_d[l, dg * 512:(dg + 1) * 512, :]], 4, 1024)
                        for c_ in range(4):
                            for (s0, sn) in segs:
                                bi = bank()
                                proj_FM(bi, wu, bwu, c_ * 128, 128, xnT, b_xnT, s0, sn)
                                op("act", lambda e: e.activation(out=rl[:, 0:sn], in_=banks[bi][:, 0:sn], func=AF.Relu), [bbuf[bi]], [b_rl])
                                op("dve", lambda e: e.tensor_tensor(out=actT[:, c_, s0:s0 + sn], in0=rl[:, 0:sn], in1=rl[:, 0:sn], op=ALU.mult), [b_rl], [b_actT])
                        for i, (off, n) in enumerate(tiles):
                            for half in range(2):
                                bi = bank()
                                for c_ in range(4):
                                    mm(bi, banks[bi][:n, 0:512], actT[:, c_, off:off + n], wd[:, c_, half * 512:(half + 1) * 512], c_ == 0, c_ == 3, [b_actT, bwd])
                                op("dve", lambda e: e.tensor_tensor(out=h[:n, i, half * 512:(half + 1) * 512], in0=h[:n, i, half * 512:(half + 1) * 512], in1=banks[bi][:n, 0:512], op=ALU.add), [b_h[i], bbuf[bi]], [b_h[i]])
                    S.barrier()
                    ph.close()

            with ExitStack() as ph:
                ot = sb("f_ot", [128, 2, D], F32, ph)
                fnw = sb("fnw", [128, D], F32, ph)
                b_fnw = Buf()
                S.dma([(fnw[:], fnw_d[0:1, :].partition_broadcast(128))], writes=[b_fnw], q="act")
                b_ot = [Buf(), Buf()]
                k_ = 0
                for i, (off, n) in enumerate(tiles):
                    if s == 0 and i == 0:
                        continue
                    op("act", lambda e: e.activation(out=nscr[:n, :], in_=h[:n, i, :], func=AF.Square, accum_out=nsm[:n, i:i + 1]), [b_h[i]], [b_nscr, b_nsm])
                    rsqrt_inplace(nsm[:n, i:i + 1], b_nsm, 1.0 / D, RMS_EPS)
                    op("dve", lambda e: e.scalar_tensor_tensor(out=ot[:n, k_ % 2, :], in0=h[:n, i, :], scalar=nsm[:n, i:i + 1], in1=fnw[:n, :], op0=ALU.mult, op1=ALU.mult),
                       [b_h[i], b_nsm, b_fnw], [b_ot[k_ % 2]])
                    r0 = seq0 + off - (NMETA if s == 0 else 0)
                    S.dma([(out_d[r0:r0 + n, :], ot[:n, k_ % 2, :])], reads=[b_ot[k_ % 2]], q="act")
                    k_ += 1
                S.barrier()
        S.final_wait("sp")
        S.final_wait("act")
        print("instructions", S.n_ins, "waits", S.n_wait)
    return nc


def pack_params(inp):
    pfm = np.zeros((2, 256, 128), np.float32)
    ptm = np.zeros((2, PTM_W), np.float32)
    for l in range(2):
        pfm[l, R_N1:R_N1 + 8] = np.asarray(inp["norm1_w"][l]).reshape(8, 128)
        pfm[l, R_N2:R_N2 + 8] = np.asarray(inp["norm2_w"][l]).reshape(8, 128)
        pfm[l, R_GCW:R_GCW + 96] = np.asarray(inp["gdn_conv_w"][l]).reshape(96, 128)
        pfm[l, R_SCW:R_SCW + 64] = np.asarray(inp["ssd_conv_w"][l]).reshape(64, 128)
        pfm[l, R_SCB:R_SCB + 16] = np.asarray(inp["ssd_conv_b"][l]).reshape(16, 128)
        pfm[l, R_GNW] = np.asarray(inp["gdn_norm_w"][l])
        pfm[l, R_SNW:R_SNW + 8] = np.asarray(inp["ssd_norm_w"][l]).reshape(8, 128)
        ptm[l, C_GAL:C_GAL + 8] = np.asarray(inp["gdn_a_log"][l])
        ptm[l, C_GDB:C_GDB + 8] = np.asarray(inp["gdn_dt_bias"][l])
        ptm[l, C_SDB:C_SDB + 16] = np.asarray(inp["ssd_dt_bias"][l])
        ptm[l, C_SAL:C_SAL + 16] = np.asarray(inp["ssd_a_log"][l])
        ptm[l, C_SD:C_SD + 16] = np.asarray(inp["ssd_d"][l])
        ptm[l, C_SNK:C_SNK + 16] = np.asarray(inp["swa_sinks"][l])
    return pfm, ptm


_NC_CACHE = {}


def kernel(**inputs):
    inp = {k: np.asarray(v) for k, v in inputs.items()}
    n = 8
    if "nc" not in _NC_CACHE:
        _NC_CACHE["nc"] = build(NST=8, NL=2, TPS=4)
    nc = _NC_CACHE["nc"]
    pfm, ptm = pack_params(inp)
    f32 = lambda a: np.ascontiguousarray(a, dtype=np.float32)
    shared = dict(meta=f32(inp["meta_tokens"]), w_in=f32(inp["w_in"]), w_pg=f32(inp["w_proj_gdn"]), w_ps=f32(inp["w_proj_ssd"]),
                  w_pc=f32(inp["w_proj_swa"]), w_out=f32(inp["w_out"]), w_up=f32(inp["w_up"]), w_dn=f32(inp["w_down"]),
                  pfm=pfm, ptm=ptm, fnw=f32(inp["final_norm_w"]).reshape(1, -1))
    in_maps = [dict(shared, x=f32(inp["x"][i])) for i in range(n)]
    res = run_bass_kernel_spmd(nc, in_maps, core_ids=list(range(n)))
    return np.stack([np.asarray(r["out"], dtype=np.float32) for r in res.results], axis=0)
```
